# Optimizing a Trainium2 kernel written in Bass

```python
import math
import jax
import jax.numpy as jnp
from jax import lax
import numpy as np

D_MODEL = 1024
BATCH = 8
SEQ = 2048
DEPTH = 4

D_PLE = 256
RWKV_WIDTH = D_MODEL // 2
RWKV_HEAD = 64
RWKV_HEADS = RWKV_WIDTH // RWKV_HEAD
DECAY_LORA = 64
AAA_LORA = 64
GATE_LORA = 128
GN_EPS = RWKV_HEAD * 1e-5
DIFF_WIDTH = D_MODEL // 2
DIFF_HEADS = 4
DIFF_HEAD_DIM = DIFF_WIDTH // DIFF_HEADS // 2
SUBLN_EPS = 1e-5
Q_BLOCK = 128
D_FF = 4 * D_MODEL
NORM_EPS = 1e-6
RWKV_COLS = 3 * RWKV_WIDTH + DECAY_LORA + AAA_LORA + GATE_LORA
ATTN_COLS = 3 * DIFF_WIDTH
GATE_COLS = 2 * D_MODEL
IN_COLS = RWKV_COLS + ATTN_COLS + GATE_COLS

kernel_name = "hybrid_rwkv7_diffattn_gated_block"


def rmsnorm(x, g, eps=NORM_EPS):
    xf = x.astype(jnp.float32)
    y = xf * lax.rsqrt(jnp.mean(xf * xf, axis=-1, keepdims=True) + eps)
    return (y * g.astype(jnp.float32)).astype(x.dtype)


def wkv7_scan(r, w, k, v, a, b):
    bsz, _, nh, n = r.shape

    def step(state, inp):
        r_t, w_t, k_t, v_t, a_t, b_t = inp
        sa = jnp.einsum('bhvk,bhk->bhv', state, a_t)
        state = (state * w_t[:, :, None, :]
                 + sa[..., None] * b_t[:, :, None, :]
                 + v_t[..., None] * k_t[:, :, None, :])
        y_t = jnp.einsum('bhvk,bhk->bhv', state, r_t)
        return state, y_t

    xs = tuple(jnp.moveaxis(t, 1, 0) for t in (r, w, k, v, a, b))
    state0 = jnp.zeros((bsz, nh, n, n), jnp.float32)
    _, ys = lax.scan(step, state0, xs)
    return jnp.moveaxis(ys, 0, 1)


def rwkv7_time_mix(cols, mu, w0, w2, a0, a2, g2, k_k, k_a, r_k, lnx_w, lnx_b):
    out_dtype = cols.dtype
    c = cols.astype(jnp.float32)
    bsz, seq, _ = c.shape
    prev = jnp.pad(c[:, :-1], ((0, 0), (1, 0), (0, 0)))
    c = c + (prev - c) * mu.astype(jnp.float32)
    o1 = RWKV_WIDTH
    o2 = 2 * RWKV_WIDTH
    o3 = 3 * RWKV_WIDTH
    o4 = o3 + DECAY_LORA
    o5 = o4 + AAA_LORA
    r, k, v, wd, ad, gd = jnp.split(c, [o1, o2, o3, o4, o5], axis=-1)
    f32 = lambda t: t.astype(jnp.float32)
    w = -jax.nn.softplus(-(f32(w0) + jnp.tanh(wd) @ f32(w2))) - 0.5
    decay = jnp.exp(-jnp.exp(w))
    a = jax.nn.sigmoid(f32(a0) + ad @ f32(a2))
    g = jax.nn.sigmoid(gd) @ f32(g2)
    heads = lambda t: t.reshape(bsz, seq, RWKV_HEADS, RWKV_HEAD)
    kk = heads(k * f32(k_k))
    kk = kk / jnp.maximum(jnp.sqrt(jnp.sum(kk * kk, axis=-1, keepdims=True)), 1e-12)
    k = k * (1.0 + (a - 1.0) * f32(k_a))
    r_h, k_h, v_h, a_h, w_h = heads(r), heads(k), heads(v), heads(a), heads(decay)
    y = wkv7_scan(r_h, w_h, k_h, v_h, -kk, kk * a_h)
    mean = jnp.mean(y, axis=-1, keepdims=True)
    var = jnp.mean(jnp.square(y - mean), axis=-1, keepdims=True)
    y = (y - mean) * lax.rsqrt(var + GN_EPS)
    y = y.reshape(bsz, seq, RWKV_WIDTH) * f32(lnx_w) + f32(lnx_b)
    bonus = jnp.sum(r_h * k_h * f32(r_k), axis=-1, keepdims=True) * v_h
    y = (y + bonus.reshape(bsz, seq, RWKV_WIDTH)) * g
    return y.astype(out_dtype)


def diff_attention(cols, lam_q1, lam_k1, lam_q2, lam_k2, subln_g, lambda_init):
    out_dtype = cols.dtype
    bsz, seq, _ = cols.shape
    q, k, v = jnp.split(cols.astype(jnp.float32), 3, axis=-1)
    q = q.reshape(bsz, seq, DIFF_HEADS, 2, DIFF_HEAD_DIM) * (DIFF_HEAD_DIM ** -0.5)
    k = k.reshape(bsz, seq, DIFF_HEADS, 2, DIFF_HEAD_DIM)
    v = v.reshape(bsz, seq, DIFF_HEADS, 2 * DIFF_HEAD_DIM)
    f32 = lambda t: t.astype(jnp.float32)
    lam = (jnp.exp(jnp.sum(f32(lam_q1) * f32(lam_k1)))
           - jnp.exp(jnp.sum(f32(lam_q2) * f32(lam_k2))) + lambda_init)
    slopes = 2.0 ** (-8.0 * jnp.arange(1, DIFF_HEADS + 1, dtype=jnp.float32) / DIFF_HEADS)
    outs = []
    for blk in range(seq // Q_BLOCK):
        q0 = blk * Q_BLOCK
        kend = q0 + Q_BLOCK
        qb = q[:, q0:kend]
        kb = k[:, :kend]
        vb = v[:, :kend]
        s = jnp.einsum('bqhcd,bkhcd->bhcqk', qb, kb)
        dist = (q0 + jnp.arange(Q_BLOCK))[:, None] - jnp.arange(kend)[None, :]
        s = s - slopes[None, :, None, None, None] * dist.astype(jnp.float32)
        s = jnp.where(dist >= 0, s, -jnp.inf)
        pr = jax.nn.softmax(s, axis=-1)
        attn = pr[:, :, 0] - lam * pr[:, :, 1]
        outs.append(jnp.einsum('bhqk,bkhe->bqhe', attn, vb))
    o = jnp.concatenate(outs, axis=1)
    o = rmsnorm(o, subln_g, SUBLN_EPS) * (1.0 - lambda_init)
    return o.reshape(bsz, seq, DIFF_WIDTH).astype(out_dtype)


def setup_inputs(seed: int = 0) -> dict:
    key = jax.random.key(seed)
    ks = jax.random.split(key, 32)
    L = DEPTH
    nrm = lambda k, shape, scale: jax.random.normal(k, shape, jnp.float32) * scale
    gain = lambda k, shape: 1.0 + 0.05 * jax.random.normal(k, shape, jnp.float32)
    return {
        "x": nrm(ks[0], (BATCH, SEQ, D_MODEL), 1.0),
        "p": nrm(ks[1], (DEPTH, BATCH, SEQ, D_PLE), 1.0),
        "norm_mix_g": gain(ks[2], (L, D_MODEL)),
        "w_in": nrm(ks[3], (L, D_MODEL, IN_COLS), D_MODEL ** -0.5),
        "rwkv_mu": jax.random.uniform(ks[4], (L, RWKV_COLS), jnp.float32),
        "rwkv_w0": jax.random.uniform(ks[5], (L, RWKV_WIDTH), jnp.float32, minval=-4.0, maxval=1.0),
        "rwkv_w2": nrm(ks[6], (L, DECAY_LORA, RWKV_WIDTH), 0.1 * DECAY_LORA ** -0.5),
        "rwkv_a0": nrm(ks[7], (L, RWKV_WIDTH), 0.1),
        "rwkv_a2": nrm(ks[8], (L, AAA_LORA, RWKV_WIDTH), 0.1 * AAA_LORA ** -0.5),
        "rwkv_g2": nrm(ks[9], (L, GATE_LORA, RWKV_WIDTH), GATE_LORA ** -0.5),
        "rwkv_k_k": 0.85 + 0.05 * jax.random.normal(ks[10], (L, RWKV_WIDTH), jnp.float32),
        "rwkv_k_a": gain(ks[11], (L, RWKV_WIDTH)),
        "rwkv_r_k": nrm(ks[12], (L, RWKV_HEADS, RWKV_HEAD), 0.1),
        "rwkv_lnx_w": gain(ks[13], (L, RWKV_WIDTH)),
        "rwkv_lnx_b": nrm(ks[14], (L, RWKV_WIDTH), 0.01),
        "lam_q1": nrm(ks[15], (L, DIFF_HEAD_DIM), 0.1),
        "lam_k1": nrm(ks[16], (L, DIFF_HEAD_DIM), 0.1),
        "lam_q2": nrm(ks[17], (L, DIFF_HEAD_DIM), 0.1),
        "lam_k2": nrm(ks[18], (L, DIFF_HEAD_DIM), 0.1),
        "diff_subln_g": gain(ks[19], (L, 2 * DIFF_HEAD_DIM)),
        "w_proj_a": nrm(ks[20], (L, RWKV_WIDTH, D_MODEL), RWKV_WIDTH ** -0.5),
        "w_proj_b": nrm(ks[21], (L, DIFF_WIDTH, D_MODEL), DIFF_WIDTH ** -0.5),
        "w_out": nrm(ks[22], (L, D_MODEL, D_MODEL), D_MODEL ** -0.5),
        "norm_mlp_g": gain(ks[23], (L, D_MODEL)),
        "w_ff1": nrm(ks[24], (L, D_MODEL, D_FF), D_MODEL ** -0.5),
        "w_ff2": nrm(ks[25], (L, D_FF, D_MODEL), D_FF ** -0.5),
        "norm_ple_g": gain(ks[26], (L, D_MODEL)),
        "w_ple": nrm(ks[27], (L, D_PLE, D_MODEL), D_PLE ** -0.5),
        "w_ple_gate": nrm(ks[28], (L, D_MODEL, D_MODEL), D_MODEL ** -0.5),
        "final_norm_g": gain(ks[29], (D_MODEL,)),
    }


def reference(x, p, norm_mix_g, w_in, rwkv_mu, rwkv_w0, rwkv_w2, rwkv_a0, rwkv_a2,
              rwkv_g2, rwkv_k_k, rwkv_k_a, rwkv_r_k, rwkv_lnx_w, rwkv_lnx_b,
              lam_q1, lam_k1, lam_q2, lam_k2, diff_subln_g, w_proj_a, w_proj_b,
              w_out, norm_mlp_g, w_ff1, w_ff2, norm_ple_g, w_ple, w_ple_gate,
              final_norm_g):
    for i in range(DEPTH):
        h = rmsnorm(x, norm_mix_g[i])
        u = h @ w_in[i]
        u_rwkv, u_attn, u_gate = jnp.split(u, [RWKV_COLS, RWKV_COLS + ATTN_COLS], axis=-1)
        o_a = rwkv7_time_mix(u_rwkv, rwkv_mu[i], rwkv_w0[i], rwkv_w2[i], rwkv_a0[i],
                             rwkv_a2[i], rwkv_g2[i], rwkv_k_k[i], rwkv_k_a[i],
                             rwkv_r_k[i], rwkv_lnx_w[i], rwkv_lnx_b[i])
        lambda_init = 0.8 - 0.6 * math.exp(-0.3 * i)
        o_b = diff_attention(u_attn, lam_q1[i], lam_k1[i], lam_q2[i], lam_k2[i],
                             diff_subln_g[i], lambda_init)
        g_a, g_b = jnp.split(jax.nn.sigmoid(u_gate), 2, axis=-1)
        merged = g_a * (o_a @ w_proj_a[i]) + g_b * (o_b @ w_proj_b[i])
        x = x + merged @ w_out[i]
        h = rmsnorm(x, norm_mlp_g[i])
        x = x + jnp.square(jax.nn.relu(h @ w_ff1[i])) @ w_ff2[i]
        gate = jax.nn.sigmoid(rmsnorm(x, norm_ple_g[i]) @ w_ple_gate[i])
        x = x + (p[i] @ w_ple[i]) * gate
    return rmsnorm(x, final_norm_g)
```

```python
import math
import numpy as np
import concourse.bass as bass
import concourse.mybir as mybir
from concourse.bass_utils import run_bass_kernel_spmd
from contextlib import ExitStack

F32 = mybir.dt.float32
BF16 = mybir.dt.bfloat16
AF = mybir.ActivationFunctionType
ALU = mybir.AluOpType
ENGS = ("pe", "act", "dve", "pool", "sp")


class Prog:
    def __init__(self, nc):
        self.nc = nc
        self.streams = {k: [] for k in ENGS}
        self.count = {}
        self.waited = {k: {} for k in ENGS}
        self.last_writer = {}
        self.readers = {}
        self.semkeys = list(ENGS)
        self.stack = ExitStack()

    def sbuf(self, name, shape, dtype):
        return self.stack.enter_context(self.nc.sbuf_tensor(name, list(shape), dtype))

    def psum(self, name, shape, dtype):
        return self.stack.enter_context(self.nc.psum_tensor(name, list(shape), dtype))

    def _deps(self, eng, reads, writes):
        deps = {}

        def add(k, v):
            if deps.get(k, 0) < v:
                deps[k] = v
        for r in reads:
            t = self.last_writer.get(r)
            if t is not None:
                add(*t)
        for w in writes:
            t = self.last_writer.get(w)
            if t is not None:
                add(*t)
            for k, v in self.readers.get(w, {}).items():
                add(k, v)
        out = []
        for k, v in deps.items():
            if eng == "pe" and k == "pe":
                continue
            if self.waited[eng].get(k, 0) >= v:
                continue
            self.waited[eng][k] = v
            out.append((k, v))
        return out

    def _record(self, tok, reads, writes):
        for w in writes:
            self.last_writer[w] = tok
            self.readers[w] = {}
        for r in reads:
            d = self.readers.setdefault(r, {})
            if d.get(tok[0], 0) < tok[1]:
                d[tok[0]] = tok[1]

    def op(self, eng, fn, reads=(), writes=()):
        waits = self._deps(eng, reads, writes)
        c = self.count.get(eng, 0) + 1
        self.count[eng] = c
        self._record((eng, c), reads, writes)
        self.streams[eng].append((waits, fn, eng, 1))

    def dma(self, eng, semkey, out, in_, reads=(), writes=()):
        if semkey not in self.semkeys:
            self.semkeys.append(semkey)
        waits = self._deps(eng, reads, writes)
        c = self.count.get(semkey, 0) + 16
        self.count[semkey] = c
        self._record((semkey, c), reads, writes)
        fn = lambda e, out=out, in_=in_: e.dma_start(out=out, in_=in_)
        self.streams[eng].append((waits, fn, semkey, 16))

    def wait_all(self, eng):
        waits = []
        for k, v in self.count.items():
            if self.waited[eng].get(k, 0) >= v:
                continue
            self.waited[eng][k] = v
            waits.append((k, v))
        self.streams[eng].append((waits, None, None, 0))

    def barrier(self):
        for e in ENGS:
            self.wait_all(e)

    def emit(self):
        nc = self.nc
        sems = {k: self.stack.enter_context(nc.semaphore("s_" + k)) for k in self.semkeys}
        block = self.stack.enter_context(nc.Block())

        def run(engname):
            def body(e):
                for waits, fn, semkey, inc in self.streams[engname]:
                    for k, v in waits:
                        e.wait_ge(sems[k], v)
                    if fn is not None:
                        fn(e).then_inc(sems[semkey], inc)
            return body
        block.tensor(run("pe"))
        block.scalar(run("act"))
        block.vector(run("dve"))
        block.gpsimd(run("pool"))
        block.sync(run("sp"))

    def close(self):
        self.stack.close()


D = 1024
S = 2048
NTILE = 16
C0 = math.exp(-0.5)
GN_EPS = 64e-5
ARENA_BYTES = 85 * 1024
SERIAL = False
SMALL = ["rwkv_mu", "rwkv_w0", "rwkv_a0", "rwkv_k_k", "rwkv_k_a", "rwkv_r_k", "rwkv_lnx_w", "rwkv_lnx_b",
         "lam_q1", "lam_k1", "lam_q2", "lam_k2", "diff_subln_g", "norm_mix_g", "norm_mlp_g", "norm_ple_g",
         "final_norm_g"]
BIGW = {"w_in": [4, 1024, 5376], "rwkv_w2": [4, 64, 512], "rwkv_a2": [4, 64, 512], "rwkv_g2": [4, 128, 512],
        "w_proj_a": [4, 512, 1024], "w_proj_b": [4, 512, 1024], "w_out": [4, 1024, 1024],
        "w_ff1": [4, 1024, 4096], "w_ff2": [4, 4096, 1024], "w_ple": [4, 256, 1024],
        "w_ple_gate": [4, 1024, 1024]}
SMALL_SHAPES = {"rwkv_mu": [4, 1792], "rwkv_w0": [4, 512], "rwkv_a0": [4, 512], "rwkv_k_k": [4, 512],
                "rwkv_k_a": [4, 512], "rwkv_r_k": [4, 512], "rwkv_lnx_w": [4, 512], "rwkv_lnx_b": [4, 512],
                "lam_q1": [1, 256], "lam_k1": [1, 256], "lam_q2": [1, 256], "lam_k2": [1, 256],
                "diff_subln_g": [4, 128], "norm_mix_g": [4, 1024], "norm_mlp_g": [4, 1024],
                "norm_ple_g": [4, 1024], "final_norm_g": [1, 1024]}


def build(depth=4, dbg=(), phases=6):
    nc = bass.Bass("TRN2", target_bir_lowering=False)
    P = Prog(nc)
    din = lambda n, s: nc.dram_tensor(n, list(s), F32, kind="ExternalInput").ap()
    x_d = din("x", [S, D])
    p_d = din("p", [4, S, 256])
    W = {k: din(k, s) for k, s in BIGW.items()}
    SM = {k: din(k, s) for k, s in SMALL_SHAPES.items()}
    alq_d = din("alq", [16, S])
    alk_d = din("alk", [16, S])
    y_d = nc.dram_tensor("y", [S, D], F32, kind="ExternalOutput").ap()
    dbg_d = {}

    xres = P.sbuf("xres", [128, NTILE, D], F32)
    hT = P.sbuf("hT", [128, 8, S], BF16)
    oaT = P.sbuf("oaT", [128, 4, S], BF16)
    ident_bf = P.sbuf("ident_bf", [128, 128], BF16)
    bo1_f = P.sbuf("bo1_f", [128, 128], F32)
    ones_f = P.sbuf("ones_f", [128, 128], F32)
    ones_bf = P.sbuf("ones_bf", [128, 128], BF16)
    maskN1 = P.sbuf("maskN1", [128, 128], BF16)
    maskNT1 = P.sbuf("maskNT1", [128, 128], BF16)
    maskIU1 = P.sbuf("maskIU1", [128, 64], BF16)
    trimask = P.sbuf("trimask", [128, 128], BF16)
    m64 = P.sbuf("m64", [128, 512], BF16)
    colA = P.sbuf("colA", [128, 120], F32)
    colB = P.sbuf("colB", [128, 52], F32)
    colC = P.sbuf("colC", [128, 96], F32)
    omka = P.sbuf("omka", [128, 16], F32)
    omm = P.sbuf("omm", [128, 56], F32)
    lamt = P.sbuf("lamt", [128, 16], F32)
    neglam = P.sbuf("neglam", [128, 4], F32)
    sgc = P.sbuf("sgc", [128, 4], F32)
    ssq = P.sbuf("ssq", [128, 16], F32)
    rstd = P.sbuf("rstd", [128, 16], F32)
    Zs = [[P.sbuf(f"Z{g}_{i}", [128, 128], F32) for i in range(2)] for g in range(4)]
    gprev = P.sbuf("gprev", [128, 4], F32)
    lastcol = P.sbuf("lastcol", [128, 14], F32)
    gpc = P.sbuf("gpc", [128, 8], F32)
    arena = P.sbuf("arena", [128, ARENA_BYTES // 4], F32)

    b4 = lambda ap: ap.rearrange("p (a b) -> p a b", b=128)
    maskN4 = maskN1[:, :].unsqueeze(1).broadcast_to([128, 4, 128])
    maskNT4 = maskNT1[:, :].unsqueeze(1).broadcast_to([128, 4, 128])
    maskIU4 = maskIU1[:, :].unsqueeze(1).broadcast_to([128, 8, 64])
    ident4 = ident_bf[:, :].unsqueeze(1).broadcast_to([128, 4, 128])
    pf = [P.psum(f"pf{i}", [128, 512], F32) for i in range(6)]
    ptb = [P.psum(f"pt{i}", [128, 1024], BF16) for i in range(2)]

    class Arena:
        def __init__(self):
            self.off = 0

        def reset(self, off=0):
            self.off = off

        def get(self, shape, dtype):
            n = int(np.prod(shape[1:]))
            nbytes = n * (4 if dtype == F32 else 2)
            nbytes = (nbytes + 31) // 32 * 32
            o = self.off
            self.off += nbytes
            assert self.off <= ARENA_BYTES, (self.off, ARENA_BYTES)
            v = arena[:, o // 4:(o + nbytes) // 4]
            if dtype != F32:
                v = v.bitcast(dtype)
            v = v[:, 0:n]
            if len(shape) == 3:
                v = v.rearrange("p (a b) -> p a b", b=shape[2])
            return v
    AR = Arena()
    cmk = AR.get([128, 512], F32)
    lamb = AR.get([128, 4, 256], F32)
    rowsA = AR.get([128, 128], F32)
    ident_f = AR.get([128, 128], F32)
    rot = {"pf": 0}

    def nextpf():
        i = rot["pf"]
        rot["pf"] = (i + 1) % 5
        return pf[i], f"pf{i}"

    def mm(out, lhsT, rhs, start, stop, reads, writes):
        P.op("pe", lambda e: e.matmul(out=out, lhsT=lhsT, rhs=rhs, start=start, stop=stop), reads, writes)

    def tr(out, in_, ident, reads, writes):
        P.op("pe", lambda e: e.transpose(out=out, in_=in_, identity=ident), reads, writes)

    def act(out, in_, func, reads, writes, **kw):
        P.op("act", lambda e: e.activation(out=out, in_=in_, func=func, **kw), reads, writes)

    def tt(out, in0, in1, op, reads, writes, eng="dve"):
        P.op(eng, lambda e: e.tensor_tensor(out=out, in0=in0, in1=in1, op=op), reads, writes)

    def ts(out, in0, s1, s2, op0, op1, reads, writes, eng="dve"):
        if op1 is None:
            P.op(eng, lambda e: e.tensor_scalar(out=out, in0=in0, scalar1=s1, scalar2=None, op0=op0), reads, writes)
        else:
            P.op(eng, lambda e: e.tensor_scalar(out=out, in0=in0, scalar1=s1, scalar2=s2, op0=op0, op1=op1), reads, writes)

    def stt(out, in0, scalar, in1, op0, op1, reads, writes):
        P.op("dve", lambda e: e.scalar_tensor_tensor(out=out, in0=in0, scalar=scalar, in1=in1, op0=op0, op1=op1), reads, writes)

    def recip(out, in_, reads, writes):
        P.op("dve", lambda e: e.reciprocal(out=out, in_=in_), reads, writes)

    def memset(ap, val, writes, eng="pool"):
        P.op(eng, lambda e: e.memset(ap, val), (), writes)

    def asel(out, in_, pattern, cmp, fill, base, cm, reads, writes):
        P.op("pool", lambda e: e.affine_select(out=out, in_=in_, pattern=pattern, compare_op=cmp, fill=fill,
                                               base=base, channel_multiplier=cm), reads, writes)

    def dump(name, ap, shape, reads):
        if name in dbg:
            d = nc.dram_tensor("dbg_" + name, list(shape), F32, kind="ExternalOutput").ap()
            dbg_d[name] = d
            P.dma("pool", "dbg_" + name, d, ap, reads=reads)

    memset(ident_f[:], 0.0, ["ident_f"])
    asel(ident_f[:], ident_f[:], [[-1, 128]], ALU.not_equal, 1.0, 0, 1, ["ident_f"], ["ident_f"])
    tt(ident_bf[:], ident_f[:], ident_f[:], ALU.mult, ["ident_f"], ["ident_bf"])
    memset(ones_f[:], 1.0, ["ones_f"])
    memset(ones_bf[:], 1.0, ["ones_bf"])
    memset(bo1_f[:], 0.0, ["bo1_f"])
    memset(bo1_f[0:64, 0:64], 1.0, ["bo1_f"])
    memset(bo1_f[64:128, 64:128], 1.0, ["bo1_f"])
    memset(cmk[:, 0:128], 0.0, ["cmk"])
    for hh in range(2):
        sl = slice(64 * hh, 64 * hh + 64)
        memset(cmk[sl, 64 * hh:64 * hh + 64], 1.0, ["cmk"])
        asel(cmk[sl, 64 * hh:64 * hh + 64], cmk[sl, 64 * hh:64 * hh + 64], [[1, 64]], ALU.is_gt, 0.0, 0, -1, ["cmk"], ["cmk"])
    tt(maskN1[:], cmk[:, 0:128], cmk[:, 0:128], ALU.mult, ["cmk"], ["maskN4"])
    memset(cmk[:, 128:256], 0.0, ["cmk2"])
    for hh in range(2):
        sl = slice(64 * hh, 64 * hh + 64)
        memset(cmk[sl, 128 + 64 * hh:128 + 64 * hh + 64], 1.0, ["cmk2"])
        asel(cmk[sl, 128 + 64 * hh:128 + 64 * hh + 64], cmk[sl, 128 + 64 * hh:128 + 64 * hh + 64], [[-1, 64]], ALU.is_gt, 0.0, 0, 1, ["cmk2"], ["cmk2"])
    tt(maskNT1[:], cmk[:, 128:256], cmk[:, 128:256], ALU.mult, ["cmk2"], ["maskNT4"])
    memset(cmk[:, 256:320], 1.0, ["cmk3"])
    for hh in range(2):
        sl = slice(64 * hh, 64 * hh + 64)
        asel(cmk[sl, 256:320], cmk[sl, 256:320], [[1, 64]], ALU.is_ge, 0.0, 0, -1, ["cmk3"], ["cmk3"])
    tt(maskIU1[:], cmk[:, 256:320], cmk[:, 256:320], ALU.mult, ["cmk3"], ["maskIU4"])
    memset(cmk[:, 384:512], 0.0, ["cmk4"])
    asel(cmk[:, 384:512], cmk[:, 384:512], [[1, 128]], ALU.is_ge, -30000.0, 0, -1, ["cmk4"], ["cmk4"])
    tt(trimask[:], cmk[:, 384:512], ones_f[:], ALU.mult, ["cmk4", "ones_f"], ["trimask"])
    memset(m64[:], 1.0, ["m64"])
    memset(m64[:].rearrange("p (c t) -> p c t", t=64)[:, :, 0:1], 0.0, ["m64"])
    for g in range(4):
        for i in range(2):
            memset(Zs[g][i][:], 0.0, [f"Z{g}_{i}"])

    def load_rows(specs, dst, ncols, dkey):
        r = 0
        for name, nrow in specs:
            src = SM[name].rearrange("l (c p) -> (l c) p", p=128)
            P.dma("sp", "d_rows", rowsA[r:r + nrow, :], src, writes=["rowsA"])
            r += nrow
        assert r == ncols
        ps, pk = nextpf()
        tr(ps[:, 0:ncols], rowsA[0:ncols, :], ident_f[0:ncols, 0:ncols], ["rowsA", "ident_f"], [pk])
        act(dst[:, 0:ncols], ps[:, 0:ncols], AF.Copy, [pk], [dkey])
    load_rows([("rwkv_mu", 56), ("rwkv_w0", 16), ("rwkv_a0", 16), ("rwkv_k_k", 16), ("rwkv_k_a", 16)], colA, 120, "colA")
    load_rows([("rwkv_r_k", 16), ("rwkv_lnx_w", 16), ("rwkv_lnx_b", 16), ("diff_subln_g", 4)], colB, 52, "colB")
    load_rows([("norm_mix_g", 32), ("norm_mlp_g", 32), ("norm_ple_g", 32)], colC, 96, "colC")
    CA, CB, CC = "colA", "colB", "colC"
    ts(omka[:], colA[:, 104:120], -1.0, 1.0, ALU.mult, ALU.add, [CA], ["omka"])
    ts(omm[:], colA[:, 0:56], -1.0, 1.0, ALU.mult, ALU.add, [CA], ["omm"])
    for i, nm in enumerate(["lam_q1", "lam_k1", "lam_q2", "lam_k2"]):
        P.dma("sp", "d_lam", lamb[:, i, :], SM[nm].partition_broadcast(128).rearrange("p o n -> p (o n)"),
              writes=["lamb"])
    tt(lamb[:, 0, :], lamb[:, 0, :], lamb[:, 1, :], ALU.mult, ["lamb"], ["lamb"])
    tt(lamb[:, 2, :], lamb[:, 2, :], lamb[:, 3, :], ALU.mult, ["lamb"], ["lamb"])
    P.op("dve", lambda e: e.tensor_reduce(out=lamt[:, 0:4], in_=lamb[:, 0, :].rearrange("p (l d) -> p l d", d=64),
                                          axis=mybir.AxisListType.X, op=ALU.add), ["lamb"], ["lamt"])
    P.op("dve", lambda e: e.tensor_reduce(out=lamt[:, 4:8], in_=lamb[:, 2, :].rearrange("p (l d) -> p l d", d=64),
                                          axis=mybir.AxisListType.X, op=ALU.add), ["lamb"], ["lamt"])
    act(lamt[:, 8:16], lamt[:, 0:8], AF.Exp, ["lamt"], ["lamt"])
    tt(lamt[:, 0:4], lamt[:, 12:16], lamt[:, 8:12], ALU.subtract, ["lamt"], ["lamt"])
    for l in range(4):
        li = 0.8 - 0.6 * math.exp(-0.3 * l)
        ts(neglam[:, l:l + 1], lamt[:, l:l + 1], -li, None, ALU.add, None, ["lamt"], ["neglam"])
        ts(sgc[:, l:l + 1], colB[:, 48 + l:49 + l], 1.0 - li, None, ALU.mult, None, [CB], ["sgc"])

    xv = x_d.rearrange("(t p) d -> p t d", p=128)
    for i in range(4):
        P.dma("sp", f"d_x{i}", xres[:, 4 * i:4 * i + 4, :], xv[:, 4 * i:4 * i + 4, :], writes=[f"x{t}" for t in range(4 * i, 4 * i + 4)])

    wdma = {"i": 0}

    def wload(dst, src, key):
        P.dma("pool", "dw_" + key, dst, src, writes=[key])

    XK = [f"x{t}" for t in range(NTILE)]
    HK = [f"hT{t}" for t in range(NTILE)]

    XN_OFF = ARENA_BYTES - 4096
    AR.reset(XN_OFF)
    xn_fix = [AR.get([128, D], BF16) for _ in range(2)]
    AR.reset(0)
    yv = y_d.rearrange("(t p) d -> p t d", p=128)

    def tile_rstd(t):
        b = t % 2
        act(xn_fix[b], xres[:, t, :], AF.Square, [f"x{t}"], [f"xn{b}", f"ssq{t}"], accum_out=ssq[:, t:t + 1])
        act(rstd[:, t:t + 1], ssq[:, t:t + 1], AF.Sqrt, [f"ssq{t}"], [f"rstd{t}"], scale=1.0 / D, bias=1e-6)
        recip(rstd[:, t:t + 1], rstd[:, t:t + 1], [f"rstd{t}"], [f"rstd{t}"])

    def norm_front(t, gbase):
        b = t % 2
        tile_rstd(t)
        ts(xn_fix[b], xres[:, t, :], rstd[:, t:t + 1], None, ALU.mult, None, [f"x{t}", f"rstd{t}"], [f"xn{b}"])

    def norm_back(t, gbase):
        b = t % 2
        for kc in range(8):
            tr(ptb[b][:, kc * 128:(kc + 1) * 128], xn_fix[b][:, kc * 128:(kc + 1) * 128], ident_bf[:], [f"xn{b}", "ident_bf"], [f"pt{b}"])
        tt(hT[:, :, t * 128:(t + 1) * 128], ptb[b][:, :].rearrange("p (a b) -> p a b", b=128),
           colC[:, gbase:gbase + 8].unsqueeze(2).broadcast_to([128, 8, 128]), ALU.mult, [f"pt{b}", CC], [f"hT{t}"])

    def norm_tile(t, gbase):
        norm_front(t, gbase)
        norm_back(t, gbase)

    norm_def = []

    def norm_flush(keep=0):
        while len(norm_def) > keep:
            norm_def.pop(0)[1]()

    def norm_tile_def(t, gbase):
        while any(tp % 2 == t % 2 for tp, _ in norm_def):
            norm_def.pop(0)[1]()
        norm_front(t, gbase)
        norm_def.append((t, lambda t=t, gbase=gbase: norm_back(t, gbase)))

    def norm_hT(gbase):
        for t in range(NTILE):
            norm_tile(t, gbase)

    fin = {}

    def final_tile(t):
        tile_rstd(t)
        stt(fin["ob"], xres[:, t, :], rstd[:, t:t + 1], fin["gfin"], ALU.mult, ALU.mult, [f"x{t}", f"rstd{t}", "gfin"], ["obf"])
        P.dma("sp", "d_y", yv[:, t, :], fin["ob"], reads=["obf"])

    def hk(tok0, n):
        return [f"hT{t}" for t in range(tok0 // 128, (tok0 + n) // 128)]

    def rwkv(l):
        AR.reset(0)
        wsmU = [[AR.get([128, 8, 128], BF16) for _ in range(3)] for _ in range(2)]
        wsmL = [AR.get([128, 8, 128], BF16) for _ in range(2)]
        wl2 = AR.get([128, 512], BF16)
        wg2 = AR.get([128, 512], BF16)
        raw = AR.get([128, 520], F32)
        T1 = AR.get([128, 512], F32)
        T2 = AR.get([128, 512], F32)
        xr = AR.get([128, 512], F32)
        xk = AR.get([128, 512], F32)
        xvv = AR.get([128, 512], F32)
        sg = AR.get([128, 512], F32)
        cs = AR.get([128, 512], F32)
        eg = AR.get([128, 512], F32)
        einv = AR.get([128, 512], F32)
        al = AR.get([128, 512], F32)
        lwh = AR.get([128, 512], BF16)
        lgh = AR.get([128, 512], BF16)
        pkgs = []
        for pi in range(2):
            pg = {k: AR.get([128, 8, 128], BF16) for k in "abkv"}
            pg["rt"] = AR.get([128, 512], BF16)
            pg["bon"] = AR.get([128, 512], BF16)
            pg["gg"] = AR.get([128, 512], BF16)
            pg["gpc"] = AR.get([128, 8], F32)
            pkgs.append(pg)
        yT = AR.get([128, 512], F32)
        XS = [[AR.get([128, 512], BF16) for _ in range(2)]] * 2
        XTS = [[AR.get([128, 512], BF16) for _ in range(2)]] * 2
        QS = [[AR.get([128, 512], BF16) for _ in range(2)]] * 2
        QFIN = [None, None]
        AKT = AR.get([128, 512], BF16)
        ARS = AR.get([128, 512], BF16)
        TMab = AR.get([128, 1024], BF16)
        TMkv = AR.get([128, 1024], BF16)
        AH = AR.get([128, 512], BF16)
        AV = AR.get([128, 512], BF16)
        PV = AR.get([128, 512], BF16)
        RH = AR.get([128, 256], BF16)
        MT = AR.get([128, 512], F32)
        G1 = MT
        S0b = AR.get([128, 128], BF16)

        for pi in range(2):
            for k in "abkv":
                memset(pkgs[pi][k], 0.0, [f"pad{k}{pi}"])
        memset(lastcol[:], 0.0, ["lastcol"])
        memset(gprev[:], 0.0, ["gprev"])
        wload(wl2[0:64, :], W["rwkv_w2"][l], "wl2")
        wload(wl2[64:128, :], W["rwkv_a2"][l], "wl2")
        wload(wg2[:, :], W["rwkv_g2"][l], "wg2")
        win = W["w_in"][l].rearrange("(kc p) n -> p kc n", p=128)
        wload(wsmL[0], win[:, :, 1536:1664], "wsmL0")
        wload(wsmL[1], win[:, :, 1664:1792], "wsmL1")
        zi = [0, 0, 0, 0]

        def load_unit_weights(u):
            qt, g = divmod(u, 4)
            st = u % 2
            for ti in range(3):
                cc = ti * 4 + g
                wload(wsmU[st][ti], win[:, :, cc * 128:(cc + 1) * 128], f"wsmU{st}{ti}")

        def shift_proj(wbuf, wkey, cc, tok0, dst, dkey):
            ps, pk = nextpf()
            for kc in range(8):
                mm(ps[:], wbuf[:, kc, :], hT[:, kc, tok0:tok0 + 512], kc == 0, kc == 7, [wkey] + hk(tok0, 512), [pk])
            mc = l * 14 + cc
            act(raw[:, 0:1], lastcol[:, cc:cc + 1], AF.Copy, ["lastcol"], ["raw"])
            act(raw[:, 1:513], ps[:], AF.Copy, [pk, CA], ["raw"], scale=colA[:, mc:mc + 1])
            act(T1, ps[:], AF.Copy, [pk, "omm"], ["T1"], scale=omm[:, mc:mc + 1])
            yield
            act(lastcol[:, cc:cc + 1], raw[:, 512:513], AF.Copy, ["raw"], ["lastcol"])
            tt(dst, T1, raw[:, 0:512], ALU.add, ["T1", "raw"], [dkey])
            yield

        v3 = lambda ap, sl: ap[sl, :].rearrange("p (c t) -> p c t", t=64)

        def prep1(u):
            qt, g = divmod(u, 4)
            tok0 = qt * 512
            gs = slice(g * 128, (g + 1) * 128)
            cg = l * 4 + g
            st = u % 2
            if u + 1 < 16:
                load_unit_weights(u + 1)
            if g == 0:
                yield from shift_proj(wsmL[0], "wsmL0", 12, tok0, T2, "T2")
                act(lwh[0:64, :], T2[0:64, :], AF.Tanh, ["T2"], ["lwh"])
                act(lwh[64:128, :], T2[64:128, :], AF.Copy, ["T2"], ["lwh"])
                yield
                yield from shift_proj(wsmL[1], "wsmL1", 13, tok0, T2, "T2")
                act(lgh, T2, AF.Sigmoid, ["T2"], ["lgh"])
                yield
            yield from shift_proj(wsmU[st][0], f"wsmU{st}0", g, tok0, xr, "xr")
            yield from shift_proj(wsmU[st][1], f"wsmU{st}1", 4 + g, tok0, xk, "xk")
            yield from shift_proj(wsmU[st][2], f"wsmU{st}2", 8 + g, tok0, xvv, "xv")
            ps, pk = nextpf()
            mm(ps[:], wl2[0:64, gs], lwh[0:64, :], True, True, ["wl2", "lwh"], [pk])
            act(sg, ps[:], AF.Sigmoid, [pk, CA], ["sg"], bias=colA[:, 56 + cg:57 + cg])
            yield
            ps, pk = nextpf()
            mm(ps[:], wl2[64:128, gs], lwh[64:128, :], True, True, ["wl2", "lwh"], [pk])
            act(al, ps[:], AF.Sigmoid, [pk, CA], ["al"], bias=colA[:, 72 + cg:73 + cg])
            yield
            act(T1, xk, AF.Square, ["xk", CA], ["T1"], scale=colA[:, 88 + cg:89 + cg])
            yield
            P.op("dve", lambda e: e.tensor_tensor_scan(out=cs, data0=m64[:], data1=sg, initial=0.0, op0=ALU.mult, op1=ALU.add),
                 ["m64", "sg"], ["cs"])
            ps, pk = nextpf()
            mm(ps[:], bo1_f[:], T1, True, True, ["bo1_f", "T1"], [pk])
            ts(T1, ps[:], 1e-24, None, ALU.max, None, [pk], ["T1"])
            yield
            act(eg, cs, AF.Exp, ["cs"], ["eg"], scale=-C0)
            yield
            act(einv, cs, AF.Exp, ["cs"], ["einv"], scale=C0)
            ts(T2, al, colA[:, 104 + cg:105 + cg], omka[:, cg:cg + 1], ALU.mult, ALU.add, ["al", CA, "omka"], ["T2"])
            yield
            act(T1, T1, AF.Ln, ["T1"], ["T1"])
            tt(cs, cs, sg, ALU.subtract, ["cs", "sg"], ["cs"])
            yield
            act(T1, T1, AF.Exp, ["T1"], ["T1"], scale=-0.5)
            yield
            act(sg, cs, AF.Exp, ["cs"], ["sg"], scale=-C0)
            stt(cs, xk, colA[:, 88 + cg:89 + cg], T1, ALU.mult, ALU.mult, ["xk", "T1", CA], ["cs"])
            yield
            tt(xk, xk, T2, ALU.mult, ["xk", "T2"], ["xk"])
            yield
            stt(T2, xr, colB[:, cg:cg + 1], xk, ALU.mult, ALU.mult, ["xr", "xk", CB], ["T2"])
            yield
            tt(T1, al, einv, ALU.mult, ["al", "einv"], ["T1"])
            yield

        def prep2(u, pi):
            qt, g = divmod(u, 4)
            pg = pkgs[pi]
            K = lambda n: f"{n}{pi}"
            gs = slice(g * 128, (g + 1) * 128)
            kkn, egm = cs, sg
            ps, pk = nextpf()
            mm(ps[:], wg2[:, gs], lgh, True, True, ["wg2", "lgh"], [pk])
            act(pg["gg"], ps[:], AF.Copy, [pk], [K("gg")])
            yield
            ps, pk = nextpf()
            mm(ps[:], bo1_f[:], T2, True, True, ["bo1_f", "T2"], [pk])
            tt(pg["bon"], ps[:], xvv, ALU.mult, [pk, "xv"], [K("bon")])
            yield
            for hh in range(2):
                sl = slice(64 * hh, 64 * hh + 64)
                stt(pg["a"][sl, :, sl], v3(kkn, sl), -1.0, v3(egm, sl), ALU.mult, ALU.mult, ["cs", "sg"], [K("pada")])
                yield
                tt(pg["b"][sl, :, sl], v3(kkn, sl), v3(T1, sl), ALU.mult, ["cs", "T1"], [K("padb")])
                yield
                tt(pg["k"][sl, :, sl], v3(xk, sl), v3(einv, sl), ALU.mult, ["xk", "einv"], [K("padk")])
                act(pg["v"][sl, :, sl], v3(xvv, sl), AF.Copy, ["xv"], [K("padv")])
                yield
            tt(pg["rt"], xr, eg, ALU.mult, ["xr", "eg"], [K("rt")])
            act(pg["gpc"][:, 0:1], gprev[:, g:g + 1], AF.Copy, ["gprev"], [K("gpc")])
            yield
            act(pg["gpc"][:, 1:8], eg.rearrange("p (c t) -> p c t", t=64)[:, 0:7, 63], AF.Copy, ["eg"], [K("gpc")])
            act(gprev[:, g:g + 1], eg[:, 511:512], AF.Copy, ["eg"], ["gprev"])
            yield

        blk = lambda ap, i: ap[:, i * 128:(i + 1) * 128]

        def neumann(k):
            u, q = divmod(k, 2)
            pi = u % 2
            pg = pkgs[pi]
            K = lambda n: f"{n}{pi}"
            s_ = k % 2
            X_, XT_, Q_ = XS[s_], XTS[s_], QS[s_]
            xk_ = lambda n, i: f"{n}_{i}"
            pada, padb = pg["a"], pg["b"]
            chs = [4 * q + i for i in range(4)]
            ps, pk = nextpf()
            for i, c in enumerate(chs):
                mm(blk(ps, i), padb[:, c, :], pada[:, c, :], True, True, [K("pada"), K("padb")], [pk])
            tt(b4(X_[0]), b4(ps[:]), maskN4, ALU.mult, [pk, "maskN4"], [xk_("X", 0)])
            ps, pk = nextpf()
            for i, c in enumerate(chs):
                mm(blk(ps, i), pada[:, c, :], padb[:, c, :], True, True, [K("pada"), K("padb")], [pk])
            tt(b4(XT_[0]), b4(ps[:]), maskNT4, ALU.mult, [pk, "maskNT4"], [xk_("XT", 0)])
            tt(b4(Q_[0]), b4(X_[0]), ident4, ALU.add, [xk_("X", 0), "ident_bf"], [xk_("Q", 0)], eng="pool")
            yield
            cur = 0
            for lev in range(1, 6):
                nx = 1 - cur
                ps, pk = nextpf()
                for i in range(4):
                    mm(blk(ps, i), blk(X_[cur], i), blk(XT_[cur], i), True, True, [xk_("X", cur), xk_("XT", cur)], [pk])
                act(XT_[nx], ps[:], AF.Copy, [pk], [xk_("XT", nx)])
                if lev < 5:
                    ps, pk = nextpf()
                    for i in range(4):
                        mm(blk(ps, i), blk(XT_[cur], i), blk(X_[cur], i), True, True, [xk_("X", cur), xk_("XT", cur)], [pk])
                    P.op("dve", lambda e, o=X_[nx], s=ps: e.tensor_copy(out=o, in_=s[:]), [pk], [xk_("X", nx)])
                yield
                ps, pk = nextpf()
                for i in range(4):
                    mm(blk(ps, i), blk(XT_[nx], i), blk(Q_[cur], i), True, True, [xk_("XT", nx), xk_("Q", cur)], [pk])
                tt(Q_[nx], ps[:], Q_[cur], ALU.add, [pk, xk_("Q", cur)], [xk_("Q", nx)])
                cur = nx
                yield
            QFIN[s_] = (Q_[cur], xk_("Q", cur))

        def tail(k):
            u, q = divmod(k, 2)
            pi = u % 2
            qt, g = divmod(u, 4)
            tok0 = qt * 512
            pg = pkgs[pi]
            K = lambda n: f"{n}{pi}"
            cg = l * 4 + g
            pada, padb, padk, padv, rt, gpc_ = pg["a"], pg["b"], pg["k"], pg["v"], pg["rt"], pg["gpc"]
            chs = [4 * q + i for i in range(4)]
            ps, pk = nextpf()
            for i, c in enumerate(chs):
                mm(blk(ps, i), padk[:, c, :], pada[:, c, :], True, True, [K("pada"), K("padk")], [pk])
            tt(b4(AKT), b4(ps[:]), maskN4, ALU.mult, [pk, "maskN4"], ["AKT"])
            ps, pk = nextpf()
            for i, c in enumerate(chs):
                mm(ps[:, i * 128:i * 128 + 64], padb[:, c, :], rt[:, c * 64:(c + 1) * 64], True, True, [K("padb"), K("rt")], [pk])
                mm(ps[:, i * 128 + 64:(i + 1) * 128], padk[:, c, :], rt[:, c * 64:(c + 1) * 64], True, True, [K("padk"), K("rt")], [pk])
            tt(ARS.rearrange("p (a b) -> p a b", b=64), ps[:].rearrange("p (a b) -> p a b", b=64), maskIU4, ALU.mult, [pk, "maskIU4"], ["ARS"])
            yield
            for i, c in enumerate(chs):
                tr(ptb[0][:, i * 128:(i + 1) * 128], pada[:, c, :], ident_bf[:], [K("pada"), "ident_bf"], ["pt0"])
                tr(ptb[0][:, (4 + i) * 128:(5 + i) * 128], padb[:, c, :], ident_bf[:], [K("padb"), "ident_bf"], ["pt0"])
            act(TMab, ptb[0][:, :], AF.Copy, ["pt0"], ["TMab"])
            for i, c in enumerate(chs):
                tr(ptb[1][:, i * 128:(i + 1) * 128], padk[:, c, :], ident_bf[:], [K("padk"), "ident_bf"], ["pt1"])
                tr(ptb[1][:, (4 + i) * 128:(5 + i) * 128], padv[:, c, :], ident_bf[:], [K("padv"), "ident_bf"], ["pt1"])
            P.op("dve", lambda e: e.tensor_copy(out=TMkv, in_=ptb[1][:, :]), ["pt1"], ["TMkv"])
            yield
            QF, QK = QFIN[k % 2]
            TMa = lambda i: TMab[:, i * 128:(i + 1) * 128]
            TMb = lambda i: TMab[:, (4 + i) * 128:(5 + i) * 128]
            TMk = lambda i: TMkv[:, i * 128:(i + 1) * 128]
            TMv = lambda i: TMkv[:, (4 + i) * 128:(5 + i) * 128]
            ps, pk = nextpf()
            for i in range(4):
                mm(blk(ps, i), blk(QF, i), TMa(i), True, True, [QK, "TMab"], [pk])
            act(AH, ps[:], AF.Copy, [pk], ["AH"])
            ps, pk = nextpf()
            for i in range(4):
                mm(blk(ps, i), blk(AKT, i), TMv(i), True, True, ["AKT", "TMkv"], [pk])
            P.op("dve", lambda e, s=ps: e.tensor_copy(out=AV, in_=s[:]), [pk], ["AV"])
            yield
            ps, pk = nextpf()
            for i in range(4):
                mm(blk(ps, i), blk(QF, i), blk(AV, i), True, True, [QK, "AV"], [pk])
            act(PV, ps[:], AF.Copy, [pk], ["PV"])
            ps, pk = nextpf()
            for i in range(4):
                mm(ps[:, i * 64:(i + 1) * 64], blk(AH, i), ARS[:, i * 128:i * 128 + 64], True, True, ["AH", "ARS"], [pk])
            tt(RH, ps[:, 0:256], rt[:, q * 256:(q + 1) * 256], ALU.add, [pk, K("rt")], ["RH"])
            ps, pk = nextpf()
            for i in range(4):
                mm(blk(ps, i), blk(AH, i), TMb(i), True, True, ["AH", "TMab"], [pk])
            tt(b4(MT), b4(ps[:]), gpc_[:, 4 * q:4 * q + 4].unsqueeze(2).broadcast_to([128, 4, 128]), ALU.mult, [pk, K("gpc")], ["MT"])
            yield
            psy, pyk = pf[5], "pf5"
            for i, c in enumerate(chs):
                gcol = gpc_[:, c:c + 1]
                zc, zck = Zs[g][zi[g]], f"Z{g}_{zi[g]}"
                zn, znk = Zs[g][1 - zi[g]], f"Z{g}_{1 - zi[g]}"
                act(S0b, zc[:], AF.Copy, [zck, K("gpc")], ["S0b"], scale=gcol)
                pss, psk = nextpf()
                mm(pss[:, 0:128], TMb(i), blk(PV, i), True, False, ["TMab", "PV"], [psk])
                mm(pss[:, 0:128], TMk(i), TMv(i), False, False, ["TMkv"], [psk])
                mm(pss[:, 0:128], blk(MT, i), zc[:], False, True, ["MT", zck], [psk])
                stt(zn[:], zc[:], gcol, pss[:, 0:128], ALU.mult, ALU.add, [zck, K("gpc"), psk], [znk])
                yo = psy[:, i * 64:(i + 1) * 64]
                mm(yo, S0b, RH[:, i * 64:(i + 1) * 64], True, False, ["S0b", "RH"], [pyk])
                mm(yo, blk(PV, i), ARS[:, i * 128:i * 128 + 64], False, False, ["PV", "ARS"], [pyk])
                mm(yo, TMv(i), ARS[:, i * 128 + 64:(i + 1) * 128], False, True, ["TMkv", "ARS"], [pyk])
                zi[g] = 1 - zi[g]
                yield
            act(yT[:, q * 256:(q + 1) * 256], psy[:, 0:256], AF.Copy, [pyk], ["yT"])
            if q == 0:
                return
            ps, pk = nextpf()
            mm(ps[:], bo1_f[:], yT, True, True, ["bo1_f", "yT"], [pk])
            stt(yT, ps[:], -1.0 / 64.0, yT, ALU.mult, ALU.add, ["yT", pk], ["yT"])
            act(G1, yT, AF.Square, ["yT"], ["MT"])
            ps, pk = nextpf()
            mm(ps[:], bo1_f[:], G1, True, True, ["bo1_f", "MT"], [pk])
            act(G1, ps[:], AF.Ln, [pk], ["MT"], bias=GN_EPS, scale=1.0 / 64.0)
            act(G1, G1, AF.Exp, ["MT"], ["MT"], scale=-0.5)
            yield
            tt(yT, yT, G1, ALU.mult, ["yT", "MT"], ["yT"])
            ts(yT, yT, colB[:, 16 + cg:17 + cg], colB[:, 32 + cg:33 + cg], ALU.mult, ALU.add, ["yT", CB], ["yT"])
            tt(yT, yT, pg["bon"], ALU.add, ["yT", K("bon")], ["yT"])
            tt(oaT[:, g, tok0:tok0 + 512], yT, pg["gg"], ALU.mult, ["yT", K("gg")], [f"oa{g}_{qt}"])
            yield

        def run_gens(items):
            if SERIAL:
                for gen, n in items:
                    for _ in gen:
                        pass
                return
            alive = [list(it) for it in items]
            while alive:
                for it in list(alive):
                    gen, n = it
                    for _ in range(n):
                        try:
                            next(gen)
                        except StopIteration:
                            alive.remove(it)
                            break

        load_unit_weights(0)
        run_gens([(prep1(0), 1)])
        run_gens([(prep2(0, 0), 1)])
        run_gens([(neumann(0), 1), (prep1(1), 2)])
        NQ = 32
        for k in range(NQ):
            tg = tail(k)
            for _ in range(4):
                next(tg)
            items = [(tg, 1)]
            if k + 1 < NQ:
                items.append((neumann(k + 1), 1))
            u, q = divmod(k, 2)
            if q == 0 and u + 1 < 16:
                items.append((prep2(u + 1, (u + 1) % 2), 1))
            if q == 1 and u + 2 < 16:
                items.append((prep1(u + 2), 2))
            run_gens(items)
        dump("oaT", oaT[:], [128, 4, S], [f"oa{g}_{qt}" for g in range(4) for qt in range(4)])

    def attention(l):
        AR.reset(0)
        obT = AR.get([128, 4, S], BF16)
        vtm = AR.get([128, NTILE, 512], BF16)
        qa = [[AR.get([128, S], BF16) for _ in range(2)] for _ in range(2)]
        ka = [[AR.get([128, S], BF16) for _ in range(2)] for _ in range(2)]
        stage = AR.get([128, S], BF16)
        wq = [AR.get([128, 8, 128], BF16) for _ in range(2)]
        wv_off = AR.off
        wv = AR.get([128, 8, 512], BF16)
        PT = [AR.get([128, 512], BF16) for _ in range(4)]
        STILES = [(pf[0], "pf0"), (pf[1], "pf1"), (ptb[1][:, :].bitcast(F32), "pt1")]
        win = W["w_in"][l].rearrange("(kc p) n -> p kc n", p=128)
        wload(wv, win[:, :, 2816:3328], "wv")
        for kt in range(NTILE):
            ps, pk = nextpf()
            for kc in range(8):
                mm(ps[:], hT[:, kc, kt * 128:(kt + 1) * 128], wv[:, kc, :], kc == 0, kc == 7, [f"hT{kt}", "wv"], [pk])
            act(vtm[:, kt, :], ps[:], AF.Copy, [pk], [f"vtm{kt}"])
        P.barrier()
        sv = AR.off
        AR.reset(wv_off)
        Rc = AR.get([128, 512], F32)
        A0 = AR.get([128, 512], F32)
        A1 = AR.get([128, 512], F32)
        Tq = AR.get([128, 512], F32)
        AR.reset(sv)
        wr = 0
        ptr = 0
        deferred = []
        def proj(h):
            nonlocal wr
            hb = h % 2
            for which, cb, aug, sc in (("q", 1792 + h * 128, qa[hb], 0.125), ("k", 2304 + h * 128, ka[hb], 1.0)):
                wload(wq[wr], win[:, :, cb:cb + 128], f"wq{wr}")
                for tc in range(4):
                    ps, pk = nextpf()
                    for kc in range(8):
                        mm(ps[:], wq[wr][:, kc, :], hT[:, kc, tc * 512:(tc + 1) * 512], kc == 0, kc == 7, [f"wq{wr}"] + hk(tc * 512, 512), [pk])
                    ts(aug[0][0:64, tc * 512:(tc + 1) * 512], ps[0:64, :], sc, None, ALU.mult, None, [pk], [f"{which}a{hb}0"])
                    ts(stage[64:128, tc * 512:(tc + 1) * 512], ps[64:128, :], sc, None, ALU.mult, None, [pk], ["stage"])
                P.dma("sp", f"d_rep{which}{hb}", aug[1][0:64, :], stage[64:128, :], reads=["stage"], writes=[f"{which}a{hb}1"])
                wr = 1 - wr
            for c in range(2):
                P.dma("pool", f"d_alq{hb}{c}", qa[hb][c][64:68, :], alq_d[4 * h:4 * h + 4, :], writes=[f"qa{hb}{c}"])
                P.dma("pool", f"d_alk{hb}{c}", ka[hb][c][64:68, :], alk_d[4 * h:4 * h + 4, :], writes=[f"ka{hb}{c}"])

        proj(0)
        pend = []
        sctr = [0]

        def drain(n):
            while len(pend) > n:
                ofn, st, endfn = pend.pop(0)
                ofn(st)
                if endfn is not None:
                    endfn()

        for h in range(4):
            for j in range(4):
                nkt = 4 * (j + 1)
                for c in range(2):
                    O, Ok = pf[2 + 2 * c], f"pf{2 + 2 * c}"
                    Lp, Lk = pf[3 + 2 * c], f"pf{3 + 2 * c}"

                    def s_stage(kt, j=j, c=c, h=h):
                        nonlocal ptr
                        t = kt - 4 * j
                        n0 = 128 * t if t > 0 else 0
                        Sp, Sk = STILES[sctr[0] % 3]
                        sctr[0] += 1
                        mm(Sp[:, n0:512], ka[h % 2][c][0:68, kt * 128:(kt + 1) * 128], qa[h % 2][c][0:68, j * 512 + n0:(j + 1) * 512], True, t < 0,
                           [f"ka{h % 2}{c}", f"qa{h % 2}{c}"], [Sk])
                        if t >= 0:
                            mm(Sp[:, n0:n0 + 128], ident_bf[:], trimask[:], False, True, ["ident_bf", "trimask"], [Sk])
                        pb = PT[ptr]
                        pbk = f"PT{ptr}"
                        ptr = (ptr + 1) % 4
                        act(pb[:, n0:512], Sp[:, n0:512], AF.Exp, [Sk], [pbk])
                        return kt, n0, pb, pbk

                    def o_stage(st, nkt=nkt, O=O, Ok=Ok, Lp=Lp, Lk=Lk, h=h):
                        kt, n0, pb, pbk = st
                        mm(O[:, n0:512], vtm[:, kt, h * 128:(h + 1) * 128], pb[:, n0:512], kt == 0, kt == nkt - 1, [f"vtm{kt}", pbk], [Ok])
                        mm(Lp[:, n0:512], ones_bf[:], pb[:, n0:512], kt == 0, kt == nkt - 1, ["ones_bf", pbk], [Lk])

                    def block_end(c=c, O=O, Ok=Ok, Lp=Lp, Lk=Lk, h=h, j=j):
                        act(Rc, Lp[:], AF.Ln, [Lk], ["Rc"])
                        act(Rc, Rc, AF.Exp, ["Rc"], ["Rc"], scale=-1.0)
                        tt((A0, A1)[c], O[:], Rc, ALU.mult, [Ok, "Rc"], [f"A{c}"])
                        if c == 1:
                            stt(A0, A1, neglam[:, l:l + 1], A0, ALU.mult, ALU.add, ["A0", "A1", "neglam"], ["A0"])
                            act(Tq, A0, AF.Square, ["A0"], ["Tq"])

                            def tail(h=h, j=j):
                                ps = ptb[0][:, :].bitcast(F32)
                                mm(ps, ones_f[:], Tq, True, True, ["ones_f", "Tq"], ["pt0"])
                                act(Tq, ps, AF.Ln, ["pt0"], ["Tq"], scale=1.0 / 128.0, bias=1e-5)
                                act(Tq, Tq, AF.Exp, ["Tq"], ["Tq"], scale=-0.5)
                                tt(A0, A0, Tq, ALU.mult, ["A0", "Tq"], ["A0"])
                                ts(obT[:, h, j * 512:(j + 1) * 512], A0, sgc[:, l:l + 1], None, ALU.mult, None, ["A0", "sgc"], [f"ob{h}_{j}"])
                            deferred.append(tail)

                    for kt in range(nkt):
                        pend.append((o_stage, s_stage(kt), block_end if kt == nkt - 1 else None))
                        drain(2)
                        if kt == 3:
                            while deferred:
                                deferred.pop(0)()
                if j == 0 and h + 1 < 4:
                    drain(0)
                    proj(h + 1)
        drain(0)
        while deferred:
            deferred.pop(0)()
        dump("obT", obT, [128, 4, S], [f"ob{h}_{j}" for h in range(4) for j in range(4)])
        return obT

    def merge(l, obT, after_tile=None):
        AR.reset(16384)
        mT = AR.get([128, 8, S], BF16)
        sv = AR.off
        wga = [AR.get([128, 8, 128], BF16) for _ in range(2)]
        wgb = [AR.get([128, 8, 128], BF16) for _ in range(2)]
        pa = [AR.get([128, 4, 128], BF16) for _ in range(2)]
        pb_ = [AR.get([128, 4, 128], BF16) for _ in range(2)]
        ga = [AR.get([128, 512], BF16) for _ in range(2)]
        gb = [AR.get([128, 512], BF16) for _ in range(2)]
        t1 = AR.get([128, 512], F32)
        t2 = AR.get([128, 512], F32)
        win = W["w_in"][l].rearrange("(kc p) n -> p kc n", p=128)
        wpa = W["w_proj_a"][l].rearrange("(kc p) n -> p kc n", p=128)
        wpb = W["w_proj_b"][l].rearrange("(kc p) n -> p kc n", p=128)
        OAK = [f"oa{g}_{qt}" for g in range(4) for qt in range(4)]
        OBK = [f"ob{h}_{j}" for h in range(4) for j in range(4)]
        gi = 0
        for dch in range(8):
            b = dch % 2
            wload(wga[b], win[:, :, 3328 + dch * 128:3328 + (dch + 1) * 128], f"wga{b}")
            wload(wgb[b], win[:, :, 4352 + dch * 128:4352 + (dch + 1) * 128], f"wgb{b}")
            wload(pa[b], wpa[:, :, dch * 128:(dch + 1) * 128], f"pa{b}")
            wload(pb_[b], wpb[:, :, dch * 128:(dch + 1) * 128], f"pb{b}")
            for j in range(4):
                tk = slice(j * 512, (j + 1) * 512)
                psG, kg_ = nextpf()
                for kc in range(8):
                    mm(psG[:], wga[b][:, kc, :], hT[:, kc, tk], kc == 0, kc == 7, [f"wga{b}"] + hk(j * 512, 512), [kg_])
                act(ga[gi], psG[:], AF.Sigmoid, [kg_], [f"ga{gi}"])
                psH, kh_ = nextpf()
                for kc in range(8):
                    mm(psH[:], wgb[b][:, kc, :], hT[:, kc, tk], kc == 0, kc == 7, [f"wgb{b}"] + hk(j * 512, 512), [kh_])
                act(gb[gi], psH[:], AF.Sigmoid, [kh_], [f"gb{gi}"])
                psA, ka_ = nextpf()
                for kc in range(4):
                    mm(psA[:], pa[b][:, kc, :], oaT[:, kc, tk], kc == 0, kc == 3, [f"pa{b}"] + OAK, [ka_])
                tt(t1, psA[:], ga[gi], ALU.mult, [ka_, f"ga{gi}"], ["t1"])
                psB, kb_ = nextpf()
                for kc in range(4):
                    mm(psB[:], pb_[b][:, kc, :], obT[:, kc, tk], kc == 0, kc == 3, [f"pb{b}"] + OBK, [kb_])
                tt(t2, psB[:], gb[gi], ALU.mult, [kb_, f"gb{gi}"], ["t2"])
                tt(mT[:, dch, tk], t1, t2, ALU.add, ["t1", "t2"], [f"mT{dch}_{j}"])
                gi = 1 - gi
        P.barrier()
        AR.reset(sv)
        wo = [AR.get([128, 8, 512], BF16) for _ in range(2)]
        wov = W["w_out"][l].rearrange("(kc p) n -> p kc n", p=128)
        for half in range(2):
            wload(wo[half], wov[:, :, half * 512:(half + 1) * 512], f"wo{half}")
        for t in range(NTILE):
            norm_flush(keep=1)
            for half in range(2):
                ps, pk = nextpf()
                for kc in range(8):
                    mm(ps[:], mT[:, kc, t * 128:(t + 1) * 128], wo[half][:, kc, :], kc == 0, kc == 7, [f"mT{kc}_{t // 4}", f"wo{half}"], [pk])
                hs = slice(half * 512, (half + 1) * 512)
                tt(xres[:, t, hs], xres[:, t, hs], ps[:], ALU.add, [f"x{t}", pk], [f"x{t}"])
            if after_tile is not None:
                after_tile(t)

    def ffn(l, after_tile=None):
        AR.reset(8192)
        wb = [AR.get([128, 8, 512], BF16) for _ in range(2)]
        fT = AR.get([128, 32, 512], BF16)
        rl = [AR.get([128, 512], F32) for _ in range(2)]
        wb.append(AR.get([128, 8, 512], BF16))
        w1 = W["w_ff1"][l].rearrange("(kc p) n -> p kc n", p=128)
        w2 = W["w_ff2"][l].rearrange("(fc p) n -> p fc n", p=128)
        wi = 0
        ri = 0
        for j in range(4):
            tk = slice(j * 512, (j + 1) * 512)
            for piece in range(8):
                norm_flush(keep=max(0, len(norm_def) - 1))
                wload(wb[wi], w1[:, :, piece * 512:(piece + 1) * 512], f"wb{wi}")
                for f4 in range(4):
                    fc = piece * 4 + f4
                    ps, pk = pf[4 + fc % 2], f"pf{4 + fc % 2}"
                    for kc in range(8):
                        mm(ps[:], wb[wi][:, kc, f4 * 128:(f4 + 1) * 128], hT[:, kc, tk], kc == 0, kc == 7, [f"wb{wi}"] + hk(j * 512, 512), [pk])
                    act(rl[ri], ps[:], AF.Relu, [pk], [f"rl{ri}"])
                    tt(fT[:, fc, :], ps[:], rl[ri], ALU.mult, [pk, f"rl{ri}"], [f"fT{fc}"])
                    ri = 1 - ri
                wi = (wi + 1) % 3
            for half in range(2):
                hs = slice(half * 512, (half + 1) * 512)
                for piece in range(4):
                    wload(wb[wi], w2[:, piece * 8:(piece + 1) * 8, hs], f"wb{wi}")
                    for f8 in range(8):
                        fc = piece * 8 + f8
                        for t4 in range(4):
                            mm(pf[t4][:], fT[:, fc, t4 * 128:(t4 + 1) * 128], wb[wi][:, f8, :], fc == 0, fc == 31, [f"fT{fc}", f"wb{wi}"], [f"pf{t4}"])
                    wi = (wi + 1) % 3
                for t4 in range(4):
                    t = 4 * j + t4
                    tt(xres[:, t, hs], xres[:, t, hs], pf[t4][:], ALU.add, [f"x{t}", f"pf{t4}"], [f"x{t}"])
                    if half == 1 and after_tile is not None:
                        after_tile(t)

    def ple(l, after_tile=None):
        AR.reset(8192)
        wb = [AR.get([128, 8, 512], BF16) for _ in range(2)]
        AR.reset(8192 + 16384 + 32768)
        gt = [AR.get([128, 512], F32) for _ in range(2)]
        AR.reset(8192 + 16384 + 32768 + 4096 + 8192)
        wple = AR.get([128, 2, 1024], BF16)
        pT = AR.get([128, 2, S], BF16)
        pin = [AR.get([128, 256], BF16) for _ in range(2)]
        assert AR.off <= XN_OFF, (AR.off, XN_OFF)
        wg = W["w_ple_gate"][l].rearrange("(kc p) n -> p kc n", p=128)
        for half in range(2):
            wload(wb[half], wg[:, :, half * 512:(half + 1) * 512], f"wb{half}")
        wload(wple, W["w_ple"][l].rearrange("(kc p) n -> p kc n", p=128), "wple")
        for t in range(NTILE):
            b = t % 2
            P.dma("pool", f"d_pin{b}", pin[b], p_d[l, t * 128:(t + 1) * 128, :], writes=[f"pin{b}"])
            for kc in range(2):
                tr(ptb[b][:, kc * 128:(kc + 1) * 128], pin[b][:, kc * 128:(kc + 1) * 128], ident_bf[:], [f"pin{b}", "ident_bf"], [f"pt{b}"])
            act(pT[:, :, t * 128:(t + 1) * 128], ptb[b][:, 0:256].rearrange("p (a b) -> p a b", b=128), AF.Copy, [f"pt{b}"], [f"pT{t}"])
        gi = 0
        for t in range(NTILE):
            norm_flush(keep=1)
            for half in range(2):
                hs = slice(half * 512, (half + 1) * 512)
                psg, kg_ = nextpf()
                for kc in range(8):
                    mm(psg[:], hT[:, kc, t * 128:(t + 1) * 128], wb[half][:, kc, :], kc == 0, kc == 7, [f"hT{t}", f"wb{half}"], [kg_])
                psp, kp_ = nextpf()
                for kc in range(2):
                    mm(psp[:], pT[:, kc, t * 128:(t + 1) * 128], wple[:, kc, hs], kc == 0, kc == 1, [f"pT{t}", "wple"], [kp_])
                act(gt[gi], psg[:], AF.Sigmoid, [kg_], [f"rl{gi}"])
                tt(gt[gi], gt[gi], psp[:], ALU.mult, [f"rl{gi}", kp_], [f"rl{gi}"])
                tt(xres[:, t, hs], xres[:, t, hs], gt[gi], ALU.add, [f"x{t}", f"rl{gi}"], [f"x{t}"])
                gi = 1 - gi
            if after_tile is not None:
                after_tile(t)

    P.barrier()
    norm_hT(0)
    for l in range(depth):
        last = (l == depth - 1)
        P.barrier()
        if phases >= 2:
            rwkv(l)
        P.barrier()
        if phases >= 3:
            obT = attention(l)
        P.barrier()
        if phases >= 4:
            merge(l, obT, after_tile=(lambda t, l=l: norm_tile_def(t, 32 + l * 8)) if phases >= 5 else None)
            norm_flush()
        dump("x1", xres[:], [128, NTILE, D], XK)
        P.barrier()
        if phases >= 5:
            ffn(l, after_tile=(lambda t, l=l: norm_tile_def(t, 64 + l * 8)) if phases >= 6 else None)
            norm_flush()
        dump("x2", xres[:], [128, NTILE, D], XK)
        if phases >= 6:
            if last:
                AR.reset(0)
                fin["gfin"] = AR.get([128, D], F32)
                fin["ob"] = AR.get([128, D], F32)
                P.dma("sp", "d_gfin", fin["gfin"], SM["final_norm_g"].partition_broadcast(128).rearrange("p o n -> p (o n)"), writes=["gfin"])
                ple(l, after_tile=final_tile)
            else:
                ple(l, after_tile=lambda t, l=l: norm_tile_def(t, (l + 1) * 8))
                norm_flush()
        dump("x3", xres[:], [128, NTILE, D], XK)
    P.wait_all("sp")
    P.emit()
    P.close()
    return nc, dbg_d


_CACHE = {}


def _alibi_tables():
    pos = np.arange(S)
    alq = np.zeros((16, S), np.float32)
    alk = np.zeros((16, S), np.float32)
    for h in range(4):
        slope = 2.0 ** (-8.0 * (h + 1) / 4)
        alq[4 * h + 0] = -slope * 128.0 * (pos // 128)
        alq[4 * h + 1] = -slope * (pos % 128)
        alq[4 * h + 2] = 1.0
        alq[4 * h + 3] = 1.0
        alk[4 * h + 0] = 1.0
        alk[4 * h + 1] = 1.0
        alk[4 * h + 2] = slope * 128.0 * (pos // 128)
        alk[4 * h + 3] = slope * (pos % 128)
    return alq, alk


def make_in_maps(inputs, ncores=8):
    alq, alk = _alibi_tables()
    shared = {}
    for k in BIGW:
        shared[k] = np.ascontiguousarray(inputs[k], dtype=np.float32)
    for k, shp in SMALL_SHAPES.items():
        shared[k] = np.ascontiguousarray(np.asarray(inputs[k], dtype=np.float32).reshape(shp))
    shared["alq"] = alq
    shared["alk"] = alk
    maps = []
    for b in range(ncores):
        m = dict(shared)
        m["x"] = np.ascontiguousarray(inputs["x"][b], dtype=np.float32)
        m["p"] = np.ascontiguousarray(inputs["p"][:, b], dtype=np.float32)
        maps.append(m)
    return maps


def kernel(**inputs):
    if "nc" not in _CACHE:
        _CACHE["nc"] = build(4)[0]
    nc = _CACHE["nc"]
    maps = make_in_maps(inputs, 8)
    res = run_bass_kernel_spmd(nc, maps, core_ids=list(range(8)))
    out = np.stack([np.asarray(r["y"], dtype=np.float32) for r in res.results], axis=0)
    return out
```

```python
import math
import numpy as np
import concourse.bass as bass
import concourse.mybir as mybir
from concourse.bass_utils import run_bass_kernel_spmd
from contextlib import ExitStack

F32 = mybir.dt.float32
BF16 = mybir.dt.bfloat16
AF = mybir.ActivationFunctionType
ALU = mybir.AluOpType
ENGS = ("pe", "act", "dve", "pool", "sp")


class Prog:
    def __init__(self, nc):
        self.nc = nc
        self.streams = {k: [] for k in ENGS}
        self.count = {}
        self.waited = {k: {} for k in ENGS}
        self.last_writer = {}
        self.readers = {}
        self.semkeys = list(ENGS)
        self.stack = ExitStack()

    def sbuf(self, name, shape, dtype):
        return self.stack.enter_context(self.nc.sbuf_tensor(name, list(shape), dtype))

    def psum(self, name, shape, dtype):
        return self.stack.enter_context(self.nc.psum_tensor(name, list(shape), dtype))

    def _deps(self, eng, reads, writes):
        deps = {}

        def add(k, v):
            if deps.get(k, 0) < v:
                deps[k] = v
        for r in reads:
            t = self.last_writer.get(r)
            if t is not None:
                add(*t)
        for w in writes:
            t = self.last_writer.get(w)
            if t is not None:
                add(*t)
            for k, v in self.readers.get(w, {}).items():
                add(k, v)
        out = []
        for k, v in deps.items():
            if eng == "pe" and k == "pe":
                continue
            if self.waited[eng].get(k, 0) >= v:
                continue
            self.waited[eng][k] = v
            out.append((k, v))
        return out

    def _record(self, tok, reads, writes):
        for w in writes:
            self.last_writer[w] = tok
            self.readers[w] = {}
        for r in reads:
            d = self.readers.setdefault(r, {})
            if d.get(tok[0], 0) < tok[1]:
                d[tok[0]] = tok[1]

    def op(self, eng, fn, reads=(), writes=()):
        waits = self._deps(eng, reads, writes)
        c = self.count.get(eng, 0) + 1
        self.count[eng] = c
        self._record((eng, c), reads, writes)
        self.streams[eng].append((waits, fn, eng, 1))

    def dma(self, eng, semkey, out, in_, reads=(), writes=()):
        if semkey not in self.semkeys:
            self.semkeys.append(semkey)
        waits = self._deps(eng, reads, writes)
        c = self.count.get(semkey, 0) + 16
        self.count[semkey] = c
        self._record((semkey, c), reads, writes)
        fn = lambda e, out=out, in_=in_: e.dma_start(out=out, in_=in_)
        self.streams[eng].append((waits, fn, semkey, 16))

    def wait_all(self, eng):
        waits = []
        for k, v in self.count.items():
            if self.waited[eng].get(k, 0) >= v:
                continue
            self.waited[eng][k] = v
            waits.append((k, v))
        self.streams[eng].append((waits, None, None, 0))

    def barrier(self):
        for e in ENGS:
            self.wait_all(e)

    def emit(self):
        nc = self.nc
        sems = {k: self.stack.enter_context(nc.semaphore("s_" + k)) for k in self.semkeys}
        block = self.stack.enter_context(nc.Block())

        def run(engname):
            def body(e):
                for waits, fn, semkey, inc in self.streams[engname]:
                    for k, v in waits:
                        e.wait_ge(sems[k], v)
                    if fn is not None:
                        fn(e).then_inc(sems[semkey], inc)
            return body
        block.tensor(run("pe"))
        block.scalar(run("act"))
        block.vector(run("dve"))
        block.gpsimd(run("pool"))
        block.sync(run("sp"))

    def close(self):
        self.stack.close()


D = 1024
S = 2048
NTILE = 16
C0 = math.exp(-0.5)
GN_EPS = 64e-5
ARENA_BYTES = 85 * 1024
SERIAL = False
SMALL = ["rwkv_mu", "rwkv_w0", "rwkv_a0", "rwkv_k_k", "rwkv_k_a", "rwkv_r_k", "rwkv_lnx_w", "rwkv_lnx_b",
         "lam_q1", "lam_k1", "lam_q2", "lam_k2", "diff_subln_g", "norm_mix_g", "norm_mlp_g", "norm_ple_g",
         "final_norm_g"]
BIGW = {"w_in": [4, 1024, 5376], "rwkv_w2": [4, 64, 512], "rwkv_a2": [4, 64, 512], "rwkv_g2": [4, 128, 512],
        "w_proj_a": [4, 512, 1024], "w_proj_b": [4, 512, 1024], "w_out": [4, 1024, 1024],
        "w_ff1": [4, 1024, 4096], "w_ff2": [4, 4096, 1024], "w_ple": [4, 256, 1024],
        "w_ple_gate": [4, 1024, 1024]}
SMALL_SHAPES = {"rwkv_mu": [4, 1792], "rwkv_w0": [4, 512], "rwkv_a0": [4, 512], "rwkv_k_k": [4, 512],
                "rwkv_k_a": [4, 512], "rwkv_r_k": [4, 512], "rwkv_lnx_w": [4, 512], "rwkv_lnx_b": [4, 512],
                "lam_q1": [1, 256], "lam_k1": [1, 256], "lam_q2": [1, 256], "lam_k2": [1, 256],
                "diff_subln_g": [4, 128], "norm_mix_g": [4, 1024], "norm_mlp_g": [4, 1024],
                "norm_ple_g": [4, 1024], "final_norm_g": [1, 1024]}


def build(depth=4, dbg=(), phases=6):
    nc = bass.Bass("TRN2", target_bir_lowering=False)
    P = Prog(nc)
    din = lambda n, s: nc.dram_tensor(n, list(s), F32, kind="ExternalInput").ap()
    x_d = din("x", [S, D])
    p_d = din("p", [4, S, 256])
    W = {k: din(k, s) for k, s in BIGW.items()}
    SM = {k: din(k, s) for k, s in SMALL_SHAPES.items()}
    alq_d = din("alq", [16, S])
    alk_d = din("alk", [16, S])
    y_d = nc.dram_tensor("y", [S, D], F32, kind="ExternalOutput").ap()
    dbg_d = {}

    xres = P.sbuf("xres", [128, NTILE, D], F32)
    hT = P.sbuf("hT", [128, 8, S], BF16)
    oaT = P.sbuf("oaT", [128, 4, S], BF16)
    ident_bf = P.sbuf("ident_bf", [128, 128], BF16)
    bo1_f = P.sbuf("bo1_f", [128, 128], F32)
    ones_f = P.sbuf("ones_f", [128, 128], F32)
    ones_bf = P.sbuf("ones_bf", [128, 128], BF16)
    maskN1 = P.sbuf("maskN1", [128, 128], BF16)
    maskNT1 = P.sbuf("maskNT1", [128, 128], BF16)
    maskIU1 = P.sbuf("maskIU1", [128, 64], BF16)
    trimask = P.sbuf("trimask", [128, 128], BF16)
    m64 = P.sbuf("m64", [128, 512], BF16)
    colA = P.sbuf("colA", [128, 120], F32)
    colB = P.sbuf("colB", [128, 52], F32)
    colC = P.sbuf("colC", [128, 96], F32)
    omka = P.sbuf("omka", [128, 16], F32)
    lamt = P.sbuf("lamt", [128, 16], F32)
    neglam = P.sbuf("neglam", [128, 4], F32)
    sgc = P.sbuf("sgc", [128, 4], F32)
    ssq = P.sbuf("ssq", [128, 16], F32)
    rstd = P.sbuf("rstd", [128, 16], F32)
    Zs = [[P.sbuf(f"Z{g}_{i}", [128, 128], F32) for i in range(2)] for g in range(4)]
    gprev = P.sbuf("gprev", [128, 4], F32)
    lastcol = P.sbuf("lastcol", [128, 14], F32)
    gpc = P.sbuf("gpc", [128, 8], F32)
    arena = P.sbuf("arena", [128, ARENA_BYTES // 4], F32)

    b4 = lambda ap: ap.rearrange("p (a b) -> p a b", b=128)
    maskN4 = maskN1[:, :].unsqueeze(1).broadcast_to([128, 4, 128])
    maskNT4 = maskNT1[:, :].unsqueeze(1).broadcast_to([128, 4, 128])
    maskIU4 = maskIU1[:, :].unsqueeze(1).broadcast_to([128, 8, 64])
    ident4 = ident_bf[:, :].unsqueeze(1).broadcast_to([128, 4, 128])
    pf = [P.psum(f"pf{i}", [128, 512], F32) for i in range(6)]
    ptb = [P.psum(f"pt{i}", [128, 1024], BF16) for i in range(2)]

    class Arena:
        def __init__(self):
            self.off = 0

        def reset(self, off=0):
            self.off = off

        def get(self, shape, dtype):
            n = int(np.prod(shape[1:]))
            nbytes = n * (4 if dtype == F32 else 2)
            nbytes = (nbytes + 31) // 32 * 32
            o = self.off
            self.off += nbytes
            assert self.off <= ARENA_BYTES, (self.off, ARENA_BYTES)
            v = arena[:, o // 4:(o + nbytes) // 4]
            if dtype != F32:
                v = v.bitcast(dtype)
            v = v[:, 0:n]
            if len(shape) == 3:
                v = v.rearrange("p (a b) -> p a b", b=shape[2])
            return v
    AR = Arena()
    cmk = AR.get([128, 512], F32)
    lamb = AR.get([128, 4, 256], F32)
    rowsA = AR.get([128, 128], F32)
    ident_f = AR.get([128, 128], F32)
    rot = {"pf": 0}

    def nextpf():
        i = rot["pf"]
        rot["pf"] = (i + 1) % 5
        return pf[i], f"pf{i}"

    def mm(out, lhsT, rhs, start, stop, reads, writes):
        P.op("pe", lambda e: e.matmul(out=out, lhsT=lhsT, rhs=rhs, start=start, stop=stop), reads, writes)

    def tr(out, in_, ident, reads, writes):
        P.op("pe", lambda e: e.transpose(out=out, in_=in_, identity=ident), reads, writes)

    def act(out, in_, func, reads, writes, **kw):
        P.op("act", lambda e: e.activation(out=out, in_=in_, func=func, **kw), reads, writes)

    def tt(out, in0, in1, op, reads, writes, eng="dve"):
        P.op(eng, lambda e: e.tensor_tensor(out=out, in0=in0, in1=in1, op=op), reads, writes)

    def ts(out, in0, s1, s2, op0, op1, reads, writes, eng="dve"):
        if op1 is None:
            P.op(eng, lambda e: e.tensor_scalar(out=out, in0=in0, scalar1=s1, scalar2=None, op0=op0), reads, writes)
        else:
            P.op(eng, lambda e: e.tensor_scalar(out=out, in0=in0, scalar1=s1, scalar2=s2, op0=op0, op1=op1), reads, writes)

    def stt(out, in0, scalar, in1, op0, op1, reads, writes):
        P.op("dve", lambda e: e.scalar_tensor_tensor(out=out, in0=in0, scalar=scalar, in1=in1, op0=op0, op1=op1), reads, writes)

    def recip(out, in_, reads, writes):
        P.op("dve", lambda e: e.reciprocal(out=out, in_=in_), reads, writes)

    def memset(ap, val, writes, eng="pool"):
        P.op(eng, lambda e: e.memset(ap, val), (), writes)

    def asel(out, in_, pattern, cmp, fill, base, cm, reads, writes):
        P.op("pool", lambda e: e.affine_select(out=out, in_=in_, pattern=pattern, compare_op=cmp, fill=fill,
                                               base=base, channel_multiplier=cm), reads, writes)

    def dump(name, ap, shape, reads):
        if name in dbg:
            d = nc.dram_tensor("dbg_" + name, list(shape), F32, kind="ExternalOutput").ap()
            dbg_d[name] = d
            P.dma("pool", "dbg_" + name, d, ap, reads=reads)

    memset(ident_f[:], 0.0, ["ident_f"])
    asel(ident_f[:], ident_f[:], [[-1, 128]], ALU.not_equal, 1.0, 0, 1, ["ident_f"], ["ident_f"])
    tt(ident_bf[:], ident_f[:], ident_f[:], ALU.mult, ["ident_f"], ["ident_bf"])
    memset(ones_f[:], 1.0, ["ones_f"])
    memset(ones_bf[:], 1.0, ["ones_bf"])
    memset(bo1_f[:], 0.0, ["bo1_f"])
    memset(bo1_f[0:64, 0:64], 1.0, ["bo1_f"])
    memset(bo1_f[64:128, 64:128], 1.0, ["bo1_f"])
    memset(cmk[:, 0:128], 0.0, ["cmk"])
    for hh in range(2):
        sl = slice(64 * hh, 64 * hh + 64)
        memset(cmk[sl, 64 * hh:64 * hh + 64], 1.0, ["cmk"])
        asel(cmk[sl, 64 * hh:64 * hh + 64], cmk[sl, 64 * hh:64 * hh + 64], [[1, 64]], ALU.is_gt, 0.0, 0, -1, ["cmk"], ["cmk"])
    tt(maskN1[:], cmk[:, 0:128], cmk[:, 0:128], ALU.mult, ["cmk"], ["maskN4"])
    memset(cmk[:, 128:256], 0.0, ["cmk2"])
    for hh in range(2):
        sl = slice(64 * hh, 64 * hh + 64)
        memset(cmk[sl, 128 + 64 * hh:128 + 64 * hh + 64], 1.0, ["cmk2"])
        asel(cmk[sl, 128 + 64 * hh:128 + 64 * hh + 64], cmk[sl, 128 + 64 * hh:128 + 64 * hh + 64], [[-1, 64]], ALU.is_gt, 0.0, 0, 1, ["cmk2"], ["cmk2"])
    tt(maskNT1[:], cmk[:, 128:256], cmk[:, 128:256], ALU.mult, ["cmk2"], ["maskNT4"])
    memset(cmk[:, 256:320], 1.0, ["cmk3"])
    for hh in range(2):
        sl = slice(64 * hh, 64 * hh + 64)
        asel(cmk[sl, 256:320], cmk[sl, 256:320], [[1, 64]], ALU.is_ge, 0.0, 0, -1, ["cmk3"], ["cmk3"])
    tt(maskIU1[:], cmk[:, 256:320], cmk[:, 256:320], ALU.mult, ["cmk3"], ["maskIU4"])
    memset(cmk[:, 384:512], 0.0, ["cmk4"])
    asel(cmk[:, 384:512], cmk[:, 384:512], [[1, 128]], ALU.is_ge, -30000.0, 0, -1, ["cmk4"], ["cmk4"])
    tt(trimask[:], cmk[:, 384:512], ones_f[:], ALU.mult, ["cmk4", "ones_f"], ["trimask"])
    memset(m64[:], 1.0, ["m64"])
    memset(m64[:].rearrange("p (c t) -> p c t", t=64)[:, :, 0:1], 0.0, ["m64"])
    for g in range(4):
        for i in range(2):
            memset(Zs[g][i][:], 0.0, [f"Z{g}_{i}"])

    def load_rows(specs, dst, ncols, dkey):
        r = 0
        for name, nrow in specs:
            src = SM[name].rearrange("l (c p) -> (l c) p", p=128)
            P.dma("sp", "d_rows", rowsA[r:r + nrow, :], src, writes=["rowsA"])
            r += nrow
        assert r == ncols
        ps, pk = nextpf()
        tr(ps[:, 0:ncols], rowsA[0:ncols, :], ident_f[0:ncols, 0:ncols], ["rowsA", "ident_f"], [pk])
        act(dst[:, 0:ncols], ps[:, 0:ncols], AF.Copy, [pk], [dkey])
    load_rows([("rwkv_mu", 56), ("rwkv_w0", 16), ("rwkv_a0", 16), ("rwkv_k_k", 16), ("rwkv_k_a", 16)], colA, 120, "colA")
    load_rows([("rwkv_r_k", 16), ("rwkv_lnx_w", 16), ("rwkv_lnx_b", 16), ("diff_subln_g", 4)], colB, 52, "colB")
    load_rows([("norm_mix_g", 32), ("norm_mlp_g", 32), ("norm_ple_g", 32)], colC, 96, "colC")
    CA, CB, CC = "colA", "colB", "colC"
    ts(omka[:], colA[:, 104:120], -1.0, 1.0, ALU.mult, ALU.add, [CA], ["omka"])
    for i, nm in enumerate(["lam_q1", "lam_k1", "lam_q2", "lam_k2"]):
        P.dma("sp", "d_lam", lamb[:, i, :], SM[nm].partition_broadcast(128).rearrange("p o n -> p (o n)"),
              writes=["lamb"])
    tt(lamb[:, 0, :], lamb[:, 0, :], lamb[:, 1, :], ALU.mult, ["lamb"], ["lamb"])
    tt(lamb[:, 2, :], lamb[:, 2, :], lamb[:, 3, :], ALU.mult, ["lamb"], ["lamb"])
    P.op("dve", lambda e: e.tensor_reduce(out=lamt[:, 0:4], in_=lamb[:, 0, :].rearrange("p (l d) -> p l d", d=64),
                                          axis=mybir.AxisListType.X, op=ALU.add), ["lamb"], ["lamt"])
    P.op("dve", lambda e: e.tensor_reduce(out=lamt[:, 4:8], in_=lamb[:, 2, :].rearrange("p (l d) -> p l d", d=64),
                                          axis=mybir.AxisListType.X, op=ALU.add), ["lamb"], ["lamt"])
    act(lamt[:, 8:16], lamt[:, 0:8], AF.Exp, ["lamt"], ["lamt"])
    tt(lamt[:, 0:4], lamt[:, 12:16], lamt[:, 8:12], ALU.subtract, ["lamt"], ["lamt"])
    for l in range(4):
        li = 0.8 - 0.6 * math.exp(-0.3 * l)
        ts(neglam[:, l:l + 1], lamt[:, l:l + 1], -li, None, ALU.add, None, ["lamt"], ["neglam"])
        ts(sgc[:, l:l + 1], colB[:, 48 + l:49 + l], 1.0 - li, None, ALU.mult, None, [CB], ["sgc"])

    xv = x_d.rearrange("(t p) d -> p t d", p=128)
    for i in range(4):
        P.dma("sp", f"d_x{i}", xres[:, 4 * i:4 * i + 4, :], xv[:, 4 * i:4 * i + 4, :], writes=[f"x{t}" for t in range(4 * i, 4 * i + 4)])

    wdma = {"i": 0}

    def wload(dst, src, key):
        P.dma("pool", "dw_" + key, dst, src, writes=[key])

    XK = [f"x{t}" for t in range(NTILE)]
    HK = [f"hT{t}" for t in range(NTILE)]

    XN_OFF = ARENA_BYTES - 4096
    AR.reset(XN_OFF)
    xn_fix = [AR.get([128, D], BF16) for _ in range(2)]
    AR.reset(0)
    yv = y_d.rearrange("(t p) d -> p t d", p=128)

    def tile_rstd(t):
        b = t % 2
        act(xn_fix[b], xres[:, t, :], AF.Square, [f"x{t}"], [f"xn{b}", f"ssq{t}"], accum_out=ssq[:, t:t + 1])
        act(rstd[:, t:t + 1], ssq[:, t:t + 1], AF.Sqrt, [f"ssq{t}"], [f"rstd{t}"], scale=1.0 / D, bias=1e-6)
        recip(rstd[:, t:t + 1], rstd[:, t:t + 1], [f"rstd{t}"], [f"rstd{t}"])

    def norm_front(t, gbase):
        b = t % 2
        tile_rstd(t)
        ts(xn_fix[b], xres[:, t, :], rstd[:, t:t + 1], None, ALU.mult, None, [f"x{t}", f"rstd{t}"], [f"xn{b}"])

    def norm_back(t, gbase):
        b = t % 2
        for kc in range(8):
            tr(ptb[b][:, kc * 128:(kc + 1) * 128], xn_fix[b][:, kc * 128:(kc + 1) * 128], ident_bf[:], [f"xn{b}", "ident_bf"], [f"pt{b}"])
        tt(hT[:, :, t * 128:(t + 1) * 128], ptb[b][:, :].rearrange("p (a b) -> p a b", b=128),
           colC[:, gbase:gbase + 8].unsqueeze(2).broadcast_to([128, 8, 128]), ALU.mult, [f"pt{b}", CC], [f"hT{t}"])

    def norm_tile(t, gbase):
        norm_front(t, gbase)
        norm_back(t, gbase)

    norm_def = []

    def norm_flush(keep=0):
        while len(norm_def) > keep:
            norm_def.pop(0)[1]()

    def norm_tile_def(t, gbase):
        while any(tp % 2 == t % 2 for tp, _ in norm_def):
            norm_def.pop(0)[1]()
        norm_front(t, gbase)
        norm_def.append((t, lambda t=t, gbase=gbase: norm_back(t, gbase)))

    def norm_hT(gbase):
        for t in range(NTILE):
            norm_tile(t, gbase)

    fin = {}

    def final_tile(t):
        tile_rstd(t)
        stt(fin["ob"], xres[:, t, :], rstd[:, t:t + 1], fin["gfin"], ALU.mult, ALU.mult, [f"x{t}", f"rstd{t}", "gfin"], ["obf"])
        P.dma("sp", "d_y", yv[:, t, :], fin["ob"], reads=["obf"])

    def hk(tok0, n):
        return [f"hT{t}" for t in range(tok0 // 128, (tok0 + n) // 128)]

    def rwkv(l):
        AR.reset(0)
        wsmU = [[AR.get([128, 8, 128], BF16) for _ in range(3)] for _ in range(2)]
        wsmL = [AR.get([128, 8, 128], BF16) for _ in range(2)]
        wl2 = AR.get([128, 512], BF16)
        wg2 = AR.get([128, 512], BF16)
        raw = AR.get([128, 520], F32)
        T1 = AR.get([128, 512], F32)
        T2 = AR.get([128, 512], F32)
        xr = AR.get([128, 512], F32)
        xk = AR.get([128, 512], F32)
        xvv = AR.get([128, 512], F32)
        sg = AR.get([128, 512], F32)
        cs = AR.get([128, 512], F32)
        eg = AR.get([128, 512], F32)
        einv = AR.get([128, 512], F32)
        al = AR.get([128, 512], F32)
        lwh = AR.get([128, 512], BF16)
        lgh = AR.get([128, 512], BF16)
        pkgs = []
        for pi in range(2):
            pg = {k: AR.get([128, 8, 128], BF16) for k in "abkv"}
            pg["rt"] = AR.get([128, 512], BF16)
            pg["bon"] = AR.get([128, 512], BF16)
            pg["gg"] = AR.get([128, 512], BF16)
            pg["gpc"] = AR.get([128, 8], F32)
            pkgs.append(pg)
        yT = AR.get([128, 512], F32)
        XS = [[AR.get([128, 512], BF16) for _ in range(2)]] * 2
        XTS = [[AR.get([128, 512], BF16) for _ in range(2)]] * 2
        QS = [[AR.get([128, 512], BF16) for _ in range(2)]] * 2
        QFIN = [None, None]
        AKT = AR.get([128, 512], BF16)
        ARS = AR.get([128, 512], BF16)
        TMab = AR.get([128, 1024], BF16)
        TMkv = AR.get([128, 1024], BF16)
        AH = AR.get([128, 512], BF16)
        AV = AR.get([128, 512], BF16)
        PV = AR.get([128, 512], BF16)
        RH = AR.get([128, 256], BF16)
        MT = AR.get([128, 512], F32)
        G1 = MT
        S0b = AR.get([128, 128], BF16)

        for pi in range(2):
            for k in "abkv":
                memset(pkgs[pi][k], 0.0, [f"pad{k}{pi}"])
        memset(lastcol[:], 0.0, ["lastcol"])
        memset(gprev[:], 0.0, ["gprev"])
        wload(wl2[0:64, :], W["rwkv_w2"][l], "wl2")
        wload(wl2[64:128, :], W["rwkv_a2"][l], "wl2")
        wload(wg2[:, :], W["rwkv_g2"][l], "wg2")
        win = W["w_in"][l].rearrange("(kc p) n -> p kc n", p=128)
        wload(wsmL[0], win[:, :, 1536:1664], "wsmL0")
        wload(wsmL[1], win[:, :, 1664:1792], "wsmL1")
        zi = [0, 0, 0, 0]

        def load_unit_weights(u):
            qt, g = divmod(u, 4)
            st = u % 2
            for ti in range(3):
                cc = ti * 4 + g
                wload(wsmU[st][ti], win[:, :, cc * 128:(cc + 1) * 128], f"wsmU{st}{ti}")

        def shift_proj(wbuf, wkey, cc, tok0, dst, dkey):
            ps, pk = nextpf()
            for kc in range(8):
                mm(ps[:], wbuf[:, kc, :], hT[:, kc, tok0:tok0 + 512], kc == 0, kc == 7, [wkey] + hk(tok0, 512), [pk])
            act(raw[:, 0:1], lastcol[:, cc:cc + 1], AF.Copy, ["lastcol"], ["raw"])
            act(raw[:, 1:513], ps[:], AF.Copy, [pk], ["raw"])
            yield
            act(lastcol[:, cc:cc + 1], raw[:, 512:513], AF.Copy, ["raw"], ["lastcol"])
            tt(T1, raw[:, 0:512], raw[:, 1:513], ALU.subtract, ["raw"], ["T1"])
            yield
            stt(dst, T1, colA[:, l * 14 + cc:l * 14 + cc + 1], raw[:, 1:513], ALU.mult, ALU.add, ["T1", "raw", CA], [dkey])
            yield

        v3 = lambda ap, sl: ap[sl, :].rearrange("p (c t) -> p c t", t=64)

        def prep1(u):
            qt, g = divmod(u, 4)
            tok0 = qt * 512
            gs = slice(g * 128, (g + 1) * 128)
            cg = l * 4 + g
            st = u % 2
            if u + 1 < 16:
                load_unit_weights(u + 1)
            if g == 0:
                yield from shift_proj(wsmL[0], "wsmL0", 12, tok0, T2, "T2")
                act(lwh[0:64, :], T2[0:64, :], AF.Tanh, ["T2"], ["lwh"])
                act(lwh[64:128, :], T2[64:128, :], AF.Copy, ["T2"], ["lwh"])
                yield
                yield from shift_proj(wsmL[1], "wsmL1", 13, tok0, T2, "T2")
                act(lgh, T2, AF.Sigmoid, ["T2"], ["lgh"])
                yield
            yield from shift_proj(wsmU[st][0], f"wsmU{st}0", g, tok0, xr, "xr")
            yield from shift_proj(wsmU[st][1], f"wsmU{st}1", 4 + g, tok0, xk, "xk")
            yield from shift_proj(wsmU[st][2], f"wsmU{st}2", 8 + g, tok0, xvv, "xv")
            ps, pk = nextpf()
            mm(ps[:], wl2[0:64, gs], lwh[0:64, :], True, True, ["wl2", "lwh"], [pk])
            act(sg, ps[:], AF.Sigmoid, [pk, CA], ["sg"], bias=colA[:, 56 + cg:57 + cg])
            yield
            ps, pk = nextpf()
            mm(ps[:], wl2[64:128, gs], lwh[64:128, :], True, True, ["wl2", "lwh"], [pk])
            act(al, ps[:], AF.Sigmoid, [pk, CA], ["al"], bias=colA[:, 72 + cg:73 + cg])
            yield
            act(T1, xk, AF.Square, ["xk", CA], ["T1"], scale=colA[:, 88 + cg:89 + cg])
            yield
            P.op("dve", lambda e: e.tensor_tensor_scan(out=cs, data0=m64[:], data1=sg, initial=0.0, op0=ALU.mult, op1=ALU.add),
                 ["m64", "sg"], ["cs"])
            ps, pk = nextpf()
            mm(ps[:], bo1_f[:], T1, True, True, ["bo1_f", "T1"], [pk])
            ts(T1, ps[:], 1e-24, None, ALU.max, None, [pk], ["T1"])
            yield
            act(eg, cs, AF.Exp, ["cs"], ["eg"], scale=-C0)
            yield
            act(einv, cs, AF.Exp, ["cs"], ["einv"], scale=C0)
            ts(T2, al, colA[:, 104 + cg:105 + cg], omka[:, cg:cg + 1], ALU.mult, ALU.add, ["al", CA, "omka"], ["T2"])
            yield
            act(T1, T1, AF.Ln, ["T1"], ["T1"])
            tt(cs, cs, sg, ALU.subtract, ["cs", "sg"], ["cs"])
            yield
            act(T1, T1, AF.Exp, ["T1"], ["T1"], scale=-0.5)
            yield
            act(sg, cs, AF.Exp, ["cs"], ["sg"], scale=-C0)
            stt(cs, xk, colA[:, 88 + cg:89 + cg], T1, ALU.mult, ALU.mult, ["xk", "T1", CA], ["cs"])
            yield
            tt(xk, xk, T2, ALU.mult, ["xk", "T2"], ["xk"])
            yield
            stt(T2, xr, colB[:, cg:cg + 1], xk, ALU.mult, ALU.mult, ["xr", "xk", CB], ["T2"])
            yield
            tt(T1, al, einv, ALU.mult, ["al", "einv"], ["T1"])
            yield

        def prep2(u, pi):
            qt, g = divmod(u, 4)
            pg = pkgs[pi]
            K = lambda n: f"{n}{pi}"
            gs = slice(g * 128, (g + 1) * 128)
            kkn, egm = cs, sg
            ps, pk = nextpf()
            mm(ps[:], wg2[:, gs], lgh, True, True, ["wg2", "lgh"], [pk])
            act(pg["gg"], ps[:], AF.Copy, [pk], [K("gg")])
            yield
            ps, pk = nextpf()
            mm(ps[:], bo1_f[:], T2, True, True, ["bo1_f", "T2"], [pk])
            tt(pg["bon"], ps[:], xvv, ALU.mult, [pk, "xv"], [K("bon")])
            yield
            for hh in range(2):
                sl = slice(64 * hh, 64 * hh + 64)
                stt(pg["a"][sl, :, sl], v3(kkn, sl), -1.0, v3(egm, sl), ALU.mult, ALU.mult, ["cs", "sg"], [K("pada")])
                yield
                tt(pg["b"][sl, :, sl], v3(kkn, sl), v3(T1, sl), ALU.mult, ["cs", "T1"], [K("padb")])
                yield
                tt(pg["k"][sl, :, sl], v3(xk, sl), v3(einv, sl), ALU.mult, ["xk", "einv"], [K("padk")])
                act(pg["v"][sl, :, sl], v3(xvv, sl), AF.Copy, ["xv"], [K("padv")])
                yield
            tt(pg["rt"], xr, eg, ALU.mult, ["xr", "eg"], [K("rt")])
            act(pg["gpc"][:, 0:1], gprev[:, g:g + 1], AF.Copy, ["gprev"], [K("gpc")])
            yield
            act(pg["gpc"][:, 1:8], eg.rearrange("p (c t) -> p c t", t=64)[:, 0:7, 63], AF.Copy, ["eg"], [K("gpc")])
            act(gprev[:, g:g + 1], eg[:, 511:512], AF.Copy, ["eg"], ["gprev"])
            yield

        blk = lambda ap, i: ap[:, i * 128:(i + 1) * 128]

        def neumann(k):
            u, q = divmod(k, 2)
            pi = u % 2
            pg = pkgs[pi]
            K = lambda n: f"{n}{pi}"
            s_ = k % 2
            X_, XT_, Q_ = XS[s_], XTS[s_], QS[s_]
            xk_ = lambda n, i: f"{n}_{i}"
            pada, padb = pg["a"], pg["b"]
            chs = [4 * q + i for i in range(4)]
            ps, pk = nextpf()
            for i, c in enumerate(chs):
                mm(blk(ps, i), padb[:, c, :], pada[:, c, :], True, True, [K("pada"), K("padb")], [pk])
            tt(b4(X_[0]), b4(ps[:]), maskN4, ALU.mult, [pk, "maskN4"], [xk_("X", 0)])
            ps, pk = nextpf()
            for i, c in enumerate(chs):
                mm(blk(ps, i), pada[:, c, :], padb[:, c, :], True, True, [K("pada"), K("padb")], [pk])
            tt(b4(XT_[0]), b4(ps[:]), maskNT4, ALU.mult, [pk, "maskNT4"], [xk_("XT", 0)])
            tt(b4(Q_[0]), b4(X_[0]), ident4, ALU.add, [xk_("X", 0), "ident_bf"], [xk_("Q", 0)], eng="pool")
            yield
            cur = 0
            for lev in range(1, 6):
                nx = 1 - cur
                ps, pk = nextpf()
                for i in range(4):
                    mm(blk(ps, i), blk(X_[cur], i), blk(XT_[cur], i), True, True, [xk_("X", cur), xk_("XT", cur)], [pk])
                act(XT_[nx], ps[:], AF.Copy, [pk], [xk_("XT", nx)])
                if lev < 5:
                    ps, pk = nextpf()
                    for i in range(4):
                        mm(blk(ps, i), blk(XT_[cur], i), blk(X_[cur], i), True, True, [xk_("X", cur), xk_("XT", cur)], [pk])
                    P.op("dve", lambda e, o=X_[nx], s=ps: e.tensor_copy(out=o, in_=s[:]), [pk], [xk_("X", nx)])
                yield
                ps, pk = nextpf()
                for i in range(4):
                    mm(blk(ps, i), blk(XT_[nx], i), blk(Q_[cur], i), True, True, [xk_("XT", nx), xk_("Q", cur)], [pk])
                tt(Q_[nx], ps[:], Q_[cur], ALU.add, [pk, xk_("Q", cur)], [xk_("Q", nx)])
                cur = nx
                yield
            QFIN[s_] = (Q_[cur], xk_("Q", cur))

        def tail(k):
            u, q = divmod(k, 2)
            pi = u % 2
            qt, g = divmod(u, 4)
            tok0 = qt * 512
            pg = pkgs[pi]
            K = lambda n: f"{n}{pi}"
            cg = l * 4 + g
            pada, padb, padk, padv, rt, gpc_ = pg["a"], pg["b"], pg["k"], pg["v"], pg["rt"], pg["gpc"]
            chs = [4 * q + i for i in range(4)]
            ps, pk = nextpf()
            for i, c in enumerate(chs):
                mm(blk(ps, i), padk[:, c, :], pada[:, c, :], True, True, [K("pada"), K("padk")], [pk])
            tt(b4(AKT), b4(ps[:]), maskN4, ALU.mult, [pk, "maskN4"], ["AKT"])
            ps, pk = nextpf()
            for i, c in enumerate(chs):
                mm(ps[:, i * 128:i * 128 + 64], padb[:, c, :], rt[:, c * 64:(c + 1) * 64], True, True, [K("padb"), K("rt")], [pk])
                mm(ps[:, i * 128 + 64:(i + 1) * 128], padk[:, c, :], rt[:, c * 64:(c + 1) * 64], True, True, [K("padk"), K("rt")], [pk])
            tt(ARS.rearrange("p (a b) -> p a b", b=64), ps[:].rearrange("p (a b) -> p a b", b=64), maskIU4, ALU.mult, [pk, "maskIU4"], ["ARS"])
            yield
            for i, c in enumerate(chs):
                tr(ptb[0][:, i * 128:(i + 1) * 128], pada[:, c, :], ident_bf[:], [K("pada"), "ident_bf"], ["pt0"])
                tr(ptb[0][:, (4 + i) * 128:(5 + i) * 128], padb[:, c, :], ident_bf[:], [K("padb"), "ident_bf"], ["pt0"])
            act(TMab, ptb[0][:, :], AF.Copy, ["pt0"], ["TMab"])
            for i, c in enumerate(chs):
                tr(ptb[1][:, i * 128:(i + 1) * 128], padk[:, c, :], ident_bf[:], [K("padk"), "ident_bf"], ["pt1"])
                tr(ptb[1][:, (4 + i) * 128:(5 + i) * 128], padv[:, c, :], ident_bf[:], [K("padv"), "ident_bf"], ["pt1"])
            P.op("dve", lambda e: e.tensor_copy(out=TMkv, in_=ptb[1][:, :]), ["pt1"], ["TMkv"])
            yield
            QF, QK = QFIN[k % 2]
            TMa = lambda i: TMab[:, i * 128:(i + 1) * 128]
            TMb = lambda i: TMab[:, (4 + i) * 128:(5 + i) * 128]
            TMk = lambda i: TMkv[:, i * 128:(i + 1) * 128]
            TMv = lambda i: TMkv[:, (4 + i) * 128:(5 + i) * 128]
            ps, pk = nextpf()
            for i in range(4):
                mm(blk(ps, i), blk(QF, i), TMa(i), True, True, [QK, "TMab"], [pk])
            act(AH, ps[:], AF.Copy, [pk], ["AH"])
            ps, pk = nextpf()
            for i in range(4):
                mm(blk(ps, i), blk(AKT, i), TMv(i), True, True, ["AKT", "TMkv"], [pk])
            P.op("dve", lambda e, s=ps: e.tensor_copy(out=AV, in_=s[:]), [pk], ["AV"])
            yield
            ps, pk = nextpf()
            for i in range(4):
                mm(blk(ps, i), blk(QF, i), blk(AV, i), True, True, [QK, "AV"], [pk])
            act(PV, ps[:], AF.Copy, [pk], ["PV"])
            ps, pk = nextpf()
            for i in range(4):
                mm(ps[:, i * 64:(i + 1) * 64], blk(AH, i), ARS[:, i * 128:i * 128 + 64], True, True, ["AH", "ARS"], [pk])
            tt(RH, ps[:, 0:256], rt[:, q * 256:(q + 1) * 256], ALU.add, [pk, K("rt")], ["RH"])
            ps, pk = nextpf()
            for i in range(4):
                mm(blk(ps, i), blk(AH, i), TMb(i), True, True, ["AH", "TMab"], [pk])
            tt(b4(MT), b4(ps[:]), gpc_[:, 4 * q:4 * q + 4].unsqueeze(2).broadcast_to([128, 4, 128]), ALU.mult, [pk, K("gpc")], ["MT"])
            yield
            psy, pyk = pf[5], "pf5"
            for i, c in enumerate(chs):
                gcol = gpc_[:, c:c + 1]
                zc, zck = Zs[g][zi[g]], f"Z{g}_{zi[g]}"
                zn, znk = Zs[g][1 - zi[g]], f"Z{g}_{1 - zi[g]}"
                act(S0b, zc[:], AF.Copy, [zck, K("gpc")], ["S0b"], scale=gcol)
                pss, psk = nextpf()
                mm(pss[:, 0:128], TMb(i), blk(PV, i), True, False, ["TMab", "PV"], [psk])
                mm(pss[:, 0:128], TMk(i), TMv(i), False, False, ["TMkv"], [psk])
                mm(pss[:, 0:128], blk(MT, i), zc[:], False, True, ["MT", zck], [psk])
                stt(zn[:], zc[:], gcol, pss[:, 0:128], ALU.mult, ALU.add, [zck, K("gpc"), psk], [znk])
                yo = psy[:, i * 64:(i + 1) * 64]
                mm(yo, S0b, RH[:, i * 64:(i + 1) * 64], True, False, ["S0b", "RH"], [pyk])
                mm(yo, blk(PV, i), ARS[:, i * 128:i * 128 + 64], False, False, ["PV", "ARS"], [pyk])
                mm(yo, TMv(i), ARS[:, i * 128 + 64:(i + 1) * 128], False, True, ["TMkv", "ARS"], [pyk])
                zi[g] = 1 - zi[g]
                yield
            act(yT[:, q * 256:(q + 1) * 256], psy[:, 0:256], AF.Copy, [pyk], ["yT"])
            if q == 0:
                return
            ps, pk = nextpf()
            mm(ps[:], bo1_f[:], yT, True, True, ["bo1_f", "yT"], [pk])
            stt(yT, ps[:], -1.0 / 64.0, yT, ALU.mult, ALU.add, ["yT", pk], ["yT"])
            act(G1, yT, AF.Square, ["yT"], ["MT"])
            ps, pk = nextpf()
            mm(ps[:], bo1_f[:], G1, True, True, ["bo1_f", "MT"], [pk])
            act(G1, ps[:], AF.Ln, [pk], ["MT"], bias=GN_EPS, scale=1.0 / 64.0)
            act(G1, G1, AF.Exp, ["MT"], ["MT"], scale=-0.5)
            yield
            tt(yT, yT, G1, ALU.mult, ["yT", "MT"], ["yT"])
            ts(yT, yT, colB[:, 16 + cg:17 + cg], colB[:, 32 + cg:33 + cg], ALU.mult, ALU.add, ["yT", CB], ["yT"])
            tt(yT, yT, pg["bon"], ALU.add, ["yT", K("bon")], ["yT"])
            tt(oaT[:, g, tok0:tok0 + 512], yT, pg["gg"], ALU.mult, ["yT", K("gg")], [f"oa{g}_{qt}"])
            yield

        def run_gens(items):
            if SERIAL:
                for gen, n in items:
                    for _ in gen:
                        pass
                return
            alive = [list(it) for it in items]
            while alive:
                for it in list(alive):
                    gen, n = it
                    for _ in range(n):
                        try:
                            next(gen)
                        except StopIteration:
                            alive.remove(it)
                            break

        load_unit_weights(0)
        run_gens([(prep1(0), 1)])
        run_gens([(prep2(0, 0), 1)])
        run_gens([(neumann(0), 1), (prep1(1), 2)])
        NQ = 32
        for k in range(NQ):
            tg = tail(k)
            for _ in range(4):
                next(tg)
            items = [(tg, 1)]
            if k + 1 < NQ:
                items.append((neumann(k + 1), 1))
            u, q = divmod(k, 2)
            if q == 0 and u + 1 < 16:
                items.append((prep2(u + 1, (u + 1) % 2), 1))
            if q == 1 and u + 2 < 16:
                items.append((prep1(u + 2), 2))
            run_gens(items)
        dump("oaT", oaT[:], [128, 4, S], [f"oa{g}_{qt}" for g in range(4) for qt in range(4)])

    def attention(l):
        AR.reset(0)
        obT = AR.get([128, 4, S], BF16)
        vtm = AR.get([128, NTILE, 512], BF16)
        qa = [[AR.get([128, S], BF16) for _ in range(2)] for _ in range(2)]
        ka = [[AR.get([128, S], BF16) for _ in range(2)] for _ in range(2)]
        stage = AR.get([128, S], BF16)
        wq = [AR.get([128, 8, 128], BF16) for _ in range(2)]
        wv_off = AR.off
        wv = AR.get([128, 8, 512], BF16)
        PT = [AR.get([128, 512], BF16) for _ in range(4)]
        STILES = [(pf[0], "pf0"), (pf[1], "pf1"), (ptb[1][:, :].bitcast(F32), "pt1")]
        win = W["w_in"][l].rearrange("(kc p) n -> p kc n", p=128)
        wload(wv, win[:, :, 2816:3328], "wv")
        for kt in range(NTILE):
            ps, pk = nextpf()
            for kc in range(8):
                mm(ps[:], hT[:, kc, kt * 128:(kt + 1) * 128], wv[:, kc, :], kc == 0, kc == 7, [f"hT{kt}", "wv"], [pk])
            act(vtm[:, kt, :], ps[:], AF.Copy, [pk], [f"vtm{kt}"])
        P.barrier()
        sv = AR.off
        AR.reset(wv_off)
        Rc = AR.get([128, 512], F32)
        A0 = AR.get([128, 512], F32)
        A1 = AR.get([128, 512], F32)
        Tq = AR.get([128, 512], F32)
        AR.reset(sv)
        wr = 0
        ptr = 0
        deferred = []
        def proj(h):
            nonlocal wr
            hb = h % 2
            for which, cb, aug, sc in (("q", 1792 + h * 128, qa[hb], 0.125), ("k", 2304 + h * 128, ka[hb], 1.0)):
                wload(wq[wr], win[:, :, cb:cb + 128], f"wq{wr}")
                for tc in range(4):
                    ps, pk = nextpf()
                    for kc in range(8):
                        mm(ps[:], wq[wr][:, kc, :], hT[:, kc, tc * 512:(tc + 1) * 512], kc == 0, kc == 7, [f"wq{wr}"] + hk(tc * 512, 512), [pk])
                    ts(aug[0][0:64, tc * 512:(tc + 1) * 512], ps[0:64, :], sc, None, ALU.mult, None, [pk], [f"{which}a{hb}0"])
                    ts(stage[64:128, tc * 512:(tc + 1) * 512], ps[64:128, :], sc, None, ALU.mult, None, [pk], ["stage"])
                P.dma("sp", f"d_rep{which}{hb}", aug[1][0:64, :], stage[64:128, :], reads=["stage"], writes=[f"{which}a{hb}1"])
                wr = 1 - wr
            for c in range(2):
                P.dma("pool", f"d_alq{hb}{c}", qa[hb][c][64:68, :], alq_d[4 * h:4 * h + 4, :], writes=[f"qa{hb}{c}"])
                P.dma("pool", f"d_alk{hb}{c}", ka[hb][c][64:68, :], alk_d[4 * h:4 * h + 4, :], writes=[f"ka{hb}{c}"])

        proj(0)
        pend = []
        sctr = [0]

        def drain(n):
            while len(pend) > n:
                ofn, st, endfn = pend.pop(0)
                ofn(st)
                if endfn is not None:
                    endfn()

        for h in range(4):
            for j in range(4):
                nkt = 4 * (j + 1)
                for c in range(2):
                    O, Ok = pf[2 + 2 * c], f"pf{2 + 2 * c}"
                    Lp, Lk = pf[3 + 2 * c], f"pf{3 + 2 * c}"

                    def s_stage(kt, j=j, c=c, h=h):
                        nonlocal ptr
                        t = kt - 4 * j
                        n0 = 128 * t if t > 0 else 0
                        Sp, Sk = STILES[sctr[0] % 3]
                        sctr[0] += 1
                        mm(Sp[:, n0:512], ka[h % 2][c][0:68, kt * 128:(kt + 1) * 128], qa[h % 2][c][0:68, j * 512 + n0:(j + 1) * 512], True, t < 0,
                           [f"ka{h % 2}{c}", f"qa{h % 2}{c}"], [Sk])
                        if t >= 0:
                            mm(Sp[:, n0:n0 + 128], ident_bf[:], trimask[:], False, True, ["ident_bf", "trimask"], [Sk])
                        pb = PT[ptr]
                        pbk = f"PT{ptr}"
                        ptr = (ptr + 1) % 4
                        act(pb[:, n0:512], Sp[:, n0:512], AF.Exp, [Sk], [pbk])
                        return kt, n0, pb, pbk

                    def o_stage(st, nkt=nkt, O=O, Ok=Ok, Lp=Lp, Lk=Lk, h=h):
                        kt, n0, pb, pbk = st
                        mm(O[:, n0:512], vtm[:, kt, h * 128:(h + 1) * 128], pb[:, n0:512], kt == 0, kt == nkt - 1, [f"vtm{kt}", pbk], [Ok])
                        mm(Lp[:, n0:512], ones_bf[:], pb[:, n0:512], kt == 0, kt == nkt - 1, ["ones_bf", pbk], [Lk])

                    def block_end(c=c, O=O, Ok=Ok, Lp=Lp, Lk=Lk, h=h, j=j):
                        act(Rc, Lp[:], AF.Ln, [Lk], ["Rc"])
                        act(Rc, Rc, AF.Exp, ["Rc"], ["Rc"], scale=-1.0)
                        tt((A0, A1)[c], O[:], Rc, ALU.mult, [Ok, "Rc"], [f"A{c}"])
                        if c == 1:
                            stt(A0, A1, neglam[:, l:l + 1], A0, ALU.mult, ALU.add, ["A0", "A1", "neglam"], ["A0"])
                            act(Tq, A0, AF.Square, ["A0"], ["Tq"])

                            def tail(h=h, j=j):
                                ps = ptb[0][:, :].bitcast(F32)
                                mm(ps, ones_f[:], Tq, True, True, ["ones_f", "Tq"], ["pt0"])
                                act(Tq, ps, AF.Ln, ["pt0"], ["Tq"], scale=1.0 / 128.0, bias=1e-5)
                                act(Tq, Tq, AF.Exp, ["Tq"], ["Tq"], scale=-0.5)
                                tt(A0, A0, Tq, ALU.mult, ["A0", "Tq"], ["A0"])
                                ts(obT[:, h, j * 512:(j + 1) * 512], A0, sgc[:, l:l + 1], None, ALU.mult, None, ["A0", "sgc"], [f"ob{h}_{j}"])
                            deferred.append(tail)

                    for kt in range(nkt):
                        pend.append((o_stage, s_stage(kt), block_end if kt == nkt - 1 else None))
                        drain(2)
                        if kt == 3:
                            while deferred:
                                deferred.pop(0)()
                if j == 0 and h + 1 < 4:
                    drain(0)
                    proj(h + 1)
        drain(0)
        while deferred:
            deferred.pop(0)()
        dump("obT", obT, [128, 4, S], [f"ob{h}_{j}" for h in range(4) for j in range(4)])
        return obT

    def merge(l, obT, after_tile=None):
        AR.reset(16384)
        mT = AR.get([128, 8, S], BF16)
        sv = AR.off
        wga = [AR.get([128, 8, 128], BF16) for _ in range(2)]
        wgb = [AR.get([128, 8, 128], BF16) for _ in range(2)]
        pa = [AR.get([128, 4, 128], BF16) for _ in range(2)]
        pb_ = [AR.get([128, 4, 128], BF16) for _ in range(2)]
        ga = [AR.get([128, 512], BF16) for _ in range(2)]
        gb = [AR.get([128, 512], BF16) for _ in range(2)]
        t1 = AR.get([128, 512], F32)
        t2 = AR.get([128, 512], F32)
        win = W["w_in"][l].rearrange("(kc p) n -> p kc n", p=128)
        wpa = W["w_proj_a"][l].rearrange("(kc p) n -> p kc n", p=128)
        wpb = W["w_proj_b"][l].rearrange("(kc p) n -> p kc n", p=128)
        OAK = [f"oa{g}_{qt}" for g in range(4) for qt in range(4)]
        OBK = [f"ob{h}_{j}" for h in range(4) for j in range(4)]
        gi = 0
        for dch in range(8):
            b = dch % 2
            wload(wga[b], win[:, :, 3328 + dch * 128:3328 + (dch + 1) * 128], f"wga{b}")
            wload(wgb[b], win[:, :, 4352 + dch * 128:4352 + (dch + 1) * 128], f"wgb{b}")
            wload(pa[b], wpa[:, :, dch * 128:(dch + 1) * 128], f"pa{b}")
            wload(pb_[b], wpb[:, :, dch * 128:(dch + 1) * 128], f"pb{b}")
            for j in range(4):
                tk = slice(j * 512, (j + 1) * 512)
                psG, kg_ = nextpf()
                for kc in range(8):
                    mm(psG[:], wga[b][:, kc, :], hT[:, kc, tk], kc == 0, kc == 7, [f"wga{b}"] + hk(j * 512, 512), [kg_])
                act(ga[gi], psG[:], AF.Sigmoid, [kg_], [f"ga{gi}"])
                psH, kh_ = nextpf()
                for kc in range(8):
                    mm(psH[:], wgb[b][:, kc, :], hT[:, kc, tk], kc == 0, kc == 7, [f"wgb{b}"] + hk(j * 512, 512), [kh_])
                act(gb[gi], psH[:], AF.Sigmoid, [kh_], [f"gb{gi}"])
                psA, ka_ = nextpf()
                for kc in range(4):
                    mm(psA[:], pa[b][:, kc, :], oaT[:, kc, tk], kc == 0, kc == 3, [f"pa{b}"] + OAK, [ka_])
                tt(t1, psA[:], ga[gi], ALU.mult, [ka_, f"ga{gi}"], ["t1"])
                psB, kb_ = nextpf()
                for kc in range(4):
                    mm(psB[:], pb_[b][:, kc, :], obT[:, kc, tk], kc == 0, kc == 3, [f"pb{b}"] + OBK, [kb_])
                tt(t2, psB[:], gb[gi], ALU.mult, [kb_, f"gb{gi}"], ["t2"])
                tt(mT[:, dch, tk], t1, t2, ALU.add, ["t1", "t2"], [f"mT{dch}_{j}"])
                gi = 1 - gi
        P.barrier()
        AR.reset(sv)
        wo = [AR.get([128, 8, 512], BF16) for _ in range(2)]
        wov = W["w_out"][l].rearrange("(kc p) n -> p kc n", p=128)
        for half in range(2):
            wload(wo[half], wov[:, :, half * 512:(half + 1) * 512], f"wo{half}")
        for t in range(NTILE):
            norm_flush(keep=1)
            for half in range(2):
                ps, pk = nextpf()
                for kc in range(8):
                    mm(ps[:], mT[:, kc, t * 128:(t + 1) * 128], wo[half][:, kc, :], kc == 0, kc == 7, [f"mT{kc}_{t // 4}", f"wo{half}"], [pk])
                hs = slice(half * 512, (half + 1) * 512)
                tt(xres[:, t, hs], xres[:, t, hs], ps[:], ALU.add, [f"x{t}", pk], [f"x{t}"])
            if after_tile is not None:
                after_tile(t)

    def ffn(l, after_tile=None):
        AR.reset(8192)
        wb = [AR.get([128, 8, 512], BF16) for _ in range(2)]
        fT = AR.get([128, 32, 512], BF16)
        rl = [AR.get([128, 512], F32) for _ in range(2)]
        wb.append(AR.get([128, 8, 512], BF16))
        w1 = W["w_ff1"][l].rearrange("(kc p) n -> p kc n", p=128)
        w2 = W["w_ff2"][l].rearrange("(fc p) n -> p fc n", p=128)
        wi = 0
        ri = 0
        for j in range(4):
            tk = slice(j * 512, (j + 1) * 512)
            for piece in range(8):
                norm_flush(keep=max(0, len(norm_def) - 1))
                wload(wb[wi], w1[:, :, piece * 512:(piece + 1) * 512], f"wb{wi}")
                for f4 in range(4):
                    fc = piece * 4 + f4
                    ps, pk = pf[4 + fc % 2], f"pf{4 + fc % 2}"
                    for kc in range(8):
                        mm(ps[:], wb[wi][:, kc, f4 * 128:(f4 + 1) * 128], hT[:, kc, tk], kc == 0, kc == 7, [f"wb{wi}"] + hk(j * 512, 512), [pk])
                    act(rl[ri], ps[:], AF.Relu, [pk], [f"rl{ri}"])
                    tt(fT[:, fc, :], ps[:], rl[ri], ALU.mult, [pk, f"rl{ri}"], [f"fT{fc}"])
                    ri = 1 - ri
                wi = (wi + 1) % 3
            for half in range(2):
                hs = slice(half * 512, (half + 1) * 512)
                for piece in range(4):
                    wload(wb[wi], w2[:, piece * 8:(piece + 1) * 8, hs], f"wb{wi}")
                    for f8 in range(8):
                        fc = piece * 8 + f8
                        for t4 in range(4):
                            mm(pf[t4][:], fT[:, fc, t4 * 128:(t4 + 1) * 128], wb[wi][:, f8, :], fc == 0, fc == 31, [f"fT{fc}", f"wb{wi}"], [f"pf{t4}"])
                    wi = (wi + 1) % 3
                for t4 in range(4):
                    t = 4 * j + t4
                    tt(xres[:, t, hs], xres[:, t, hs], pf[t4][:], ALU.add, [f"x{t}", f"pf{t4}"], [f"x{t}"])
                    if half == 1 and after_tile is not None:
                        after_tile(t)

    def ple(l, after_tile=None):
        AR.reset(8192)
        wb = [AR.get([128, 8, 512], BF16) for _ in range(2)]
        AR.reset(8192 + 16384 + 32768)
        gt = [AR.get([128, 512], F32) for _ in range(2)]
        AR.reset(8192 + 16384 + 32768 + 4096 + 8192)
        wple = AR.get([128, 2, 1024], BF16)
        pT = AR.get([128, 2, S], BF16)
        pin = [AR.get([128, 256], BF16) for _ in range(2)]
        assert AR.off <= XN_OFF, (AR.off, XN_OFF)
        wg = W["w_ple_gate"][l].rearrange("(kc p) n -> p kc n", p=128)
        for half in range(2):
            wload(wb[half], wg[:, :, half * 512:(half + 1) * 512], f"wb{half}")
        wload(wple, W["w_ple"][l].rearrange("(kc p) n -> p kc n", p=128), "wple")
        def dma_pin(t):
            b = t % 2
            P.dma("pool", f"d_pin{b}", pin[b], p_d[l, t * 128:(t + 1) * 128, :], writes=[f"pin{b}"])

        def trans_pT(t):
            b = t % 2
            for kc in range(2):
                tr(ptb[b][:, kc * 128:(kc + 1) * 128], pin[b][:, kc * 128:(kc + 1) * 128], ident_bf[:], [f"pin{b}", "ident_bf"], [f"pt{b}"])
            act(pT[:, :, t * 128:(t + 1) * 128], ptb[b][:, 0:256].rearrange("p (a b) -> p a b", b=128), AF.Copy, [f"pt{b}"], [f"pT{t}"])

        dma_pin(0)
        dma_pin(1)
        trans_pT(0)
        gi = 0
        for t in range(NTILE):
            norm_flush(keep=1)
            if t + 1 < NTILE:
                trans_pT(t + 1)
            if t + 2 < NTILE:
                dma_pin(t + 2)
            for half in range(2):
                hs = slice(half * 512, (half + 1) * 512)
                psg, kg_ = nextpf()
                for kc in range(8):
                    mm(psg[:], hT[:, kc, t * 128:(t + 1) * 128], wb[half][:, kc, :], kc == 0, kc == 7, [f"hT{t}", f"wb{half}"], [kg_])
                psp, kp_ = nextpf()
                for kc in range(2):
                    mm(psp[:], pT[:, kc, t * 128:(t + 1) * 128], wple[:, kc, hs], kc == 0, kc == 1, [f"pT{t}", "wple"], [kp_])
                act(gt[gi], psg[:], AF.Sigmoid, [kg_], [f"rl{gi}"])
                tt(gt[gi], gt[gi], psp[:], ALU.mult, [f"rl{gi}", kp_], [f"rl{gi}"])
                tt(xres[:, t, hs], xres[:, t, hs], gt[gi], ALU.add, [f"x{t}", f"rl{gi}"], [f"x{t}"])
                gi = 1 - gi
            if after_tile is not None:
                after_tile(t)

    P.barrier()
    norm_hT(0)
    for l in range(depth):
        last = (l == depth - 1)
        P.barrier()
        if phases >= 2:
            rwkv(l)
        P.barrier()
        if phases >= 3:
            obT = attention(l)
        P.barrier()
        if phases >= 4:
            merge(l, obT, after_tile=(lambda t, l=l: norm_tile_def(t, 32 + l * 8)) if phases >= 5 else None)
            norm_flush()
        dump("x1", xres[:], [128, NTILE, D], XK)
        P.barrier()
        if phases >= 5:
            ffn(l, after_tile=(lambda t, l=l: norm_tile_def(t, 64 + l * 8)) if phases >= 6 else None)
            norm_flush()
        dump("x2", xres[:], [128, NTILE, D], XK)
        if phases >= 6:
            if last:
                AR.reset(0)
                fin["gfin"] = AR.get([128, D], F32)
                fin["ob"] = AR.get([128, D], F32)
                P.dma("sp", "d_gfin", fin["gfin"], SM["final_norm_g"].partition_broadcast(128).rearrange("p o n -> p (o n)"), writes=["gfin"])
                ple(l, after_tile=final_tile)
            else:
                ple(l, after_tile=lambda t, l=l: norm_tile_def(t, (l + 1) * 8))
                norm_flush()
        dump("x3", xres[:], [128, NTILE, D], XK)
    P.wait_all("sp")
    P.emit()
    P.close()
    return nc, dbg_d


_CACHE = {}


def _alibi_tables():
    pos = np.arange(S)
    alq = np.zeros((16, S), np.float32)
    alk = np.zeros((16, S), np.float32)
    for h in range(4):
        slope = 2.0 ** (-8.0 * (h + 1) / 4)
        alq[4 * h + 0] = -slope * 128.0 * (pos // 128)
        alq[4 * h + 1] = -slope * (pos % 128)
        alq[4 * h + 2] = 1.0
        alq[4 * h + 3] = 1.0
        alk[4 * h + 0] = 1.0
        alk[4 * h + 1] = 1.0
        alk[4 * h + 2] = slope * 128.0 * (pos // 128)
        alk[4 * h + 3] = slope * (pos % 128)
    return alq, alk


def make_in_maps(inputs, ncores=8):
    alq, alk = _alibi_tables()
    shared = {}
    for k in BIGW:
        shared[k] = np.ascontiguousarray(inputs[k], dtype=np.float32)
    for k, shp in SMALL_SHAPES.items():
        shared[k] = np.ascontiguousarray(np.asarray(inputs[k], dtype=np.float32).reshape(shp))
    shared["alq"] = alq
    shared["alk"] = alk
    maps = []
    for b in range(ncores):
        m = dict(shared)
        m["x"] = np.ascontiguousarray(inputs["x"][b], dtype=np.float32)
        m["p"] = np.ascontiguousarray(inputs["p"][:, b], dtype=np.float32)
        maps.append(m)
    return maps


def kernel(**inputs):
    if "nc" not in _CACHE:
        _CACHE["nc"] = build(4)[0]
    nc = _CACHE["nc"]
    maps = make_in_maps(inputs, 8)
    res = run_bass_kernel_spmd(nc, maps, core_ids=list(range(8)))
    out = np.stack([np.asarray(r["y"], dtype=np.float32) for r in res.results], axis=0)
    return out
```

```python
import math
import numpy as np
import concourse.bass as bass
import concourse.mybir as mybir
from concourse.bass_utils import run_bass_kernel_spmd
from contextlib import ExitStack

F32 = mybir.dt.float32
BF16 = mybir.dt.bfloat16
AF = mybir.ActivationFunctionType
ALU = mybir.AluOpType
ENGS = ("pe", "act", "dve", "pool", "sp")


class Prog:
    def __init__(self, nc):
        self.nc = nc
        self.streams = {k: [] for k in ENGS}
        self.count = {}
        self.waited = {k: {} for k in ENGS}
        self.last_writer = {}
        self.readers = {}
        self.semkeys = list(ENGS)
        self.stack = ExitStack()

    def sbuf(self, name, shape, dtype):
        return self.stack.enter_context(self.nc.sbuf_tensor(name, list(shape), dtype))

    def psum(self, name, shape, dtype):
        return self.stack.enter_context(self.nc.psum_tensor(name, list(shape), dtype))

    def _deps(self, eng, reads, writes):
        deps = {}

        def add(k, v):
            if deps.get(k, 0) < v:
                deps[k] = v
        for r in reads:
            t = self.last_writer.get(r)
            if t is not None:
                add(*t)
        for w in writes:
            t = self.last_writer.get(w)
            if t is not None:
                add(*t)
            for k, v in self.readers.get(w, {}).items():
                add(k, v)
        out = []
        for k, v in deps.items():
            if eng == "pe" and k == "pe":
                continue
            if self.waited[eng].get(k, 0) >= v:
                continue
            self.waited[eng][k] = v
            out.append((k, v))
        return out

    def _record(self, tok, reads, writes):
        for w in writes:
            self.last_writer[w] = tok
            self.readers[w] = {}
        for r in reads:
            d = self.readers.setdefault(r, {})
            if d.get(tok[0], 0) < tok[1]:
                d[tok[0]] = tok[1]

    def op(self, eng, fn, reads=(), writes=()):
        waits = self._deps(eng, reads, writes)
        c = self.count.get(eng, 0) + 1
        self.count[eng] = c
        self._record((eng, c), reads, writes)
        self.streams[eng].append((waits, fn, eng, 1))

    def dma(self, eng, semkey, out, in_, reads=(), writes=()):
        if semkey not in self.semkeys:
            self.semkeys.append(semkey)
        waits = self._deps(eng, reads, writes)
        c = self.count.get(semkey, 0) + 16
        self.count[semkey] = c
        self._record((semkey, c), reads, writes)
        fn = lambda e, out=out, in_=in_: e.dma_start(out=out, in_=in_)
        self.streams[eng].append((waits, fn, semkey, 16))

    def wait_all(self, eng):
        waits = []
        for k, v in self.count.items():
            if self.waited[eng].get(k, 0) >= v:
                continue
            self.waited[eng][k] = v
            waits.append((k, v))
        self.streams[eng].append((waits, None, None, 0))

    def barrier(self):
        for e in ENGS:
            self.wait_all(e)

    def emit(self):
        nc = self.nc
        sems = {k: self.stack.enter_context(nc.semaphore("s_" + k)) for k in self.semkeys}
        block = self.stack.enter_context(nc.Block())

        def run(engname):
            def body(e):
                for waits, fn, semkey, inc in self.streams[engname]:
                    for k, v in waits:
                        e.wait_ge(sems[k], v)
                    if fn is not None:
                        fn(e).then_inc(sems[semkey], inc)
            return body
        block.tensor(run("pe"))
        block.scalar(run("act"))
        block.vector(run("dve"))
        block.gpsimd(run("pool"))
        block.sync(run("sp"))

    def close(self):
        self.stack.close()


D = 1024
S = 2048
NTILE = 16
C0 = math.exp(-0.5)
GN_EPS = 64e-5
ARENA_BYTES = 85 * 1024
SERIAL = False
SMALL = ["rwkv_mu", "rwkv_w0", "rwkv_a0", "rwkv_k_k", "rwkv_k_a", "rwkv_r_k", "rwkv_lnx_w", "rwkv_lnx_b",
         "lam_q1", "lam_k1", "lam_q2", "lam_k2", "diff_subln_g", "norm_mix_g", "norm_mlp_g", "norm_ple_g",
         "final_norm_g"]
BIGW = {"w_in": [4, 1024, 5376], "rwkv_w2": [4, 64, 512], "rwkv_a2": [4, 64, 512], "rwkv_g2": [4, 128, 512],
        "w_proj_a": [4, 512, 1024], "w_proj_b": [4, 512, 1024], "w_out": [4, 1024, 1024],
        "w_ff1": [4, 1024, 4096], "w_ff2": [4, 4096, 1024], "w_ple": [4, 256, 1024],
        "w_ple_gate": [4, 1024, 1024]}
SMALL_SHAPES = {"rwkv_mu": [4, 1792], "rwkv_w0": [4, 512], "rwkv_a0": [4, 512], "rwkv_k_k": [4, 512],
                "rwkv_k_a": [4, 512], "rwkv_r_k": [4, 512], "rwkv_lnx_w": [4, 512], "rwkv_lnx_b": [4, 512],
                "lam_q1": [1, 256], "lam_k1": [1, 256], "lam_q2": [1, 256], "lam_k2": [1, 256],
                "diff_subln_g": [4, 128], "norm_mix_g": [4, 1024], "norm_mlp_g": [4, 1024],
                "norm_ple_g": [4, 1024], "final_norm_g": [1, 1024]}


def build(depth=4, dbg=(), phases=6):
    nc = bass.Bass("TRN2", target_bir_lowering=False)
    P = Prog(nc)
    din = lambda n, s: nc.dram_tensor(n, list(s), F32, kind="ExternalInput").ap()
    x_d = din("x", [S, D])
    p_d = din("p", [4, S, 256])
    W = {k: din(k, s) for k, s in BIGW.items()}
    SM = {k: din(k, s) for k, s in SMALL_SHAPES.items()}
    alq_d = din("alq", [16, S])
    alk_d = din("alk", [16, S])
    y_d = nc.dram_tensor("y", [S, D], F32, kind="ExternalOutput").ap()
    dbg_d = {}

    xres = P.sbuf("xres", [128, NTILE, D], F32)
    hT = P.sbuf("hT", [128, 8, S], BF16)
    oaT = P.sbuf("oaT", [128, 4, S], BF16)
    ident_bf = P.sbuf("ident_bf", [128, 128], BF16)
    bo1_f = P.sbuf("bo1_f", [128, 128], F32)
    ones_f = P.sbuf("ones_f", [128, 128], F32)
    ones_bf = P.sbuf("ones_bf", [128, 128], BF16)
    maskN1 = P.sbuf("maskN1", [128, 128], BF16)
    maskNT1 = P.sbuf("maskNT1", [128, 128], BF16)
    maskIU1 = P.sbuf("maskIU1", [128, 64], BF16)
    trimask = P.sbuf("trimask", [128, 128], BF16)
    m64 = P.sbuf("m64", [128, 512], BF16)
    colA = P.sbuf("colA", [128, 120], F32)
    colB = P.sbuf("colB", [128, 52], F32)
    colC = P.sbuf("colC", [128, 96], F32)
    omka = P.sbuf("omka", [128, 16], F32)
    lamt = P.sbuf("lamt", [128, 16], F32)
    neglam = P.sbuf("neglam", [128, 4], F32)
    sgc = P.sbuf("sgc", [128, 4], F32)
    ssq = P.sbuf("ssq", [128, 16], F32)
    rstd = P.sbuf("rstd", [128, 16], F32)
    Zs = [[P.sbuf(f"Z{g}_{i}", [128, 128], F32) for i in range(2)] for g in range(4)]
    gprev = P.sbuf("gprev", [128, 4], F32)
    lastcol = P.sbuf("lastcol", [128, 14], F32)
    gpc = P.sbuf("gpc", [128, 8], F32)
    arena = P.sbuf("arena", [128, ARENA_BYTES // 4], F32)

    b4 = lambda ap: ap.rearrange("p (a b) -> p a b", b=128)
    maskN4 = maskN1[:, :].unsqueeze(1).broadcast_to([128, 4, 128])
    maskNT4 = maskNT1[:, :].unsqueeze(1).broadcast_to([128, 4, 128])
    maskIU4 = maskIU1[:, :].unsqueeze(1).broadcast_to([128, 8, 64])
    ident4 = ident_bf[:, :].unsqueeze(1).broadcast_to([128, 4, 128])
    pf = [P.psum(f"pf{i}", [128, 512], F32) for i in range(6)]
    ptb = [P.psum(f"pt{i}", [128, 1024], BF16) for i in range(2)]

    class Arena:
        def __init__(self):
            self.off = 0

        def reset(self, off=0):
            self.off = off

        def get(self, shape, dtype):
            n = int(np.prod(shape[1:]))
            nbytes = n * (4 if dtype == F32 else 2)
            nbytes = (nbytes + 31) // 32 * 32
            o = self.off
            self.off += nbytes
            assert self.off <= ARENA_BYTES, (self.off, ARENA_BYTES)
            v = arena[:, o // 4:(o + nbytes) // 4]
            if dtype != F32:
                v = v.bitcast(dtype)
            v = v[:, 0:n]
            if len(shape) == 3:
                v = v.rearrange("p (a b) -> p a b", b=shape[2])
            return v
    AR = Arena()
    cmk = AR.get([128, 512], F32)
    lamb = AR.get([128, 4, 256], F32)
    rowsA = AR.get([128, 128], F32)
    ident_f = AR.get([128, 128], F32)
    rot = {"pf": 0}

    def nextpf():
        i = rot["pf"]
        rot["pf"] = (i + 1) % 5
        return pf[i], f"pf{i}"

    def mm(out, lhsT, rhs, start, stop, reads, writes):
        P.op("pe", lambda e: e.matmul(out=out, lhsT=lhsT, rhs=rhs, start=start, stop=stop), reads, writes)

    def tr(out, in_, ident, reads, writes):
        P.op("pe", lambda e: e.transpose(out=out, in_=in_, identity=ident), reads, writes)

    def act(out, in_, func, reads, writes, **kw):
        P.op("act", lambda e: e.activation(out=out, in_=in_, func=func, **kw), reads, writes)

    def tt(out, in0, in1, op, reads, writes, eng="dve"):
        P.op(eng, lambda e: e.tensor_tensor(out=out, in0=in0, in1=in1, op=op), reads, writes)

    def ts(out, in0, s1, s2, op0, op1, reads, writes, eng="dve"):
        if op1 is None:
            P.op(eng, lambda e: e.tensor_scalar(out=out, in0=in0, scalar1=s1, scalar2=None, op0=op0), reads, writes)
        else:
            P.op(eng, lambda e: e.tensor_scalar(out=out, in0=in0, scalar1=s1, scalar2=s2, op0=op0, op1=op1), reads, writes)

    def stt(out, in0, scalar, in1, op0, op1, reads, writes):
        P.op("dve", lambda e: e.scalar_tensor_tensor(out=out, in0=in0, scalar=scalar, in1=in1, op0=op0, op1=op1), reads, writes)

    def recip(out, in_, reads, writes):
        P.op("dve", lambda e: e.reciprocal(out=out, in_=in_), reads, writes)

    def memset(ap, val, writes, eng="pool"):
        P.op(eng, lambda e: e.memset(ap, val), (), writes)

    def asel(out, in_, pattern, cmp, fill, base, cm, reads, writes):
        P.op("pool", lambda e: e.affine_select(out=out, in_=in_, pattern=pattern, compare_op=cmp, fill=fill,
                                               base=base, channel_multiplier=cm), reads, writes)

    def dump(name, ap, shape, reads):
        if name in dbg:
            d = nc.dram_tensor("dbg_" + name, list(shape), F32, kind="ExternalOutput").ap()
            dbg_d[name] = d
            P.dma("pool", "dbg_" + name, d, ap, reads=reads)

    memset(ident_f[:], 0.0, ["ident_f"])
    asel(ident_f[:], ident_f[:], [[-1, 128]], ALU.not_equal, 1.0, 0, 1, ["ident_f"], ["ident_f"])
    tt(ident_bf[:], ident_f[:], ident_f[:], ALU.mult, ["ident_f"], ["ident_bf"])
    memset(ones_f[:], 1.0, ["ones_f"])
    memset(ones_bf[:], 1.0, ["ones_bf"])
    memset(bo1_f[:], 0.0, ["bo1_f"])
    memset(bo1_f[0:64, 0:64], 1.0, ["bo1_f"])
    memset(bo1_f[64:128, 64:128], 1.0, ["bo1_f"])
    memset(cmk[:, 0:128], 0.0, ["cmk"])
    for hh in range(2):
        sl = slice(64 * hh, 64 * hh + 64)
        memset(cmk[sl, 64 * hh:64 * hh + 64], 1.0, ["cmk"])
        asel(cmk[sl, 64 * hh:64 * hh + 64], cmk[sl, 64 * hh:64 * hh + 64], [[1, 64]], ALU.is_gt, 0.0, 0, -1, ["cmk"], ["cmk"])
    tt(maskN1[:], cmk[:, 0:128], cmk[:, 0:128], ALU.mult, ["cmk"], ["maskN4"])
    memset(cmk[:, 128:256], 0.0, ["cmk2"])
    for hh in range(2):
        sl = slice(64 * hh, 64 * hh + 64)
        memset(cmk[sl, 128 + 64 * hh:128 + 64 * hh + 64], 1.0, ["cmk2"])
        asel(cmk[sl, 128 + 64 * hh:128 + 64 * hh + 64], cmk[sl, 128 + 64 * hh:128 + 64 * hh + 64], [[-1, 64]], ALU.is_gt, 0.0, 0, 1, ["cmk2"], ["cmk2"])
    tt(maskNT1[:], cmk[:, 128:256], cmk[:, 128:256], ALU.mult, ["cmk2"], ["maskNT4"])
    memset(cmk[:, 256:320], 1.0, ["cmk3"])
    for hh in range(2):
        sl = slice(64 * hh, 64 * hh + 64)
        asel(cmk[sl, 256:320], cmk[sl, 256:320], [[1, 64]], ALU.is_ge, 0.0, 0, -1, ["cmk3"], ["cmk3"])
    tt(maskIU1[:], cmk[:, 256:320], cmk[:, 256:320], ALU.mult, ["cmk3"], ["maskIU4"])
    memset(cmk[:, 384:512], 0.0, ["cmk4"])
    asel(cmk[:, 384:512], cmk[:, 384:512], [[1, 128]], ALU.is_ge, -30000.0, 0, -1, ["cmk4"], ["cmk4"])
    tt(trimask[:], cmk[:, 384:512], ones_f[:], ALU.mult, ["cmk4", "ones_f"], ["trimask"])
    memset(m64[:], 1.0, ["m64"])
    memset(m64[:].rearrange("p (c t) -> p c t", t=64)[:, :, 0:1], 0.0, ["m64"])
    for g in range(4):
        for i in range(2):
            memset(Zs[g][i][:], 0.0, [f"Z{g}_{i}"])

    def load_rows(specs, dst, ncols, dkey):
        r = 0
        for name, nrow in specs:
            src = SM[name].rearrange("l (c p) -> (l c) p", p=128)
            P.dma("sp", "d_rows", rowsA[r:r + nrow, :], src, writes=["rowsA"])
            r += nrow
        assert r == ncols
        ps, pk = nextpf()
        tr(ps[:, 0:ncols], rowsA[0:ncols, :], ident_f[0:ncols, 0:ncols], ["rowsA", "ident_f"], [pk])
        act(dst[:, 0:ncols], ps[:, 0:ncols], AF.Copy, [pk], [dkey])
    load_rows([("rwkv_mu", 56), ("rwkv_w0", 16), ("rwkv_a0", 16), ("rwkv_k_k", 16), ("rwkv_k_a", 16)], colA, 120, "colA")
    load_rows([("rwkv_r_k", 16), ("rwkv_lnx_w", 16), ("rwkv_lnx_b", 16), ("diff_subln_g", 4)], colB, 52, "colB")
    load_rows([("norm_mix_g", 32), ("norm_mlp_g", 32), ("norm_ple_g", 32)], colC, 96, "colC")
    CA, CB, CC = "colA", "colB", "colC"
    ts(omka[:], colA[:, 104:120], -1.0, 1.0, ALU.mult, ALU.add, [CA], ["omka"])
    for i, nm in enumerate(["lam_q1", "lam_k1", "lam_q2", "lam_k2"]):
        P.dma("sp", "d_lam", lamb[:, i, :], SM[nm].partition_broadcast(128).rearrange("p o n -> p (o n)"),
              writes=["lamb"])
    tt(lamb[:, 0, :], lamb[:, 0, :], lamb[:, 1, :], ALU.mult, ["lamb"], ["lamb"])
    tt(lamb[:, 2, :], lamb[:, 2, :], lamb[:, 3, :], ALU.mult, ["lamb"], ["lamb"])
    P.op("dve", lambda e: e.tensor_reduce(out=lamt[:, 0:4], in_=lamb[:, 0, :].rearrange("p (l d) -> p l d", d=64),
                                          axis=mybir.AxisListType.X, op=ALU.add), ["lamb"], ["lamt"])
    P.op("dve", lambda e: e.tensor_reduce(out=lamt[:, 4:8], in_=lamb[:, 2, :].rearrange("p (l d) -> p l d", d=64),
                                          axis=mybir.AxisListType.X, op=ALU.add), ["lamb"], ["lamt"])
    act(lamt[:, 8:16], lamt[:, 0:8], AF.Exp, ["lamt"], ["lamt"])
    tt(lamt[:, 0:4], lamt[:, 12:16], lamt[:, 8:12], ALU.subtract, ["lamt"], ["lamt"])
    for l in range(4):
        li = 0.8 - 0.6 * math.exp(-0.3 * l)
        ts(neglam[:, l:l + 1], lamt[:, l:l + 1], -li, None, ALU.add, None, ["lamt"], ["neglam"])
        ts(sgc[:, l:l + 1], colB[:, 48 + l:49 + l], 1.0 - li, None, ALU.mult, None, [CB], ["sgc"])

    xv = x_d.rearrange("(t p) d -> p t d", p=128)
    for i in range(4):
        P.dma("sp", f"d_x{i}", xres[:, 4 * i:4 * i + 4, :], xv[:, 4 * i:4 * i + 4, :], writes=[f"x{t}" for t in range(4 * i, 4 * i + 4)])

    wdma = {"i": 0}

    def wload(dst, src, key):
        P.dma("pool", "dw_" + key, dst, src, writes=[key])

    XK = [f"x{t}" for t in range(NTILE)]
    HK = [f"hT{t}" for t in range(NTILE)]

    XN_OFF = ARENA_BYTES - 4096
    AR.reset(XN_OFF)
    xn_fix = [AR.get([128, D], BF16) for _ in range(2)]
    AR.reset(0)
    yv = y_d.rearrange("(t p) d -> p t d", p=128)

    def tile_rstd(t):
        b = t % 2
        act(xn_fix[b], xres[:, t, :], AF.Square, [f"x{t}"], [f"xn{b}", f"ssq{t}"], accum_out=ssq[:, t:t + 1])
        act(rstd[:, t:t + 1], ssq[:, t:t + 1], AF.Sqrt, [f"ssq{t}"], [f"rstd{t}"], scale=1.0 / D, bias=1e-6)
        recip(rstd[:, t:t + 1], rstd[:, t:t + 1], [f"rstd{t}"], [f"rstd{t}"])

    def norm_front(t, gbase):
        b = t % 2
        tile_rstd(t)
        ts(xn_fix[b], xres[:, t, :], rstd[:, t:t + 1], None, ALU.mult, None, [f"x{t}", f"rstd{t}"], [f"xn{b}"])

    def norm_back(t, gbase):
        b = t % 2
        for kc in range(8):
            tr(ptb[b][:, kc * 128:(kc + 1) * 128], xn_fix[b][:, kc * 128:(kc + 1) * 128], ident_bf[:], [f"xn{b}", "ident_bf"], [f"pt{b}"])
        tt(hT[:, :, t * 128:(t + 1) * 128], ptb[b][:, :].rearrange("p (a b) -> p a b", b=128),
           colC[:, gbase:gbase + 8].unsqueeze(2).broadcast_to([128, 8, 128]), ALU.mult, [f"pt{b}", CC], [f"hT{t}"])

    def norm_tile(t, gbase):
        norm_front(t, gbase)
        norm_back(t, gbase)

    norm_def = []

    def norm_flush(keep=0):
        while len(norm_def) > keep:
            norm_def.pop(0)[1]()

    def norm_tile_def(t, gbase):
        while any(tp % 2 == t % 2 for tp, _ in norm_def):
            norm_def.pop(0)[1]()
        norm_front(t, gbase)
        norm_def.append((t, lambda t=t, gbase=gbase: norm_back(t, gbase)))

    def norm_hT(gbase):
        for t in range(NTILE):
            norm_tile(t, gbase)

    fin = {}

    def final_tile(t):
        tile_rstd(t)
        stt(fin["ob"], xres[:, t, :], rstd[:, t:t + 1], fin["gfin"], ALU.mult, ALU.mult, [f"x{t}", f"rstd{t}", "gfin"], ["obf"])
        P.dma("sp", "d_y", yv[:, t, :], fin["ob"], reads=["obf"])

    def hk(tok0, n):
        return [f"hT{t}" for t in range(tok0 // 128, (tok0 + n) // 128)]

    def rwkv(l):
        AR.reset(0)
        wsmU = [[AR.get([128, 8, 128], BF16) for _ in range(3)] for _ in range(2)]
        wsmL = [AR.get([128, 8, 128], BF16) for _ in range(2)]
        wl2 = AR.get([128, 512], BF16)
        wg2 = AR.get([128, 512], BF16)
        raw = AR.get([128, 520], F32)
        T1 = AR.get([128, 512], F32)
        T2 = AR.get([128, 512], F32)
        xr = AR.get([128, 512], F32)
        xk = AR.get([128, 512], F32)
        xvv = AR.get([128, 512], F32)
        sg = AR.get([128, 512], F32)
        cs = AR.get([128, 512], F32)
        eg = AR.get([128, 512], F32)
        einv = AR.get([128, 512], F32)
        al = AR.get([128, 512], F32)
        lwh = AR.get([128, 512], BF16)
        lgh = AR.get([128, 512], BF16)
        pkgs = []
        for pi in range(2):
            pg = {k: AR.get([128, 8, 128], BF16) for k in "abkv"}
            pg["rt"] = AR.get([128, 512], BF16)
            pg["bon"] = AR.get([128, 512], BF16)
            pg["gg"] = AR.get([128, 512], BF16)
            pg["gpc"] = AR.get([128, 8], F32)
            pkgs.append(pg)
        yT = AR.get([128, 512], F32)
        XS = [[AR.get([128, 512], BF16) for _ in range(2)]] * 2
        XTS = [[AR.get([128, 512], BF16) for _ in range(2)]] * 2
        QS = [[AR.get([128, 512], BF16) for _ in range(2)]] * 2
        QFIN = [None, None]
        AKT = AR.get([128, 512], BF16)
        ARS = AR.get([128, 512], BF16)
        TMab = AR.get([128, 1024], BF16)
        TMkv = AR.get([128, 1024], BF16)
        AH = AR.get([128, 512], BF16)
        AV = AR.get([128, 512], BF16)
        PV = AR.get([128, 512], BF16)
        RH = AR.get([128, 256], BF16)
        MT = AR.get([128, 512], F32)
        G1 = MT
        S0b = AR.get([128, 128], BF16)

        for pi in range(2):
            for k in "abkv":
                memset(pkgs[pi][k], 0.0, [f"pad{k}{pi}"])
        memset(lastcol[:], 0.0, ["lastcol"])
        memset(gprev[:], 0.0, ["gprev"])
        wload(wl2[0:64, :], W["rwkv_w2"][l], "wl2")
        wload(wl2[64:128, :], W["rwkv_a2"][l], "wl2")
        wload(wg2[:, :], W["rwkv_g2"][l], "wg2")
        win = W["w_in"][l].rearrange("(kc p) n -> p kc n", p=128)
        wload(wsmL[0], win[:, :, 1536:1664], "wsmL0")
        wload(wsmL[1], win[:, :, 1664:1792], "wsmL1")
        zi = [0, 0, 0, 0]

        def load_unit_weights(u):
            qt, g = divmod(u, 4)
            st = u % 2
            for ti in range(3):
                cc = ti * 4 + g
                wload(wsmU[st][ti], win[:, :, cc * 128:(cc + 1) * 128], f"wsmU{st}{ti}")

        def shift_proj(wbuf, wkey, cc, tok0, dst, dkey):
            ps, pk = nextpf()
            for kc in range(8):
                mm(ps[:], wbuf[:, kc, :], hT[:, kc, tok0:tok0 + 512], kc == 0, kc == 7, [wkey] + hk(tok0, 512), [pk])
            act(raw[:, 0:1], lastcol[:, cc:cc + 1], AF.Copy, ["lastcol"], ["raw"])
            act(raw[:, 1:513], ps[:], AF.Copy, [pk], ["raw"])
            yield
            act(lastcol[:, cc:cc + 1], raw[:, 512:513], AF.Copy, ["raw"], ["lastcol"])
            tt(T1, raw[:, 0:512], raw[:, 1:513], ALU.subtract, ["raw"], ["T1"])
            yield
            stt(dst, T1, colA[:, l * 14 + cc:l * 14 + cc + 1], raw[:, 1:513], ALU.mult, ALU.add, ["T1", "raw", CA], [dkey])
            yield

        v3 = lambda ap, sl: ap[sl, :].rearrange("p (c t) -> p c t", t=64)

        def prep1(u):
            qt, g = divmod(u, 4)
            tok0 = qt * 512
            gs = slice(g * 128, (g + 1) * 128)
            cg = l * 4 + g
            st = u % 2
            if u + 1 < 16:
                load_unit_weights(u + 1)
            if g == 0:
                yield from shift_proj(wsmL[0], "wsmL0", 12, tok0, T2, "T2")
                act(lwh[0:64, :], T2[0:64, :], AF.Tanh, ["T2"], ["lwh"])
                act(lwh[64:128, :], T2[64:128, :], AF.Copy, ["T2"], ["lwh"])
                yield
                yield from shift_proj(wsmL[1], "wsmL1", 13, tok0, T2, "T2")
                act(lgh, T2, AF.Sigmoid, ["T2"], ["lgh"])
                yield
            yield from shift_proj(wsmU[st][0], f"wsmU{st}0", g, tok0, xr, "xr")
            yield from shift_proj(wsmU[st][1], f"wsmU{st}1", 4 + g, tok0, xk, "xk")
            yield from shift_proj(wsmU[st][2], f"wsmU{st}2", 8 + g, tok0, xvv, "xv")
            ps, pk = nextpf()
            mm(ps[:], wl2[0:64, gs], lwh[0:64, :], True, True, ["wl2", "lwh"], [pk])
            act(sg, ps[:], AF.Sigmoid, [pk, CA], ["sg"], bias=colA[:, 56 + cg:57 + cg])
            yield
            ps, pk = nextpf()
            mm(ps[:], wl2[64:128, gs], lwh[64:128, :], True, True, ["wl2", "lwh"], [pk])
            act(al, ps[:], AF.Sigmoid, [pk, CA], ["al"], bias=colA[:, 72 + cg:73 + cg])
            yield
            act(T1, xk, AF.Square, ["xk", CA], ["T1"], scale=colA[:, 88 + cg:89 + cg])
            yield
            P.op("dve", lambda e: e.tensor_tensor_scan(out=cs, data0=m64[:], data1=sg, initial=0.0, op0=ALU.mult, op1=ALU.add),
                 ["m64", "sg"], ["cs"])
            ps, pk = nextpf()
            mm(ps[:], bo1_f[:], T1, True, True, ["bo1_f", "T1"], [pk])
            ts(T1, ps[:], 1e-24, None, ALU.max, None, [pk], ["T1"])
            yield
            act(eg, cs, AF.Exp, ["cs"], ["eg"], scale=-C0)
            yield
            act(einv, cs, AF.Exp, ["cs"], ["einv"], scale=C0)
            ts(T2, al, colA[:, 104 + cg:105 + cg], omka[:, cg:cg + 1], ALU.mult, ALU.add, ["al", CA, "omka"], ["T2"])
            yield
            act(T1, T1, AF.Ln, ["T1"], ["T1"])
            tt(cs, cs, sg, ALU.subtract, ["cs", "sg"], ["cs"])
            yield
            act(T1, T1, AF.Exp, ["T1"], ["T1"], scale=-0.5)
            yield
            act(sg, cs, AF.Exp, ["cs"], ["sg"], scale=-C0)
            stt(cs, xk, colA[:, 88 + cg:89 + cg], T1, ALU.mult, ALU.mult, ["xk", "T1", CA], ["cs"])
            yield
            tt(xk, xk, T2, ALU.mult, ["xk", "T2"], ["xk"])
            yield
            stt(T2, xr, colB[:, cg:cg + 1], xk, ALU.mult, ALU.mult, ["xr", "xk", CB], ["T2"])
            yield
            tt(T1, al, einv, ALU.mult, ["al", "einv"], ["T1"])
            yield

        def prep2(u, pi):
            qt, g = divmod(u, 4)
            pg = pkgs[pi]
            K = lambda n: f"{n}{pi}"
            gs = slice(g * 128, (g + 1) * 128)
            kkn, egm = cs, sg
            ps, pk = nextpf()
            mm(ps[:], wg2[:, gs], lgh, True, True, ["wg2", "lgh"], [pk])
            act(pg["gg"], ps[:], AF.Copy, [pk], [K("gg")])
            yield
            ps, pk = nextpf()
            mm(ps[:], bo1_f[:], T2, True, True, ["bo1_f", "T2"], [pk])
            tt(pg["bon"], ps[:], xvv, ALU.mult, [pk, "xv"], [K("bon")])
            yield
            for hh in range(2):
                sl = slice(64 * hh, 64 * hh + 64)
                stt(pg["a"][sl, :, sl], v3(kkn, sl), -1.0, v3(egm, sl), ALU.mult, ALU.mult, ["cs", "sg"], [K("pada")])
                yield
                tt(pg["b"][sl, :, sl], v3(kkn, sl), v3(T1, sl), ALU.mult, ["cs", "T1"], [K("padb")])
                yield
                tt(pg["k"][sl, :, sl], v3(xk, sl), v3(einv, sl), ALU.mult, ["xk", "einv"], [K("padk")])
                act(pg["v"][sl, :, sl], v3(xvv, sl), AF.Copy, ["xv"], [K("padv")])
                yield
            tt(pg["rt"], xr, eg, ALU.mult, ["xr", "eg"], [K("rt")])
            act(pg["gpc"][:, 0:1], gprev[:, g:g + 1], AF.Copy, ["gprev"], [K("gpc")])
            yield
            act(pg["gpc"][:, 1:8], eg.rearrange("p (c t) -> p c t", t=64)[:, 0:7, 63], AF.Copy, ["eg"], [K("gpc")])
            act(gprev[:, g:g + 1], eg[:, 511:512], AF.Copy, ["eg"], ["gprev"])
            yield

        blk = lambda ap, i: ap[:, i * 128:(i + 1) * 128]

        def neumann(k):
            u, q = divmod(k, 2)
            pi = u % 2
            pg = pkgs[pi]
            K = lambda n: f"{n}{pi}"
            s_ = k % 2
            X_, XT_, Q_ = XS[s_], XTS[s_], QS[s_]
            xk_ = lambda n, i: f"{n}_{i}"
            pada, padb = pg["a"], pg["b"]
            chs = [4 * q + i for i in range(4)]
            ps, pk = nextpf()
            for i, c in enumerate(chs):
                mm(blk(ps, i), padb[:, c, :], pada[:, c, :], True, True, [K("pada"), K("padb")], [pk])
            tt(b4(X_[0]), b4(ps[:]), maskN4, ALU.mult, [pk, "maskN4"], [xk_("X", 0)])
            ps, pk = nextpf()
            for i, c in enumerate(chs):
                mm(blk(ps, i), pada[:, c, :], padb[:, c, :], True, True, [K("pada"), K("padb")], [pk])
            tt(b4(XT_[0]), b4(ps[:]), maskNT4, ALU.mult, [pk, "maskNT4"], [xk_("XT", 0)])
            tt(b4(Q_[0]), b4(X_[0]), ident4, ALU.add, [xk_("X", 0), "ident_bf"], [xk_("Q", 0)], eng="pool")
            yield
            cur = 0
            for lev in range(1, 6):
                nx = 1 - cur
                ps, pk = nextpf()
                for i in range(4):
                    mm(blk(ps, i), blk(X_[cur], i), blk(XT_[cur], i), True, True, [xk_("X", cur), xk_("XT", cur)], [pk])
                act(XT_[nx], ps[:], AF.Copy, [pk], [xk_("XT", nx)])
                if lev < 5:
                    ps, pk = nextpf()
                    for i in range(4):
                        mm(blk(ps, i), blk(XT_[cur], i), blk(X_[cur], i), True, True, [xk_("X", cur), xk_("XT", cur)], [pk])
                    P.op("dve", lambda e, o=X_[nx], s=ps: e.tensor_copy(out=o, in_=s[:]), [pk], [xk_("X", nx)])
                yield
                ps, pk = nextpf()
                for i in range(4):
                    mm(blk(ps, i), blk(XT_[nx], i), blk(Q_[cur], i), True, True, [xk_("XT", nx), xk_("Q", cur)], [pk])
                tt(Q_[nx], ps[:], Q_[cur], ALU.add, [pk, xk_("Q", cur)], [xk_("Q", nx)])
                cur = nx
                yield
            QFIN[s_] = (Q_[cur], xk_("Q", cur))

        def tail(k):
            u, q = divmod(k, 2)
            pi = u % 2
            qt, g = divmod(u, 4)
            tok0 = qt * 512
            pg = pkgs[pi]
            K = lambda n: f"{n}{pi}"
            cg = l * 4 + g
            pada, padb, padk, padv, rt, gpc_ = pg["a"], pg["b"], pg["k"], pg["v"], pg["rt"], pg["gpc"]
            chs = [4 * q + i for i in range(4)]
            ps, pk = nextpf()
            for i, c in enumerate(chs):
                mm(blk(ps, i), padk[:, c, :], pada[:, c, :], True, True, [K("pada"), K("padk")], [pk])
            tt(b4(AKT), b4(ps[:]), maskN4, ALU.mult, [pk, "maskN4"], ["AKT"])
            ps, pk = nextpf()
            for i, c in enumerate(chs):
                mm(ps[:, i * 128:i * 128 + 64], padb[:, c, :], rt[:, c * 64:(c + 1) * 64], True, True, [K("padb"), K("rt")], [pk])
                mm(ps[:, i * 128 + 64:(i + 1) * 128], padk[:, c, :], rt[:, c * 64:(c + 1) * 64], True, True, [K("padk"), K("rt")], [pk])
            tt(ARS.rearrange("p (a b) -> p a b", b=64), ps[:].rearrange("p (a b) -> p a b", b=64), maskIU4, ALU.mult, [pk, "maskIU4"], ["ARS"])
            yield
            for i, c in enumerate(chs):
                tr(ptb[0][:, i * 128:(i + 1) * 128], pada[:, c, :], ident_bf[:], [K("pada"), "ident_bf"], ["pt0"])
                tr(ptb[0][:, (4 + i) * 128:(5 + i) * 128], padb[:, c, :], ident_bf[:], [K("padb"), "ident_bf"], ["pt0"])
            act(TMab, ptb[0][:, :], AF.Copy, ["pt0"], ["TMab"])
            for i, c in enumerate(chs):
                tr(ptb[1][:, i * 128:(i + 1) * 128], padk[:, c, :], ident_bf[:], [K("padk"), "ident_bf"], ["pt1"])
                tr(ptb[1][:, (4 + i) * 128:(5 + i) * 128], padv[:, c, :], ident_bf[:], [K("padv"), "ident_bf"], ["pt1"])
            P.op("dve", lambda e: e.tensor_copy(out=TMkv, in_=ptb[1][:, :]), ["pt1"], ["TMkv"])
            yield
            QF, QK = QFIN[k % 2]
            TMa = lambda i: TMab[:, i * 128:(i + 1) * 128]
            TMb = lambda i: TMab[:, (4 + i) * 128:(5 + i) * 128]
            TMk = lambda i: TMkv[:, i * 128:(i + 1) * 128]
            TMv = lambda i: TMkv[:, (4 + i) * 128:(5 + i) * 128]
            ps, pk = nextpf()
            for i in range(4):
                mm(blk(ps, i), blk(QF, i), TMa(i), True, True, [QK, "TMab"], [pk])
            act(AH, ps[:], AF.Copy, [pk], ["AH"])
            ps, pk = nextpf()
            for i in range(4):
                mm(blk(ps, i), blk(AKT, i), TMv(i), True, True, ["AKT", "TMkv"], [pk])
            P.op("dve", lambda e, s=ps: e.tensor_copy(out=AV, in_=s[:]), [pk], ["AV"])
            yield
            ps, pk = nextpf()
            for i in range(4):
                mm(blk(ps, i), blk(QF, i), blk(AV, i), True, True, [QK, "AV"], [pk])
            act(PV, ps[:], AF.Copy, [pk], ["PV"])
            ps, pk = nextpf()
            for i in range(4):
                mm(ps[:, i * 64:(i + 1) * 64], blk(AH, i), ARS[:, i * 128:i * 128 + 64], True, True, ["AH", "ARS"], [pk])
            tt(RH, ps[:, 0:256], rt[:, q * 256:(q + 1) * 256], ALU.add, [pk, K("rt")], ["RH"])
            ps, pk = nextpf()
            for i in range(4):
                mm(blk(ps, i), blk(AH, i), TMb(i), True, True, ["AH", "TMab"], [pk])
            tt(b4(MT), b4(ps[:]), gpc_[:, 4 * q:4 * q + 4].unsqueeze(2).broadcast_to([128, 4, 128]), ALU.mult, [pk, K("gpc")], ["MT"])
            yield
            psy, pyk = pf[5], "pf5"
            for i, c in enumerate(chs):
                gcol = gpc_[:, c:c + 1]
                zc, zck = Zs[g][zi[g]], f"Z{g}_{zi[g]}"
                zn, znk = Zs[g][1 - zi[g]], f"Z{g}_{1 - zi[g]}"
                act(S0b, zc[:], AF.Copy, [zck, K("gpc")], ["S0b"], scale=gcol)
                pss, psk = nextpf()
                mm(pss[:, 0:128], TMb(i), blk(PV, i), True, False, ["TMab", "PV"], [psk])
                mm(pss[:, 0:128], TMk(i), TMv(i), False, False, ["TMkv"], [psk])
                mm(pss[:, 0:128], blk(MT, i), zc[:], False, True, ["MT", zck], [psk])
                stt(zn[:], zc[:], gcol, pss[:, 0:128], ALU.mult, ALU.add, [zck, K("gpc"), psk], [znk])
                yo = psy[:, i * 64:(i + 1) * 64]
                mm(yo, S0b, RH[:, i * 64:(i + 1) * 64], True, False, ["S0b", "RH"], [pyk])
                mm(yo, blk(PV, i), ARS[:, i * 128:i * 128 + 64], False, False, ["PV", "ARS"], [pyk])
                mm(yo, TMv(i), ARS[:, i * 128 + 64:(i + 1) * 128], False, True, ["TMkv", "ARS"], [pyk])
                zi[g] = 1 - zi[g]
                yield
            act(yT[:, q * 256:(q + 1) * 256], psy[:, 0:256], AF.Copy, [pyk], ["yT"])
            if q == 0:
                return
            ps, pk = nextpf()
            mm(ps[:], bo1_f[:], yT, True, True, ["bo1_f", "yT"], [pk])
            stt(yT, ps[:], -1.0 / 64.0, yT, ALU.mult, ALU.add, ["yT", pk], ["yT"])
            act(G1, yT, AF.Square, ["yT"], ["MT"])
            ps, pk = nextpf()
            mm(ps[:], bo1_f[:], G1, True, True, ["bo1_f", "MT"], [pk])
            act(G1, ps[:], AF.Ln, [pk], ["MT"], bias=GN_EPS, scale=1.0 / 64.0)
            act(G1, G1, AF.Exp, ["MT"], ["MT"], scale=-0.5)
            yield
            tt(yT, yT, G1, ALU.mult, ["yT", "MT"], ["yT"])
            ts(yT, yT, colB[:, 16 + cg:17 + cg], colB[:, 32 + cg:33 + cg], ALU.mult, ALU.add, ["yT", CB], ["yT"])
            tt(yT, yT, pg["bon"], ALU.add, ["yT", K("bon")], ["yT"])
            tt(oaT[:, g, tok0:tok0 + 512], yT, pg["gg"], ALU.mult, ["yT", K("gg")], [f"oa{g}_{qt}"])
            yield

        def run_gens(items):
            if SERIAL:
                for gen, n in items:
                    for _ in gen:
                        pass
                return
            alive = [list(it) for it in items]
            while alive:
                for it in list(alive):
                    gen, n = it
                    for _ in range(n):
                        try:
                            next(gen)
                        except StopIteration:
                            alive.remove(it)
                            break

        load_unit_weights(0)
        run_gens([(prep1(0), 1)])
        run_gens([(prep2(0, 0), 1)])
        run_gens([(neumann(0), 1), (prep1(1), 2)])
        NQ = 32
        for k in range(NQ):
            tg = tail(k)
            for _ in range(4):
                next(tg)
            items = [(tg, 1)]
            if k + 1 < NQ:
                items.append((neumann(k + 1), 1))
            u, q = divmod(k, 2)
            if q == 0 and u + 1 < 16:
                items.append((prep2(u + 1, (u + 1) % 2), 1))
            if q == 1 and u + 2 < 16:
                items.append((prep1(u + 2), 2))
            run_gens(items)
        dump("oaT", oaT[:], [128, 4, S], [f"oa{g}_{qt}" for g in range(4) for qt in range(4)])

    def attention(l):
        AR.reset(0)
        obT = AR.get([128, 4, S], BF16)
        vtm = AR.get([128, NTILE, 512], BF16)
        qa = [[AR.get([128, S], BF16) for _ in range(2)] for _ in range(2)]
        ka = [[AR.get([128, S], BF16) for _ in range(2)] for _ in range(2)]
        stage = AR.get([128, S], BF16)
        wq = [AR.get([128, 8, 128], BF16) for _ in range(2)]
        wv_off = AR.off
        wv = AR.get([128, 8, 512], BF16)
        PT = [AR.get([128, 512], BF16) for _ in range(4)]
        STILES = [(pf[0], "pf0"), (pf[1], "pf1"), (ptb[1][:, :].bitcast(F32), "pt1")]
        win = W["w_in"][l].rearrange("(kc p) n -> p kc n", p=128)
        wload(wv, win[:, :, 2816:3328], "wv")
        for kt in range(NTILE):
            ps, pk = nextpf()
            for kc in range(8):
                mm(ps[:], hT[:, kc, kt * 128:(kt + 1) * 128], wv[:, kc, :], kc == 0, kc == 7, [f"hT{kt}", "wv"], [pk])
            P.op("dve", lambda e, o=vtm[:, kt, :], s_=ps: e.tensor_copy(out=o, in_=s_[:]), [pk], [f"vtm{kt}"])
        P.barrier()
        sv = AR.off
        AR.reset(wv_off)
        Rc = AR.get([128, 512], F32)
        A0 = AR.get([128, 512], F32)
        A1 = AR.get([128, 512], F32)
        Tq = AR.get([128, 512], F32)
        AR.reset(sv)
        wr = 0
        ptr = 0
        deferred = []
        def proj(h):
            nonlocal wr
            hb = h % 2
            for which, cb, aug, sc in (("q", 1792 + h * 128, qa[hb], 0.125), ("k", 2304 + h * 128, ka[hb], 1.0)):
                wload(wq[wr], win[:, :, cb:cb + 128], f"wq{wr}")
                for tc in range(4):
                    ps, pk = nextpf()
                    for kc in range(8):
                        mm(ps[:], wq[wr][:, kc, :], hT[:, kc, tc * 512:(tc + 1) * 512], kc == 0, kc == 7, [f"wq{wr}"] + hk(tc * 512, 512), [pk])
                    ts(aug[0][0:64, tc * 512:(tc + 1) * 512], ps[0:64, :], sc, None, ALU.mult, None, [pk], [f"{which}a{hb}0"])
                    ts(stage[64:128, tc * 512:(tc + 1) * 512], ps[64:128, :], sc, None, ALU.mult, None, [pk], ["stage"])
                P.dma("sp", f"d_rep{which}{hb}", aug[1][0:64, :], stage[64:128, :], reads=["stage"], writes=[f"{which}a{hb}1"])
                wr = 1 - wr
            for c in range(2):
                P.dma("pool", f"d_alq{hb}{c}", qa[hb][c][64:68, :], alq_d[4 * h:4 * h + 4, :], writes=[f"qa{hb}{c}"])
                P.dma("pool", f"d_alk{hb}{c}", ka[hb][c][64:68, :], alk_d[4 * h:4 * h + 4, :], writes=[f"ka{hb}{c}"])

        proj(0)
        pend = []
        sctr = [0]

        def drain(n):
            while len(pend) > n:
                ofn, st, endfn = pend.pop(0)
                ofn(st)
                if endfn is not None:
                    endfn()

        for h in range(4):
            for j in range(4):
                nkt = 4 * (j + 1)
                for c in range(2):
                    O, Ok = pf[2 + 2 * c], f"pf{2 + 2 * c}"
                    Lp, Lk = pf[3 + 2 * c], f"pf{3 + 2 * c}"

                    def s_stage(kt, j=j, c=c, h=h):
                        nonlocal ptr
                        t = kt - 4 * j
                        n0 = 128 * t if t > 0 else 0
                        Sp, Sk = STILES[sctr[0] % 3]
                        sctr[0] += 1
                        mm(Sp[:, n0:512], ka[h % 2][c][0:68, kt * 128:(kt + 1) * 128], qa[h % 2][c][0:68, j * 512 + n0:(j + 1) * 512], True, t < 0,
                           [f"ka{h % 2}{c}", f"qa{h % 2}{c}"], [Sk])
                        if t >= 0:
                            mm(Sp[:, n0:n0 + 128], ident_bf[:], trimask[:], False, True, ["ident_bf", "trimask"], [Sk])
                        pb = PT[ptr]
                        pbk = f"PT{ptr}"
                        ptr = (ptr + 1) % 4
                        act(pb[:, n0:512], Sp[:, n0:512], AF.Exp, [Sk], [pbk])
                        return kt, n0, pb, pbk

                    def o_stage(st, nkt=nkt, O=O, Ok=Ok, Lp=Lp, Lk=Lk, h=h):
                        kt, n0, pb, pbk = st
                        mm(O[:, n0:512], vtm[:, kt, h * 128:(h + 1) * 128], pb[:, n0:512], kt == 0, kt == nkt - 1, [f"vtm{kt}", pbk], [Ok])
                        mm(Lp[:, n0:512], ones_bf[:], pb[:, n0:512], kt == 0, kt == nkt - 1, ["ones_bf", pbk], [Lk])

                    def block_end(c=c, O=O, Ok=Ok, Lp=Lp, Lk=Lk, h=h, j=j):
                        act(Rc, Lp[:], AF.Ln, [Lk], ["Rc"])
                        act(Rc, Rc, AF.Exp, ["Rc"], ["Rc"], scale=-1.0)
                        tt((A0, A1)[c], O[:], Rc, ALU.mult, [Ok, "Rc"], [f"A{c}"])
                        if c == 1:
                            stt(A0, A1, neglam[:, l:l + 1], A0, ALU.mult, ALU.add, ["A0", "A1", "neglam"], ["A0"])
                            tt(Tq, A0, A0, ALU.mult, ["A0"], ["Tq"])

                            def tail(h=h, j=j):
                                ps = ptb[0][:, :].bitcast(F32)
                                mm(ps, ones_f[:], Tq, True, True, ["ones_f", "Tq"], ["pt0"])
                                act(Tq, ps, AF.Ln, ["pt0"], ["Tq"], scale=1.0 / 128.0, bias=1e-5)
                                act(Tq, Tq, AF.Exp, ["Tq"], ["Tq"], scale=-0.5)
                                tt(A0, A0, Tq, ALU.mult, ["A0", "Tq"], ["A0"])
                                ts(obT[:, h, j * 512:(j + 1) * 512], A0, sgc[:, l:l + 1], None, ALU.mult, None, ["A0", "sgc"], [f"ob{h}_{j}"])
                            deferred.append(tail)

                    for kt in range(nkt):
                        pend.append((o_stage, s_stage(kt), block_end if kt == nkt - 1 else None))
                        drain(2)
                        if kt == 3:
                            while deferred:
                                deferred.pop(0)()
                if j == 0 and h + 1 < 4:
                    drain(0)
                    proj(h + 1)
        drain(0)
        while deferred:
            deferred.pop(0)()
        dump("obT", obT, [128, 4, S], [f"ob{h}_{j}" for h in range(4) for j in range(4)])
        return obT

    def merge(l, obT, after_tile=None):
        AR.reset(16384)
        mT = AR.get([128, 8, S], BF16)
        sv = AR.off
        wga = [AR.get([128, 8, 128], BF16) for _ in range(2)]
        wgb = [AR.get([128, 8, 128], BF16) for _ in range(2)]
        pa = [AR.get([128, 4, 128], BF16) for _ in range(2)]
        pb_ = [AR.get([128, 4, 128], BF16) for _ in range(2)]
        ga = [AR.get([128, 512], BF16) for _ in range(2)]
        gb = [AR.get([128, 512], BF16) for _ in range(2)]
        t1 = AR.get([128, 512], F32)
        t2 = AR.get([128, 512], F32)
        win = W["w_in"][l].rearrange("(kc p) n -> p kc n", p=128)
        wpa = W["w_proj_a"][l].rearrange("(kc p) n -> p kc n", p=128)
        wpb = W["w_proj_b"][l].rearrange("(kc p) n -> p kc n", p=128)
        OAK = [f"oa{g}_{qt}" for g in range(4) for qt in range(4)]
        OBK = [f"ob{h}_{j}" for h in range(4) for j in range(4)]
        gi = 0
        for dch in range(8):
            b = dch % 2
            wload(wga[b], win[:, :, 3328 + dch * 128:3328 + (dch + 1) * 128], f"wga{b}")
            wload(wgb[b], win[:, :, 4352 + dch * 128:4352 + (dch + 1) * 128], f"wgb{b}")
            wload(pa[b], wpa[:, :, dch * 128:(dch + 1) * 128], f"pa{b}")
            wload(pb_[b], wpb[:, :, dch * 128:(dch + 1) * 128], f"pb{b}")
            for j in range(4):
                tk = slice(j * 512, (j + 1) * 512)
                psG, kg_ = nextpf()
                for kc in range(8):
                    mm(psG[:], wga[b][:, kc, :], hT[:, kc, tk], kc == 0, kc == 7, [f"wga{b}"] + hk(j * 512, 512), [kg_])
                act(ga[gi], psG[:], AF.Sigmoid, [kg_], [f"ga{gi}"])
                psH, kh_ = nextpf()
                for kc in range(8):
                    mm(psH[:], wgb[b][:, kc, :], hT[:, kc, tk], kc == 0, kc == 7, [f"wgb{b}"] + hk(j * 512, 512), [kh_])
                act(gb[gi], psH[:], AF.Sigmoid, [kh_], [f"gb{gi}"])
                psA, ka_ = nextpf()
                for kc in range(4):
                    mm(psA[:], pa[b][:, kc, :], oaT[:, kc, tk], kc == 0, kc == 3, [f"pa{b}"] + OAK, [ka_])
                tt(t1, psA[:], ga[gi], ALU.mult, [ka_, f"ga{gi}"], ["t1"])
                psB, kb_ = nextpf()
                for kc in range(4):
                    mm(psB[:], pb_[b][:, kc, :], obT[:, kc, tk], kc == 0, kc == 3, [f"pb{b}"] + OBK, [kb_])
                tt(t2, psB[:], gb[gi], ALU.mult, [kb_, f"gb{gi}"], ["t2"])
                tt(mT[:, dch, tk], t1, t2, ALU.add, ["t1", "t2"], [f"mT{dch}_{j}"])
                gi = 1 - gi
        P.barrier()
        AR.reset(sv)
        wo = [AR.get([128, 8, 512], BF16) for _ in range(2)]
        wov = W["w_out"][l].rearrange("(kc p) n -> p kc n", p=128)
        for half in range(2):
            wload(wo[half], wov[:, :, half * 512:(half + 1) * 512], f"wo{half}")
        for t in range(NTILE):
            norm_flush(keep=1)
            for half in range(2):
                ps, pk = nextpf()
                for kc in range(8):
                    mm(ps[:], mT[:, kc, t * 128:(t + 1) * 128], wo[half][:, kc, :], kc == 0, kc == 7, [f"mT{kc}_{t // 4}", f"wo{half}"], [pk])
                hs = slice(half * 512, (half + 1) * 512)
                tt(xres[:, t, hs], xres[:, t, hs], ps[:], ALU.add, [f"x{t}", pk], [f"x{t}"])
            if after_tile is not None:
                after_tile(t)

    def ffn(l, after_tile=None):
        AR.reset(8192)
        wb = [AR.get([128, 8, 512], BF16) for _ in range(2)]
        fT = AR.get([128, 32, 512], BF16)
        rl = [AR.get([128, 512], F32) for _ in range(2)]
        wb.append(AR.get([128, 8, 512], BF16))
        w1 = W["w_ff1"][l].rearrange("(kc p) n -> p kc n", p=128)
        w2 = W["w_ff2"][l].rearrange("(fc p) n -> p fc n", p=128)
        wi = 0
        ri = 0
        for j in range(4):
            tk = slice(j * 512, (j + 1) * 512)
            for piece in range(8):
                norm_flush(keep=max(0, len(norm_def) - 1))
                wload(wb[wi], w1[:, :, piece * 512:(piece + 1) * 512], f"wb{wi}")
                for f4 in range(4):
                    fc = piece * 4 + f4
                    ps, pk = pf[4 + fc % 2], f"pf{4 + fc % 2}"
                    for kc in range(8):
                        mm(ps[:], wb[wi][:, kc, f4 * 128:(f4 + 1) * 128], hT[:, kc, tk], kc == 0, kc == 7, [f"wb{wi}"] + hk(j * 512, 512), [pk])
                    act(rl[ri], ps[:], AF.Relu, [pk], [f"rl{ri}"])
                    tt(fT[:, fc, :], ps[:], rl[ri], ALU.mult, [pk, f"rl{ri}"], [f"fT{fc}"])
                    ri = 1 - ri
                wi = (wi + 1) % 3
            for half in range(2):
                hs = slice(half * 512, (half + 1) * 512)
                for piece in range(4):
                    wload(wb[wi], w2[:, piece * 8:(piece + 1) * 8, hs], f"wb{wi}")
                    for f8 in range(8):
                        fc = piece * 8 + f8
                        for t4 in range(4):
                            mm(pf[t4][:], fT[:, fc, t4 * 128:(t4 + 1) * 128], wb[wi][:, f8, :], fc == 0, fc == 31, [f"fT{fc}", f"wb{wi}"], [f"pf{t4}"])
                    wi = (wi + 1) % 3
                for t4 in range(4):
                    t = 4 * j + t4
                    tt(xres[:, t, hs], xres[:, t, hs], pf[t4][:], ALU.add, [f"x{t}", f"pf{t4}"], [f"x{t}"])
                    if half == 1 and after_tile is not None:
                        after_tile(t)

    def ple(l, after_tile=None):
        AR.reset(8192)
        wb = [AR.get([128, 8, 512], BF16) for _ in range(2)]
        AR.reset(8192 + 16384 + 32768)
        gt = [AR.get([128, 512], F32) for _ in range(2)]
        AR.reset(8192 + 16384 + 32768 + 4096 + 8192)
        wple = AR.get([128, 2, 1024], BF16)
        pT = AR.get([128, 2, S], BF16)
        pin = [AR.get([128, 256], BF16) for _ in range(2)]
        assert AR.off <= XN_OFF, (AR.off, XN_OFF)
        wg = W["w_ple_gate"][l].rearrange("(kc p) n -> p kc n", p=128)
        for half in range(2):
            wload(wb[half], wg[:, :, half * 512:(half + 1) * 512], f"wb{half}")
        wload(wple, W["w_ple"][l].rearrange("(kc p) n -> p kc n", p=128), "wple")
        def dma_pin(t):
            b = t % 2
            P.dma("pool", f"d_pin{b}", pin[b], p_d[l, t * 128:(t + 1) * 128, :], writes=[f"pin{b}"])

        def trans_pT(t):
            b = t % 2
            for kc in range(2):
                tr(ptb[b][:, kc * 128:(kc + 1) * 128], pin[b][:, kc * 128:(kc + 1) * 128], ident_bf[:], [f"pin{b}", "ident_bf"], [f"pt{b}"])
            act(pT[:, :, t * 128:(t + 1) * 128], ptb[b][:, 0:256].rearrange("p (a b) -> p a b", b=128), AF.Copy, [f"pt{b}"], [f"pT{t}"])

        dma_pin(0)
        dma_pin(1)
        trans_pT(0)
        gi = 0
        for t in range(NTILE):
            norm_flush(keep=1)
            if t + 1 < NTILE:
                trans_pT(t + 1)
            if t + 2 < NTILE:
                dma_pin(t + 2)
            for half in range(2):
                hs = slice(half * 512, (half + 1) * 512)
                psg, kg_ = nextpf()
                for kc in range(8):
                    mm(psg[:], hT[:, kc, t * 128:(t + 1) * 128], wb[half][:, kc, :], kc == 0, kc == 7, [f"hT{t}", f"wb{half}"], [kg_])
                psp, kp_ = nextpf()
                for kc in range(2):
                    mm(psp[:], pT[:, kc, t * 128:(t + 1) * 128], wple[:, kc, hs], kc == 0, kc == 1, [f"pT{t}", "wple"], [kp_])
                act(gt[gi], psg[:], AF.Sigmoid, [kg_], [f"rl{gi}"])
                tt(gt[gi], gt[gi], psp[:], ALU.mult, [f"rl{gi}", kp_], [f"rl{gi}"])
                tt(xres[:, t, hs], xres[:, t, hs], gt[gi], ALU.add, [f"x{t}", f"rl{gi}"], [f"x{t}"])
                gi = 1 - gi
            if after_tile is not None:
                after_tile(t)

    P.barrier()
    norm_hT(0)
    for l in range(depth):
        last = (l == depth - 1)
        P.barrier()
        if phases >= 2:
            rwkv(l)
        P.barrier()
        if phases >= 3:
            obT = attention(l)
        P.barrier()
        if phases >= 4:
            merge(l, obT, after_tile=(lambda t, l=l: norm_tile_def(t, 32 + l * 8)) if phases >= 5 else None)
            norm_flush()
        dump("x1", xres[:], [128, NTILE, D], XK)
        P.barrier()
        if phases >= 5:
            ffn(l, after_tile=(lambda t, l=l: norm_tile_def(t, 64 + l * 8)) if phases >= 6 else None)
            norm_flush()
        dump("x2", xres[:], [128, NTILE, D], XK)
        if phases >= 6:
            if last:
                AR.reset(0)
                fin["gfin"] = AR.get([128, D], F32)
                fin["ob"] = AR.get([128, D], F32)
                P.dma("sp", "d_gfin", fin["gfin"], SM["final_norm_g"].partition_broadcast(128).rearrange("p o n -> p (o n)"), writes=["gfin"])
                ple(l, after_tile=final_tile)
            else:
                ple(l, after_tile=lambda t, l=l: norm_tile_def(t, (l + 1) * 8))
                norm_flush()
        dump("x3", xres[:], [128, NTILE, D], XK)
    P.wait_all("sp")
    P.emit()
    P.close()
    return nc, dbg_d


_CACHE = {}


def _alibi_tables():
    pos = np.arange(S)
    alq = np.zeros((16, S), np.float32)
    alk = np.zeros((16, S), np.float32)
    for h in range(4):
        slope = 2.0 ** (-8.0 * (h + 1) / 4)
        alq[4 * h + 0] = -slope * 128.0 * (pos // 128)
        alq[4 * h + 1] = -slope * (pos % 128)
        alq[4 * h + 2] = 1.0
        alq[4 * h + 3] = 1.0
        alk[4 * h + 0] = 1.0
        alk[4 * h + 1] = 1.0
        alk[4 * h + 2] = slope * 128.0 * (pos // 128)
        alk[4 * h + 3] = slope * (pos % 128)
    return alq, alk


def make_in_maps(inputs, ncores=8):
    alq, alk = _alibi_tables()
    shared = {}
    for k in BIGW:
        shared[k] = np.ascontiguousarray(inputs[k], dtype=np.float32)
    for k, shp in SMALL_SHAPES.items():
        shared[k] = np.ascontiguousarray(np.asarray(inputs[k], dtype=np.float32).reshape(shp))
    shared["alq"] = alq
    shared["alk"] = alk
    maps = []
    for b in range(ncores):
        m = dict(shared)
        m["x"] = np.ascontiguousarray(inputs["x"][b], dtype=np.float32)
        m["p"] = np.ascontiguousarray(inputs["p"][:, b], dtype=np.float32)
        maps.append(m)
    return maps


def kernel(**inputs):
    if "nc" not in _CACHE:
        _CACHE["nc"] = build(4)[0]
    nc = _CACHE["nc"]
    maps = make_in_maps(inputs, 8)
    res = run_bass_kernel_spmd(nc, maps, core_ids=list(range(8)))
    out = np.stack([np.asarray(r["y"], dtype=np.float32) for r in res.results], axis=0)
    return out
```

```python
import math
import numpy as np
import concourse.bass as bass
import concourse.mybir as mybir
from concourse.bass_utils import run_bass_kernel_spmd
from contextlib import ExitStack

F32 = mybir.dt.float32
BF16 = mybir.dt.bfloat16
AF = mybir.ActivationFunctionType
ALU = mybir.AluOpType
ENGS = ("pe", "act", "dve", "pool", "sp")


class Prog:
    def __init__(self, nc):
        self.nc = nc
        self.streams = {k: [] for k in ENGS}
        self.count = {}
        self.waited = {k: {} for k in ENGS}
        self.last_writer = {}
        self.readers = {}
        self.semkeys = list(ENGS)
        self.stack = ExitStack()

    def sbuf(self, name, shape, dtype):
        return self.stack.enter_context(self.nc.sbuf_tensor(name, list(shape), dtype))

    def psum(self, name, shape, dtype):
        return self.stack.enter_context(self.nc.psum_tensor(name, list(shape), dtype))

    def _deps(self, eng, reads, writes):
        deps = {}

        def add(k, v):
            if deps.get(k, 0) < v:
                deps[k] = v
        for r in reads:
            t = self.last_writer.get(r)
            if t is not None:
                add(*t)
        for w in writes:
            t = self.last_writer.get(w)
            if t is not None:
                add(*t)
            for k, v in self.readers.get(w, {}).items():
                add(k, v)
        out = []
        for k, v in deps.items():
            if eng == "pe" and k == "pe":
                continue
            if self.waited[eng].get(k, 0) >= v:
                continue
            self.waited[eng][k] = v
            out.append((k, v))
        return out

    def _record(self, tok, reads, writes):
        for w in writes:
            self.last_writer[w] = tok
            self.readers[w] = {}
        for r in reads:
            d = self.readers.setdefault(r, {})
            if d.get(tok[0], 0) < tok[1]:
                d[tok[0]] = tok[1]

    def op(self, eng, fn, reads=(), writes=()):
        waits = self._deps(eng, reads, writes)
        c = self.count.get(eng, 0) + 1
        self.count[eng] = c
        self._record((eng, c), reads, writes)
        self.streams[eng].append((waits, fn, eng, 1))

    def dma(self, eng, semkey, out, in_, reads=(), writes=()):
        if semkey not in self.semkeys:
            self.semkeys.append(semkey)
        waits = self._deps(eng, reads, writes)
        c = self.count.get(semkey, 0) + 16
        self.count[semkey] = c
        self._record((semkey, c), reads, writes)
        fn = lambda e, out=out, in_=in_: e.dma_start(out=out, in_=in_)
        self.streams[eng].append((waits, fn, semkey, 16))

    def wait_all(self, eng):
        waits = []
        for k, v in self.count.items():
            if self.waited[eng].get(k, 0) >= v:
                continue
            self.waited[eng][k] = v
            waits.append((k, v))
        self.streams[eng].append((waits, None, None, 0))

    def barrier(self):
        for e in ENGS:
            self.wait_all(e)

    def emit(self):
        nc = self.nc
        sems = {k: self.stack.enter_context(nc.semaphore("s_" + k)) for k in self.semkeys}
        block = self.stack.enter_context(nc.Block())

        def run(engname):
            def body(e):
                for waits, fn, semkey, inc in self.streams[engname]:
                    for k, v in waits:
                        e.wait_ge(sems[k], v)
                    if fn is not None:
                        fn(e).then_inc(sems[semkey], inc)
            return body
        block.tensor(run("pe"))
        block.scalar(run("act"))
        block.vector(run("dve"))
        block.gpsimd(run("pool"))
        block.sync(run("sp"))

    def close(self):
        self.stack.close()


D = 1024
S = 2048
NTILE = 16
C0 = math.exp(-0.5)
GN_EPS = 64e-5
ARENA_BYTES = 85 * 1024
SERIAL = False
SMALL = ["rwkv_mu", "rwkv_w0", "rwkv_a0", "rwkv_k_k", "rwkv_k_a", "rwkv_r_k", "rwkv_lnx_w", "rwkv_lnx_b",
         "lam_q1", "lam_k1", "lam_q2", "lam_k2", "diff_subln_g", "norm_mix_g", "norm_mlp_g", "norm_ple_g",
         "final_norm_g"]
BIGW = {"w_in": [4, 1024, 5376], "rwkv_w2": [4, 64, 512], "rwkv_a2": [4, 64, 512], "rwkv_g2": [4, 128, 512],
        "w_proj_a": [4, 512, 1024], "w_proj_b": [4, 512, 1024], "w_out": [4, 1024, 1024],
        "w_ff1": [4, 1024, 4096], "w_ff2": [4, 4096, 1024], "w_ple": [4, 256, 1024],
        "w_ple_gate": [4, 1024, 1024]}
SMALL_SHAPES = {"rwkv_mu": [4, 1792], "rwkv_w0": [4, 512], "rwkv_a0": [4, 512], "rwkv_k_k": [4, 512],
                "rwkv_k_a": [4, 512], "rwkv_r_k": [4, 512], "rwkv_lnx_w": [4, 512], "rwkv_lnx_b": [4, 512],
                "lam_q1": [1, 256], "lam_k1": [1, 256], "lam_q2": [1, 256], "lam_k2": [1, 256],
                "diff_subln_g": [4, 128], "norm_mix_g": [4, 1024], "norm_mlp_g": [4, 1024],
                "norm_ple_g": [4, 1024], "final_norm_g": [1, 1024]}


def build(depth=4, dbg=(), phases=6):
    nc = bass.Bass("TRN2", target_bir_lowering=False)
    P = Prog(nc)
    din = lambda n, s: nc.dram_tensor(n, list(s), F32, kind="ExternalInput").ap()
    x_d = din("x", [S, D])
    p_d = din("p", [4, S, 256])
    W = {k: din(k, s) for k, s in BIGW.items()}
    SM = {k: din(k, s) for k, s in SMALL_SHAPES.items()}
    alq_d = din("alq", [16, S])
    alk_d = din("alk", [16, S])
    y_d = nc.dram_tensor("y", [S, D], F32, kind="ExternalOutput").ap()
    dbg_d = {}

    xres = P.sbuf("xres", [128, NTILE, D], F32)
    hT = P.sbuf("hT", [128, 8, S], BF16)
    oaT = P.sbuf("oaT", [128, 4, S], BF16)
    ident_bf = P.sbuf("ident_bf", [128, 128], BF16)
    bo1_f = P.sbuf("bo1_f", [128, 128], F32)
    ones_f = P.sbuf("ones_f", [128, 128], F32)
    ones_bf = P.sbuf("ones_bf", [128, 128], BF16)
    maskN1 = P.sbuf("maskN1", [128, 128], BF16)
    maskNT1 = P.sbuf("maskNT1", [128, 128], BF16)
    maskIU1 = P.sbuf("maskIU1", [128, 64], BF16)
    trimask = P.sbuf("trimask", [128, 128], BF16)
    m64 = P.sbuf("m64", [128, 512], BF16)
    colA = P.sbuf("colA", [128, 120], F32)
    colB = P.sbuf("colB", [128, 52], F32)
    colC = P.sbuf("colC", [128, 96], F32)
    omka = P.sbuf("omka", [128, 16], F32)
    lamt = P.sbuf("lamt", [128, 16], F32)
    neglam = P.sbuf("neglam", [128, 4], F32)
    sgc = P.sbuf("sgc", [128, 4], F32)
    ssq = P.sbuf("ssq", [128, 16], F32)
    rstd = P.sbuf("rstd", [128, 16], F32)
    Zs = [[P.sbuf(f"Z{g}_{i}", [128, 128], F32) for i in range(2)] for g in range(4)]
    gprev = P.sbuf("gprev", [128, 4], F32)
    lastcol = P.sbuf("lastcol", [128, 14], F32)
    gpc = P.sbuf("gpc", [128, 8], F32)
    arena = P.sbuf("arena", [128, ARENA_BYTES // 4], F32)

    b4 = lambda ap: ap.rearrange("p (a b) -> p a b", b=128)
    maskN4 = maskN1[:, :].unsqueeze(1).broadcast_to([128, 4, 128])
    maskNT4 = maskNT1[:, :].unsqueeze(1).broadcast_to([128, 4, 128])
    maskIU4 = maskIU1[:, :].unsqueeze(1).broadcast_to([128, 8, 64])
    ident4 = ident_bf[:, :].unsqueeze(1).broadcast_to([128, 4, 128])
    pf = [P.psum(f"pf{i}", [128, 512], F32) for i in range(6)]
    ptb = [P.psum(f"pt{i}", [128, 1024], BF16) for i in range(2)]

    class Arena:
        def __init__(self):
            self.off = 0

        def reset(self, off=0):
            self.off = off

        def get(self, shape, dtype):
            n = int(np.prod(shape[1:]))
            nbytes = n * (4 if dtype == F32 else 2)
            nbytes = (nbytes + 31) // 32 * 32
            o = self.off
            self.off += nbytes
            assert self.off <= ARENA_BYTES, (self.off, ARENA_BYTES)
            v = arena[:, o // 4:(o + nbytes) // 4]
            if dtype != F32:
                v = v.bitcast(dtype)
            v = v[:, 0:n]
            if len(shape) == 3:
                v = v.rearrange("p (a b) -> p a b", b=shape[2])
            return v
    AR = Arena()
    cmk = AR.get([128, 512], F32)
    lamb = AR.get([128, 4, 256], F32)
    rowsA = AR.get([128, 128], F32)
    ident_f = AR.get([128, 128], F32)
    rot = {"pf": 0}

    def nextpf():
        i = rot["pf"]
        rot["pf"] = (i + 1) % 5
        return pf[i], f"pf{i}"

    def mm(out, lhsT, rhs, start, stop, reads, writes):
        P.op("pe", lambda e: e.matmul(out=out, lhsT=lhsT, rhs=rhs, start=start, stop=stop), reads, writes)

    def tr(out, in_, ident, reads, writes):
        P.op("pe", lambda e: e.transpose(out=out, in_=in_, identity=ident), reads, writes)

    def act(out, in_, func, reads, writes, **kw):
        P.op("act", lambda e: e.activation(out=out, in_=in_, func=func, **kw), reads, writes)

    def tt(out, in0, in1, op, reads, writes, eng="dve"):
        P.op(eng, lambda e: e.tensor_tensor(out=out, in0=in0, in1=in1, op=op), reads, writes)

    def ts(out, in0, s1, s2, op0, op1, reads, writes, eng="dve"):
        if op1 is None:
            P.op(eng, lambda e: e.tensor_scalar(out=out, in0=in0, scalar1=s1, scalar2=None, op0=op0), reads, writes)
        else:
            P.op(eng, lambda e: e.tensor_scalar(out=out, in0=in0, scalar1=s1, scalar2=s2, op0=op0, op1=op1), reads, writes)

    def stt(out, in0, scalar, in1, op0, op1, reads, writes):
        P.op("dve", lambda e: e.scalar_tensor_tensor(out=out, in0=in0, scalar=scalar, in1=in1, op0=op0, op1=op1), reads, writes)

    def recip(out, in_, reads, writes):
        P.op("dve", lambda e: e.reciprocal(out=out, in_=in_), reads, writes)

    def memset(ap, val, writes, eng="pool"):
        P.op(eng, lambda e: e.memset(ap, val), (), writes)

    def asel(out, in_, pattern, cmp, fill, base, cm, reads, writes):
        P.op("pool", lambda e: e.affine_select(out=out, in_=in_, pattern=pattern, compare_op=cmp, fill=fill,
                                               base=base, channel_multiplier=cm), reads, writes)

    def dump(name, ap, shape, reads):
        if name in dbg:
            d = nc.dram_tensor("dbg_" + name, list(shape), F32, kind="ExternalOutput").ap()
            dbg_d[name] = d
            P.dma("pool", "dbg_" + name, d, ap, reads=reads)

    memset(ident_f[:], 0.0, ["ident_f"])
    asel(ident_f[:], ident_f[:], [[-1, 128]], ALU.not_equal, 1.0, 0, 1, ["ident_f"], ["ident_f"])
    tt(ident_bf[:], ident_f[:], ident_f[:], ALU.mult, ["ident_f"], ["ident_bf"])
    memset(ones_f[:], 1.0, ["ones_f"])
    memset(ones_bf[:], 1.0, ["ones_bf"])
    memset(bo1_f[:], 0.0, ["bo1_f"])
    memset(bo1_f[0:64, 0:64], 1.0, ["bo1_f"])
    memset(bo1_f[64:128, 64:128], 1.0, ["bo1_f"])
    memset(cmk[:, 0:128], 0.0, ["cmk"])
    for hh in range(2):
        sl = slice(64 * hh, 64 * hh + 64)
        memset(cmk[sl, 64 * hh:64 * hh + 64], 1.0, ["cmk"])
        asel(cmk[sl, 64 * hh:64 * hh + 64], cmk[sl, 64 * hh:64 * hh + 64], [[1, 64]], ALU.is_gt, 0.0, 0, -1, ["cmk"], ["cmk"])
    tt(maskN1[:], cmk[:, 0:128], cmk[:, 0:128], ALU.mult, ["cmk"], ["maskN4"])
    memset(cmk[:, 128:256], 0.0, ["cmk2"])
    for hh in range(2):
        sl = slice(64 * hh, 64 * hh + 64)
        memset(cmk[sl, 128 + 64 * hh:128 + 64 * hh + 64], 1.0, ["cmk2"])
        asel(cmk[sl, 128 + 64 * hh:128 + 64 * hh + 64], cmk[sl, 128 + 64 * hh:128 + 64 * hh + 64], [[-1, 64]], ALU.is_gt, 0.0, 0, 1, ["cmk2"], ["cmk2"])
    tt(maskNT1[:], cmk[:, 128:256], cmk[:, 128:256], ALU.mult, ["cmk2"], ["maskNT4"])
    memset(cmk[:, 256:320], 1.0, ["cmk3"])
    for hh in range(2):
        sl = slice(64 * hh, 64 * hh + 64)
        asel(cmk[sl, 256:320], cmk[sl, 256:320], [[1, 64]], ALU.is_ge, 0.0, 0, -1, ["cmk3"], ["cmk3"])
    tt(maskIU1[:], cmk[:, 256:320], cmk[:, 256:320], ALU.mult, ["cmk3"], ["maskIU4"])
    memset(cmk[:, 384:512], 0.0, ["cmk4"])
    asel(cmk[:, 384:512], cmk[:, 384:512], [[1, 128]], ALU.is_ge, -30000.0, 0, -1, ["cmk4"], ["cmk4"])
    tt(trimask[:], cmk[:, 384:512], ones_f[:], ALU.mult, ["cmk4", "ones_f"], ["trimask"])
    memset(m64[:], 1.0, ["m64"])
    memset(m64[:].rearrange("p (c t) -> p c t", t=64)[:, :, 0:1], 0.0, ["m64"])
    for g in range(4):
        for i in range(2):
            memset(Zs[g][i][:], 0.0, [f"Z{g}_{i}"])

    def load_rows(specs, dst, ncols, dkey):
        r = 0
        for name, nrow in specs:
            src = SM[name].rearrange("l (c p) -> (l c) p", p=128)
            P.dma("sp", "d_rows", rowsA[r:r + nrow, :], src, writes=["rowsA"])
            r += nrow
        assert r == ncols
        ps, pk = nextpf()
        tr(ps[:, 0:ncols], rowsA[0:ncols, :], ident_f[0:ncols, 0:ncols], ["rowsA", "ident_f"], [pk])
        act(dst[:, 0:ncols], ps[:, 0:ncols], AF.Copy, [pk], [dkey])
    load_rows([("rwkv_mu", 56), ("rwkv_w0", 16), ("rwkv_a0", 16), ("rwkv_k_k", 16), ("rwkv_k_a", 16)], colA, 120, "colA")
    load_rows([("rwkv_r_k", 16), ("rwkv_lnx_w", 16), ("rwkv_lnx_b", 16), ("diff_subln_g", 4)], colB, 52, "colB")
    load_rows([("norm_mix_g", 32), ("norm_mlp_g", 32), ("norm_ple_g", 32)], colC, 96, "colC")
    CA, CB, CC = "colA", "colB", "colC"
    ts(omka[:], colA[:, 104:120], -1.0, 1.0, ALU.mult, ALU.add, [CA], ["omka"])
    for i, nm in enumerate(["lam_q1", "lam_k1", "lam_q2", "lam_k2"]):
        P.dma("sp", "d_lam", lamb[:, i, :], SM[nm].partition_broadcast(128).rearrange("p o n -> p (o n)"),
              writes=["lamb"])
    tt(lamb[:, 0, :], lamb[:, 0, :], lamb[:, 1, :], ALU.mult, ["lamb"], ["lamb"])
    tt(lamb[:, 2, :], lamb[:, 2, :], lamb[:, 3, :], ALU.mult, ["lamb"], ["lamb"])
    P.op("dve", lambda e: e.tensor_reduce(out=lamt[:, 0:4], in_=lamb[:, 0, :].rearrange("p (l d) -> p l d", d=64),
                                          axis=mybir.AxisListType.X, op=ALU.add), ["lamb"], ["lamt"])
    P.op("dve", lambda e: e.tensor_reduce(out=lamt[:, 4:8], in_=lamb[:, 2, :].rearrange("p (l d) -> p l d", d=64),
                                          axis=mybir.AxisListType.X, op=ALU.add), ["lamb"], ["lamt"])
    act(lamt[:, 8:16], lamt[:, 0:8], AF.Exp, ["lamt"], ["lamt"])
    tt(lamt[:, 0:4], lamt[:, 12:16], lamt[:, 8:12], ALU.subtract, ["lamt"], ["lamt"])
    for l in range(4):
        li = 0.8 - 0.6 * math.exp(-0.3 * l)
        ts(neglam[:, l:l + 1], lamt[:, l:l + 1], -li, None, ALU.add, None, ["lamt"], ["neglam"])
        ts(sgc[:, l:l + 1], colB[:, 48 + l:49 + l], 1.0 - li, None, ALU.mult, None, [CB], ["sgc"])

    xv = x_d.rearrange("(t p) d -> p t d", p=128)
    for i in range(4):
        P.dma("sp", f"d_x{i}", xres[:, 4 * i:4 * i + 4, :], xv[:, 4 * i:4 * i + 4, :], writes=[f"x{t}" for t in range(4 * i, 4 * i + 4)])

    wdma = {"i": 0}

    def wload(dst, src, key):
        P.dma("pool", "dw_" + key, dst, src, writes=[key])

    XK = [f"x{t}" for t in range(NTILE)]
    HK = [f"hT{t}" for t in range(NTILE)]

    XN_OFF = ARENA_BYTES - 4096
    AR.reset(XN_OFF)
    xn_fix = [AR.get([128, D], BF16) for _ in range(2)]
    AR.reset(0)
    yv = y_d.rearrange("(t p) d -> p t d", p=128)

    def tile_rstd(t):
        b = t % 2
        act(xn_fix[b], xres[:, t, :], AF.Square, [f"x{t}"], [f"xn{b}", f"ssq{t}"], accum_out=ssq[:, t:t + 1])
        act(rstd[:, t:t + 1], ssq[:, t:t + 1], AF.Sqrt, [f"ssq{t}"], [f"rstd{t}"], scale=1.0 / D, bias=1e-6)
        recip(rstd[:, t:t + 1], rstd[:, t:t + 1], [f"rstd{t}"], [f"rstd{t}"])

    def norm_front(t, gbase):
        b = t % 2
        tile_rstd(t)
        ts(xn_fix[b], xres[:, t, :], rstd[:, t:t + 1], None, ALU.mult, None, [f"x{t}", f"rstd{t}"], [f"xn{b}"])

    def norm_back(t, gbase):
        b = t % 2
        for kc in range(8):
            tr(ptb[b][:, kc * 128:(kc + 1) * 128], xn_fix[b][:, kc * 128:(kc + 1) * 128], ident_bf[:], [f"xn{b}", "ident_bf"], [f"pt{b}"])
        tt(hT[:, :, t * 128:(t + 1) * 128], ptb[b][:, :].rearrange("p (a b) -> p a b", b=128),
           colC[:, gbase:gbase + 8].unsqueeze(2).broadcast_to([128, 8, 128]), ALU.mult, [f"pt{b}", CC], [f"hT{t}"])

    def norm_tile(t, gbase):
        norm_front(t, gbase)
        norm_back(t, gbase)

    norm_def = []

    def norm_flush(keep=0):
        while len(norm_def) > keep:
            norm_def.pop(0)[1]()

    def norm_tile_def(t, gbase):
        while any(tp % 2 == t % 2 for tp, _ in norm_def):
            norm_def.pop(0)[1]()
        norm_front(t, gbase)
        norm_def.append((t, lambda t=t, gbase=gbase: norm_back(t, gbase)))

    def norm_hT(gbase):
        for t in range(NTILE):
            norm_tile(t, gbase)

    fin = {}

    def final_tile(t):
        tile_rstd(t)
        stt(fin["ob"], xres[:, t, :], rstd[:, t:t + 1], fin["gfin"], ALU.mult, ALU.mult, [f"x{t}", f"rstd{t}", "gfin"], ["obf"])
        P.dma("sp", "d_y", yv[:, t, :], fin["ob"], reads=["obf"])

    def hk(tok0, n):
        return [f"hT{t}" for t in range(tok0 // 128, (tok0 + n) // 128)]

    def rwkv(l):
        AR.reset(0)
        wsmU = [[AR.get([128, 8, 128], BF16) for _ in range(3)] for _ in range(2)]
        wsmL = [AR.get([128, 8, 128], BF16) for _ in range(2)]
        wl2 = AR.get([128, 512], BF16)
        wg2 = AR.get([128, 512], BF16)
        raw = AR.get([128, 520], F32)
        T1 = AR.get([128, 512], F32)
        T2 = AR.get([128, 512], F32)
        xr = AR.get([128, 512], F32)
        xk = AR.get([128, 512], F32)
        xvv = AR.get([128, 512], F32)
        sg = AR.get([128, 512], F32)
        cs = AR.get([128, 512], F32)
        eg = AR.get([128, 512], F32)
        einv = AR.get([128, 512], F32)
        al = AR.get([128, 512], F32)
        lwh = AR.get([128, 512], BF16)
        lgh = AR.get([128, 512], BF16)
        pkgs = []
        for pi in range(2):
            pg = {k: AR.get([128, 8, 128], BF16) for k in "abkv"}
            pg["rt"] = AR.get([128, 512], BF16)
            pg["bon"] = AR.get([128, 512], BF16)
            pg["gg"] = AR.get([128, 512], BF16)
            pg["gpc"] = AR.get([128, 8], F32)
            pkgs.append(pg)
        yT = AR.get([128, 512], F32)
        XS = [[AR.get([128, 512], BF16) for _ in range(2)]] * 2
        XTS = [[AR.get([128, 512], BF16) for _ in range(2)]] * 2
        QS = [[AR.get([128, 512], BF16) for _ in range(2)]] * 2
        QFIN = [None, None]
        AKT = AR.get([128, 512], BF16)
        ARS = AR.get([128, 512], BF16)
        TMab = AR.get([128, 1024], BF16)
        TMkv = AR.get([128, 1024], BF16)
        AH = AR.get([128, 512], BF16)
        AV = AR.get([128, 512], BF16)
        PV = AR.get([128, 512], BF16)
        RH = AR.get([128, 256], BF16)
        MT = AR.get([128, 512], F32)
        G1 = MT
        S0b = AR.get([128, 128], BF16)

        for pi in range(2):
            for k in "abkv":
                memset(pkgs[pi][k], 0.0, [f"pad{k}{pi}"])
        memset(lastcol[:], 0.0, ["lastcol"])
        memset(gprev[:], 0.0, ["gprev"])
        wload(wl2[0:64, :], W["rwkv_w2"][l], "wl2")
        wload(wl2[64:128, :], W["rwkv_a2"][l], "wl2")
        wload(wg2[:, :], W["rwkv_g2"][l], "wg2")
        win = W["w_in"][l].rearrange("(kc p) n -> p kc n", p=128)
        wload(wsmL[0], win[:, :, 1536:1664], "wsmL0")
        wload(wsmL[1], win[:, :, 1664:1792], "wsmL1")
        zi = [0, 0, 0, 0]

        def load_unit_weights(u):
            qt, g = divmod(u, 4)
            st = u % 2
            for ti in range(3):
                cc = ti * 4 + g
                wload(wsmU[st][ti], win[:, :, cc * 128:(cc + 1) * 128], f"wsmU{st}{ti}")

        def shift_proj(wbuf, wkey, cc, tok0, dst, dkey):
            ps, pk = nextpf()
            for kc in range(8):
                mm(ps[:], wbuf[:, kc, :], hT[:, kc, tok0:tok0 + 512], kc == 0, kc == 7, [wkey] + hk(tok0, 512), [pk])
            act(raw[:, 0:1], lastcol[:, cc:cc + 1], AF.Copy, ["lastcol"], ["raw"])
            act(raw[:, 1:513], ps[:], AF.Copy, [pk], ["raw"])
            yield
            act(lastcol[:, cc:cc + 1], raw[:, 512:513], AF.Copy, ["raw"], ["lastcol"])
            tt(T1, raw[:, 0:512], raw[:, 1:513], ALU.subtract, ["raw"], ["T1"])
            yield
            stt(dst, T1, colA[:, l * 14 + cc:l * 14 + cc + 1], raw[:, 1:513], ALU.mult, ALU.add, ["T1", "raw", CA], [dkey])
            yield

        v3 = lambda ap, sl: ap[sl, :].rearrange("p (c t) -> p c t", t=64)

        def prep1(u):
            qt, g = divmod(u, 4)
            tok0 = qt * 512
            gs = slice(g * 128, (g + 1) * 128)
            cg = l * 4 + g
            st = u % 2
            if u + 1 < 16:
                load_unit_weights(u + 1)
            if g == 0:
                yield from shift_proj(wsmL[0], "wsmL0", 12, tok0, T2, "T2")
                act(lwh[0:64, :], T2[0:64, :], AF.Tanh, ["T2"], ["lwh"])
                act(lwh[64:128, :], T2[64:128, :], AF.Copy, ["T2"], ["lwh"])
                yield
                yield from shift_proj(wsmL[1], "wsmL1", 13, tok0, T2, "T2")
                act(lgh, T2, AF.Sigmoid, ["T2"], ["lgh"])
                yield
            yield from shift_proj(wsmU[st][0], f"wsmU{st}0", g, tok0, xr, "xr")
            yield from shift_proj(wsmU[st][1], f"wsmU{st}1", 4 + g, tok0, xk, "xk")
            yield from shift_proj(wsmU[st][2], f"wsmU{st}2", 8 + g, tok0, xvv, "xv")
            ps, pk = nextpf()
            mm(ps[:], wl2[0:64, gs], lwh[0:64, :], True, True, ["wl2", "lwh"], [pk])
            act(sg, ps[:], AF.Sigmoid, [pk, CA], ["sg"], bias=colA[:, 56 + cg:57 + cg])
            yield
            ps, pk = nextpf()
            mm(ps[:], wl2[64:128, gs], lwh[64:128, :], True, True, ["wl2", "lwh"], [pk])
            act(al, ps[:], AF.Sigmoid, [pk, CA], ["al"], bias=colA[:, 72 + cg:73 + cg])
            yield
            act(T1, xk, AF.Square, ["xk", CA], ["T1"], scale=colA[:, 88 + cg:89 + cg])
            yield
            P.op("dve", lambda e: e.tensor_tensor_scan(out=cs, data0=m64[:], data1=sg, initial=0.0, op0=ALU.mult, op1=ALU.add),
                 ["m64", "sg"], ["cs"])
            ps, pk = nextpf()
            mm(ps[:], bo1_f[:], T1, True, True, ["bo1_f", "T1"], [pk])
            ts(T1, ps[:], 1e-24, None, ALU.max, None, [pk], ["T1"])
            yield
            act(eg, cs, AF.Exp, ["cs"], ["eg"], scale=-C0)
            yield
            act(einv, cs, AF.Exp, ["cs"], ["einv"], scale=C0)
            ts(T2, al, colA[:, 104 + cg:105 + cg], omka[:, cg:cg + 1], ALU.mult, ALU.add, ["al", CA, "omka"], ["T2"])
            yield
            act(T1, T1, AF.Ln, ["T1"], ["T1"])
            tt(cs, cs, sg, ALU.subtract, ["cs", "sg"], ["cs"])
            yield
            act(T1, T1, AF.Exp, ["T1"], ["T1"], scale=-0.5)
            yield
            act(sg, cs, AF.Exp, ["cs"], ["sg"], scale=-C0)
            stt(cs, xk, colA[:, 88 + cg:89 + cg], T1, ALU.mult, ALU.mult, ["xk", "T1", CA], ["cs"])
            yield
            tt(xk, xk, T2, ALU.mult, ["xk", "T2"], ["xk"])
            yield
            stt(T2, xr, colB[:, cg:cg + 1], xk, ALU.mult, ALU.mult, ["xr", "xk", CB], ["T2"])
            yield
            tt(T1, al, einv, ALU.mult, ["al", "einv"], ["T1"])
            yield

        def prep2(u, pi):
            qt, g = divmod(u, 4)
            pg = pkgs[pi]
            K = lambda n: f"{n}{pi}"
            gs = slice(g * 128, (g + 1) * 128)
            kkn, egm = cs, sg
            ps, pk = nextpf()
            mm(ps[:], wg2[:, gs], lgh, True, True, ["wg2", "lgh"], [pk])
            act(pg["gg"], ps[:], AF.Copy, [pk], [K("gg")])
            yield
            ps, pk = nextpf()
            mm(ps[:], bo1_f[:], T2, True, True, ["bo1_f", "T2"], [pk])
            tt(pg["bon"], ps[:], xvv, ALU.mult, [pk, "xv"], [K("bon")])
            yield
            for hh in range(2):
                sl = slice(64 * hh, 64 * hh + 64)
                stt(pg["a"][sl, :, sl], v3(kkn, sl), -1.0, v3(egm, sl), ALU.mult, ALU.mult, ["cs", "sg"], [K("pada")])
                yield
                tt(pg["b"][sl, :, sl], v3(kkn, sl), v3(T1, sl), ALU.mult, ["cs", "T1"], [K("padb")])
                yield
                tt(pg["k"][sl, :, sl], v3(xk, sl), v3(einv, sl), ALU.mult, ["xk", "einv"], [K("padk")])
                act(pg["v"][sl, :, sl], v3(xvv, sl), AF.Copy, ["xv"], [K("padv")])
                yield
            tt(pg["rt"], xr, eg, ALU.mult, ["xr", "eg"], [K("rt")])
            act(pg["gpc"][:, 0:1], gprev[:, g:g + 1], AF.Copy, ["gprev"], [K("gpc")])
            yield
            act(pg["gpc"][:, 1:8], eg.rearrange("p (c t) -> p c t", t=64)[:, 0:7, 63], AF.Copy, ["eg"], [K("gpc")])
            act(gprev[:, g:g + 1], eg[:, 511:512], AF.Copy, ["eg"], ["gprev"])
            yield

        blk = lambda ap, i: ap[:, i * 128:(i + 1) * 128]

        def neumann(k):
            u, q = divmod(k, 2)
            pi = u % 2
            pg = pkgs[pi]
            K = lambda n: f"{n}{pi}"
            s_ = k % 2
            X_, XT_, Q_ = XS[s_], XTS[s_], QS[s_]
            xk_ = lambda n, i: f"{n}_{i}"
            pada, padb = pg["a"], pg["b"]
            chs = [4 * q + i for i in range(4)]
            ps, pk = nextpf()
            for i, c in enumerate(chs):
                mm(blk(ps, i), padb[:, c, :], pada[:, c, :], True, True, [K("pada"), K("padb")], [pk])
            tt(b4(X_[0]), b4(ps[:]), maskN4, ALU.mult, [pk, "maskN4"], [xk_("X", 0)])
            ps, pk = nextpf()
            for i, c in enumerate(chs):
                mm(blk(ps, i), pada[:, c, :], padb[:, c, :], True, True, [K("pada"), K("padb")], [pk])
            tt(b4(XT_[0]), b4(ps[:]), maskNT4, ALU.mult, [pk, "maskNT4"], [xk_("XT", 0)])
            tt(b4(Q_[0]), b4(X_[0]), ident4, ALU.add, [xk_("X", 0), "ident_bf"], [xk_("Q", 0)], eng="pool")
            yield
            cur = 0
            for lev in range(1, 6):
                nx = 1 - cur
                ps, pk = nextpf()
                for i in range(4):
                    mm(blk(ps, i), blk(X_[cur], i), blk(XT_[cur], i), True, True, [xk_("X", cur), xk_("XT", cur)], [pk])
                act(XT_[nx], ps[:], AF.Copy, [pk], [xk_("XT", nx)])
                if lev < 5:
                    ps, pk = nextpf()
                    for i in range(4):
                        mm(blk(ps, i), blk(XT_[cur], i), blk(X_[cur], i), True, True, [xk_("X", cur), xk_("XT", cur)], [pk])
                    P.op("dve", lambda e, o=X_[nx], s=ps: e.tensor_copy(out=o, in_=s[:]), [pk], [xk_("X", nx)])
                yield
                ps, pk = nextpf()
                for i in range(4):
                    mm(blk(ps, i), blk(XT_[nx], i), blk(Q_[cur], i), True, True, [xk_("XT", nx), xk_("Q", cur)], [pk])
                tt(Q_[nx], ps[:], Q_[cur], ALU.add, [pk, xk_("Q", cur)], [xk_("Q", nx)])
                cur = nx
                yield
            QFIN[s_] = (Q_[cur], xk_("Q", cur))

        def tail(k):
            u, q = divmod(k, 2)
            pi = u % 2
            qt, g = divmod(u, 4)
            tok0 = qt * 512
            pg = pkgs[pi]
            K = lambda n: f"{n}{pi}"
            cg = l * 4 + g
            pada, padb, padk, padv, rt, gpc_ = pg["a"], pg["b"], pg["k"], pg["v"], pg["rt"], pg["gpc"]
            chs = [4 * q + i for i in range(4)]
            ps, pk = nextpf()
            for i, c in enumerate(chs):
                mm(blk(ps, i), padk[:, c, :], pada[:, c, :], True, True, [K("pada"), K("padk")], [pk])
            tt(b4(AKT), b4(ps[:]), maskN4, ALU.mult, [pk, "maskN4"], ["AKT"])
            ps, pk = nextpf()
            for i, c in enumerate(chs):
                mm(ps[:, i * 128:i * 128 + 64], padb[:, c, :], rt[:, c * 64:(c + 1) * 64], True, True, [K("padb"), K("rt")], [pk])
                mm(ps[:, i * 128 + 64:(i + 1) * 128], padk[:, c, :], rt[:, c * 64:(c + 1) * 64], True, True, [K("padk"), K("rt")], [pk])
            tt(ARS.rearrange("p (a b) -> p a b", b=64), ps[:].rearrange("p (a b) -> p a b", b=64), maskIU4, ALU.mult, [pk, "maskIU4"], ["ARS"])
            yield
            for i, c in enumerate(chs):
                tr(ptb[0][:, i * 128:(i + 1) * 128], pada[:, c, :], ident_bf[:], [K("pada"), "ident_bf"], ["pt0"])
                tr(ptb[0][:, (4 + i) * 128:(5 + i) * 128], padb[:, c, :], ident_bf[:], [K("padb"), "ident_bf"], ["pt0"])
            act(TMab, ptb[0][:, :], AF.Copy, ["pt0"], ["TMab"])
            for i, c in enumerate(chs):
                tr(ptb[1][:, i * 128:(i + 1) * 128], padk[:, c, :], ident_bf[:], [K("padk"), "ident_bf"], ["pt1"])
                tr(ptb[1][:, (4 + i) * 128:(5 + i) * 128], padv[:, c, :], ident_bf[:], [K("padv"), "ident_bf"], ["pt1"])
            P.op("dve", lambda e: e.tensor_copy(out=TMkv, in_=ptb[1][:, :]), ["pt1"], ["TMkv"])
            yield
            QF, QK = QFIN[k % 2]
            TMa = lambda i: TMab[:, i * 128:(i + 1) * 128]
            TMb = lambda i: TMab[:, (4 + i) * 128:(5 + i) * 128]
            TMk = lambda i: TMkv[:, i * 128:(i + 1) * 128]
            TMv = lambda i: TMkv[:, (4 + i) * 128:(5 + i) * 128]
            ps, pk = nextpf()
            for i in range(4):
                mm(blk(ps, i), blk(QF, i), TMa(i), True, True, [QK, "TMab"], [pk])
            act(AH, ps[:], AF.Copy, [pk], ["AH"])
            ps, pk = nextpf()
            for i in range(4):
                mm(blk(ps, i), blk(AKT, i), TMv(i), True, True, ["AKT", "TMkv"], [pk])
            P.op("dve", lambda e, s=ps: e.tensor_copy(out=AV, in_=s[:]), [pk], ["AV"])
            yield
            ps, pk = nextpf()
            for i in range(4):
                mm(blk(ps, i), blk(QF, i), blk(AV, i), True, True, [QK, "AV"], [pk])
            act(PV, ps[:], AF.Copy, [pk], ["PV"])
            ps, pk = nextpf()
            for i in range(4):
                mm(ps[:, i * 64:(i + 1) * 64], blk(AH, i), ARS[:, i * 128:i * 128 + 64], True, True, ["AH", "ARS"], [pk])
            tt(RH, ps[:, 0:256], rt[:, q * 256:(q + 1) * 256], ALU.add, [pk, K("rt")], ["RH"])
            ps, pk = nextpf()
            for i in range(4):
                mm(blk(ps, i), blk(AH, i), TMb(i), True, True, ["AH", "TMab"], [pk])
            tt(b4(MT), b4(ps[:]), gpc_[:, 4 * q:4 * q + 4].unsqueeze(2).broadcast_to([128, 4, 128]), ALU.mult, [pk, K("gpc")], ["MT"])
            yield
            psy, pyk = pf[5], "pf5"
            for i, c in enumerate(chs):
                gcol = gpc_[:, c:c + 1]
                zc, zck = Zs[g][zi[g]], f"Z{g}_{zi[g]}"
                zn, znk = Zs[g][1 - zi[g]], f"Z{g}_{1 - zi[g]}"
                act(S0b, zc[:], AF.Copy, [zck, K("gpc")], ["S0b"], scale=gcol)
                pss, psk = nextpf()
                mm(pss[:, 0:128], TMb(i), blk(PV, i), True, False, ["TMab", "PV"], [psk])
                mm(pss[:, 0:128], TMk(i), TMv(i), False, False, ["TMkv"], [psk])
                mm(pss[:, 0:128], blk(MT, i), zc[:], False, True, ["MT", zck], [psk])
                stt(zn[:], zc[:], gcol, pss[:, 0:128], ALU.mult, ALU.add, [zck, K("gpc"), psk], [znk])
                yo = psy[:, i * 64:(i + 1) * 64]
                mm(yo, S0b, RH[:, i * 64:(i + 1) * 64], True, False, ["S0b", "RH"], [pyk])
                mm(yo, blk(PV, i), ARS[:, i * 128:i * 128 + 64], False, False, ["PV", "ARS"], [pyk])
                mm(yo, TMv(i), ARS[:, i * 128 + 64:(i + 1) * 128], False, True, ["TMkv", "ARS"], [pyk])
                zi[g] = 1 - zi[g]
                yield
            act(yT[:, q * 256:(q + 1) * 256], psy[:, 0:256], AF.Copy, [pyk], ["yT"])
            if q == 0:
                return
            ps, pk = nextpf()
            mm(ps[:], bo1_f[:], yT, True, True, ["bo1_f", "yT"], [pk])
            stt(yT, ps[:], -1.0 / 64.0, yT, ALU.mult, ALU.add, ["yT", pk], ["yT"])
            act(G1, yT, AF.Square, ["yT"], ["MT"])
            ps, pk = nextpf()
            mm(ps[:], bo1_f[:], G1, True, True, ["bo1_f", "MT"], [pk])
            act(G1, ps[:], AF.Ln, [pk], ["MT"], bias=GN_EPS, scale=1.0 / 64.0)
            act(G1, G1, AF.Exp, ["MT"], ["MT"], scale=-0.5)
            yield
            tt(yT, yT, G1, ALU.mult, ["yT", "MT"], ["yT"])
            ts(yT, yT, colB[:, 16 + cg:17 + cg], colB[:, 32 + cg:33 + cg], ALU.mult, ALU.add, ["yT", CB], ["yT"])
            tt(yT, yT, pg["bon"], ALU.add, ["yT", K("bon")], ["yT"])
            tt(oaT[:, g, tok0:tok0 + 512], yT, pg["gg"], ALU.mult, ["yT", K("gg")], [f"oa{g}_{qt}"])
            yield

        def run_gens(items):
            if SERIAL:
                for gen, n in items:
                    for _ in gen:
                        pass
                return
            alive = [list(it) for it in items]
            while alive:
                for it in list(alive):
                    gen, n = it
                    for _ in range(n):
                        try:
                            next(gen)
                        except StopIteration:
                            alive.remove(it)
                            break

        load_unit_weights(0)
        run_gens([(prep1(0), 1)])
        run_gens([(prep2(0, 0), 1)])
        run_gens([(neumann(0), 1), (prep1(1), 2)])
        NQ = 32
        for k in range(NQ):
            tg = tail(k)
            for _ in range(4):
                next(tg)
            items = [(tg, 1)]
            if k + 1 < NQ:
                items.append((neumann(k + 1), 1))
            u, q = divmod(k, 2)
            if q == 0 and u + 1 < 16:
                items.append((prep2(u + 1, (u + 1) % 2), 1))
            if q == 1 and u + 2 < 16:
                items.append((prep1(u + 2), 2))
            run_gens(items)
        dump("oaT", oaT[:], [128, 4, S], [f"oa{g}_{qt}" for g in range(4) for qt in range(4)])

    def attention(l):
        AR.reset(0)
        obT = AR.get([128, 4, S], BF16)
        vtm = AR.get([128, NTILE, 512], BF16)
        qa = [[AR.get([128, S], BF16) for _ in range(2)] for _ in range(2)]
        ka = [[AR.get([128, S], BF16) for _ in range(2)] for _ in range(2)]
        stage = AR.get([128, S], BF16)
        wq = [AR.get([128, 8, 128], BF16) for _ in range(2)]
        wv_off = AR.off
        wv = AR.get([128, 8, 512], BF16)
        PT = [AR.get([128, 512], BF16) for _ in range(4)]
        STILES = [(pf[0], "pf0"), (pf[1], "pf1"), (ptb[1][:, :].bitcast(F32), "pt1")]
        win = W["w_in"][l].rearrange("(kc p) n -> p kc n", p=128)
        wload(wv, win[:, :, 2816:3328], "wv")
        sv = AR.off
        AR.reset(wv_off)
        Rc = AR.get([128, 512], F32)
        A0 = AR.get([128, 512], F32)
        A1 = AR.get([128, 512], F32)
        Tq = AR.get([128, 512], F32)
        AR.reset(sv)
        wr = 0
        ptr = 0
        deferred = []
        def proj(h):
            nonlocal wr
            hb = h % 2
            for which, cb, aug, sc in (("q", 1792 + h * 128, qa[hb], 0.125), ("k", 2304 + h * 128, ka[hb], 1.0)):
                wload(wq[wr], win[:, :, cb:cb + 128], f"wq{wr}")
                for tc in range(4):
                    ps, pk = nextpf()
                    for kc in range(8):
                        mm(ps[:], wq[wr][:, kc, :], hT[:, kc, tc * 512:(tc + 1) * 512], kc == 0, kc == 7, [f"wq{wr}"] + hk(tc * 512, 512), [pk])
                    ts(aug[0][0:64, tc * 512:(tc + 1) * 512], ps[0:64, :], sc, None, ALU.mult, None, [pk], [f"{which}a{hb}0"])
                    ts(stage[64:128, tc * 512:(tc + 1) * 512], ps[64:128, :], sc, None, ALU.mult, None, [pk], ["stage"])
                P.dma("sp", f"d_rep{which}{hb}", aug[1][0:64, :], stage[64:128, :], reads=["stage"], writes=[f"{which}a{hb}1"])
                wr = 1 - wr
            for c in range(2):
                P.dma("pool", f"d_alq{hb}{c}", qa[hb][c][64:68, :], alq_d[4 * h:4 * h + 4, :], writes=[f"qa{hb}{c}"])
                P.dma("pool", f"d_alk{hb}{c}", ka[hb][c][64:68, :], alk_d[4 * h:4 * h + 4, :], writes=[f"ka{hb}{c}"])

        proj(0)
        for kt in range(NTILE):
            ps, pk = nextpf()
            for kc in range(8):
                mm(ps[:], hT[:, kc, kt * 128:(kt + 1) * 128], wv[:, kc, :], kc == 0, kc == 7, [f"hT{kt}", "wv"], [pk])
            P.op("dve", lambda e, o=vtm[:, kt, :], s_=ps: e.tensor_copy(out=o, in_=s_[:]), [pk], [f"vtm{kt}"])
        P.barrier()
        pend = []
        sctr = [0]

        def drain(n):
            while len(pend) > n:
                ofn, st, endfn = pend.pop(0)
                ofn(st)
                if endfn is not None:
                    endfn()

        for h in range(4):
            for j in range(4):
                nkt = 4 * (j + 1)
                for c in range(2):
                    O, Ok = pf[2 + 2 * c], f"pf{2 + 2 * c}"
                    Lp, Lk = pf[3 + 2 * c], f"pf{3 + 2 * c}"

                    def s_stage(kt, j=j, c=c, h=h):
                        nonlocal ptr
                        t = kt - 4 * j
                        n0 = 128 * t if t > 0 else 0
                        Sp, Sk = STILES[sctr[0] % 3]
                        sctr[0] += 1
                        mm(Sp[:, n0:512], ka[h % 2][c][0:68, kt * 128:(kt + 1) * 128], qa[h % 2][c][0:68, j * 512 + n0:(j + 1) * 512], True, t < 0,
                           [f"ka{h % 2}{c}", f"qa{h % 2}{c}"], [Sk])
                        if t >= 0:
                            mm(Sp[:, n0:n0 + 128], ident_bf[:], trimask[:], False, True, ["ident_bf", "trimask"], [Sk])
                        pb = PT[ptr]
                        pbk = f"PT{ptr}"
                        ptr = (ptr + 1) % 4
                        act(pb[:, n0:512], Sp[:, n0:512], AF.Exp, [Sk], [pbk])
                        return kt, n0, pb, pbk

                    def o_stage(st, nkt=nkt, O=O, Ok=Ok, Lp=Lp, Lk=Lk, h=h):
                        kt, n0, pb, pbk = st
                        mm(O[:, n0:512], vtm[:, kt, h * 128:(h + 1) * 128], pb[:, n0:512], kt == 0, kt == nkt - 1, [f"vtm{kt}", pbk], [Ok])
                        mm(Lp[:, n0:512], ones_bf[:], pb[:, n0:512], kt == 0, kt == nkt - 1, ["ones_bf", pbk], [Lk])

                    def block_end(c=c, O=O, Ok=Ok, Lp=Lp, Lk=Lk, h=h, j=j):
                        act(Rc, Lp[:], AF.Ln, [Lk], ["Rc"])
                        act(Rc, Rc, AF.Exp, ["Rc"], ["Rc"], scale=-1.0)
                        tt((A0, A1)[c], O[:], Rc, ALU.mult, [Ok, "Rc"], [f"A{c}"])
                        if c == 1:
                            stt(A0, A1, neglam[:, l:l + 1], A0, ALU.mult, ALU.add, ["A0", "A1", "neglam"], ["A0"])
                            tt(Tq, A0, A0, ALU.mult, ["A0"], ["Tq"])

                            def tail(h=h, j=j):
                                ps = ptb[0][:, :].bitcast(F32)
                                mm(ps, ones_f[:], Tq, True, True, ["ones_f", "Tq"], ["pt0"])
                                act(Tq, ps, AF.Ln, ["pt0"], ["Tq"], scale=1.0 / 128.0, bias=1e-5)
                                act(Tq, Tq, AF.Exp, ["Tq"], ["Tq"], scale=-0.5)
                                tt(A0, A0, Tq, ALU.mult, ["A0", "Tq"], ["A0"])
                                ts(obT[:, h, j * 512:(j + 1) * 512], A0, sgc[:, l:l + 1], None, ALU.mult, None, ["A0", "sgc"], [f"ob{h}_{j}"])
                            deferred.append(tail)

                    for kt in range(nkt):
                        pend.append((o_stage, s_stage(kt), block_end if kt == nkt - 1 else None))
                        drain(2)
                        if kt == 3:
                            while deferred:
                                deferred.pop(0)()
                if j == 0 and h + 1 < 4:
                    drain(0)
                    proj(h + 1)
        drain(0)
        while deferred:
            deferred.pop(0)()
        dump("obT", obT, [128, 4, S], [f"ob{h}_{j}" for h in range(4) for j in range(4)])
        return obT

    def merge(l, obT, after_tile=None):
        AR.reset(16384)
        mT = AR.get([128, 8, S], BF16)
        sv = AR.off
        wga = [AR.get([128, 8, 128], BF16) for _ in range(2)]
        wgb = [AR.get([128, 8, 128], BF16) for _ in range(2)]
        pa = [AR.get([128, 4, 128], BF16) for _ in range(2)]
        pb_ = [AR.get([128, 4, 128], BF16) for _ in range(2)]
        ga = [AR.get([128, 512], BF16) for _ in range(2)]
        gb = [AR.get([128, 512], BF16) for _ in range(2)]
        t1 = AR.get([128, 512], F32)
        t2 = AR.get([128, 512], F32)
        win = W["w_in"][l].rearrange("(kc p) n -> p kc n", p=128)
        wpa = W["w_proj_a"][l].rearrange("(kc p) n -> p kc n", p=128)
        wpb = W["w_proj_b"][l].rearrange("(kc p) n -> p kc n", p=128)
        OAK = [f"oa{g}_{qt}" for g in range(4) for qt in range(4)]
        OBK = [f"ob{h}_{j}" for h in range(4) for j in range(4)]
        gi = 0
        for dch in range(8):
            b = dch % 2
            wload(wga[b], win[:, :, 3328 + dch * 128:3328 + (dch + 1) * 128], f"wga{b}")
            wload(wgb[b], win[:, :, 4352 + dch * 128:4352 + (dch + 1) * 128], f"wgb{b}")
            wload(pa[b], wpa[:, :, dch * 128:(dch + 1) * 128], f"pa{b}")
            wload(pb_[b], wpb[:, :, dch * 128:(dch + 1) * 128], f"pb{b}")
            for j in range(4):
                tk = slice(j * 512, (j + 1) * 512)
                psG, kg_ = nextpf()
                for kc in range(8):
                    mm(psG[:], wga[b][:, kc, :], hT[:, kc, tk], kc == 0, kc == 7, [f"wga{b}"] + hk(j * 512, 512), [kg_])
                act(ga[gi], psG[:], AF.Sigmoid, [kg_], [f"ga{gi}"])
                psH, kh_ = nextpf()
                for kc in range(8):
                    mm(psH[:], wgb[b][:, kc, :], hT[:, kc, tk], kc == 0, kc == 7, [f"wgb{b}"] + hk(j * 512, 512), [kh_])
                act(gb[gi], psH[:], AF.Sigmoid, [kh_], [f"gb{gi}"])
                psA, ka_ = nextpf()
                for kc in range(4):
                    mm(psA[:], pa[b][:, kc, :], oaT[:, kc, tk], kc == 0, kc == 3, [f"pa{b}"] + OAK, [ka_])
                tt(t1, psA[:], ga[gi], ALU.mult, [ka_, f"ga{gi}"], ["t1"])
                psB, kb_ = nextpf()
                for kc in range(4):
                    mm(psB[:], pb_[b][:, kc, :], obT[:, kc, tk], kc == 0, kc == 3, [f"pb{b}"] + OBK, [kb_])
                tt(t2, psB[:], gb[gi], ALU.mult, [kb_, f"gb{gi}"], ["t2"])
                tt(mT[:, dch, tk], t1, t2, ALU.add, ["t1", "t2"], [f"mT{dch}_{j}"])
                gi = 1 - gi
        P.barrier()
        AR.reset(sv)
        wo = [AR.get([128, 8, 512], BF16) for _ in range(2)]
        wov = W["w_out"][l].rearrange("(kc p) n -> p kc n", p=128)
        for half in range(2):
            wload(wo[half], wov[:, :, half * 512:(half + 1) * 512], f"wo{half}")
        for t in range(NTILE):
            norm_flush(keep=1)
            for half in range(2):
                ps, pk = nextpf()
                for kc in range(8):
                    mm(ps[:], mT[:, kc, t * 128:(t + 1) * 128], wo[half][:, kc, :], kc == 0, kc == 7, [f"mT{kc}_{t // 4}", f"wo{half}"], [pk])
                hs = slice(half * 512, (half + 1) * 512)
                tt(xres[:, t, hs], xres[:, t, hs], ps[:], ALU.add, [f"x{t}", pk], [f"x{t}"])
            if after_tile is not None:
                after_tile(t)

    def ffn(l, after_tile=None):
        AR.reset(8192)
        wb = [AR.get([128, 8, 512], BF16) for _ in range(2)]
        fT = AR.get([128, 32, 512], BF16)
        rl = [AR.get([128, 512], F32) for _ in range(2)]
        wb.append(AR.get([128, 8, 512], BF16))
        w1 = W["w_ff1"][l].rearrange("(kc p) n -> p kc n", p=128)
        w2 = W["w_ff2"][l].rearrange("(fc p) n -> p fc n", p=128)
        wi = 0
        ri = 0
        for j in range(4):
            tk = slice(j * 512, (j + 1) * 512)
            for piece in range(8):
                norm_flush(keep=max(0, len(norm_def) - 1))
                wload(wb[wi], w1[:, :, piece * 512:(piece + 1) * 512], f"wb{wi}")
                for f4 in range(4):
                    fc = piece * 4 + f4
                    ps, pk = pf[4 + fc % 2], f"pf{4 + fc % 2}"
                    for kc in range(8):
                        mm(ps[:], wb[wi][:, kc, f4 * 128:(f4 + 1) * 128], hT[:, kc, tk], kc == 0, kc == 7, [f"wb{wi}"] + hk(j * 512, 512), [pk])
                    act(rl[ri], ps[:], AF.Relu, [pk], [f"rl{ri}"])
                    tt(fT[:, fc, :], ps[:], rl[ri], ALU.mult, [pk, f"rl{ri}"], [f"fT{fc}"])
                    ri = 1 - ri
                wi = (wi + 1) % 3
            for half in range(2):
                hs = slice(half * 512, (half + 1) * 512)
                for piece in range(4):
                    wload(wb[wi], w2[:, piece * 8:(piece + 1) * 8, hs], f"wb{wi}")
                    for f8 in range(8):
                        fc = piece * 8 + f8
                        for t4 in range(4):
                            mm(pf[t4][:], fT[:, fc, t4 * 128:(t4 + 1) * 128], wb[wi][:, f8, :], fc == 0, fc == 31, [f"fT{fc}", f"wb{wi}"], [f"pf{t4}"])
                    wi = (wi + 1) % 3
                for t4 in range(4):
                    t = 4 * j + t4
                    tt(xres[:, t, hs], xres[:, t, hs], pf[t4][:], ALU.add, [f"x{t}", f"pf{t4}"], [f"x{t}"])
                    if half == 1 and after_tile is not None:
                        after_tile(t)

    def ple(l, after_tile=None):
        AR.reset(8192)
        wb = [AR.get([128, 8, 512], BF16) for _ in range(2)]
        AR.reset(8192 + 16384 + 32768)
        gt = [AR.get([128, 512], F32) for _ in range(2)]
        AR.reset(8192 + 16384 + 32768 + 4096 + 8192)
        wple = AR.get([128, 2, 1024], BF16)
        pT = AR.get([128, 2, S], BF16)
        pin = [AR.get([128, 256], BF16) for _ in range(2)]
        assert AR.off <= XN_OFF, (AR.off, XN_OFF)
        wg = W["w_ple_gate"][l].rearrange("(kc p) n -> p kc n", p=128)
        for half in range(2):
            wload(wb[half], wg[:, :, half * 512:(half + 1) * 512], f"wb{half}")
        wload(wple, W["w_ple"][l].rearrange("(kc p) n -> p kc n", p=128), "wple")
        def dma_pin(t):
            b = t % 2
            P.dma("pool", f"d_pin{b}", pin[b], p_d[l, t * 128:(t + 1) * 128, :], writes=[f"pin{b}"])

        def trans_pT(t):
            b = t % 2
            for kc in range(2):
                tr(ptb[b][:, kc * 128:(kc + 1) * 128], pin[b][:, kc * 128:(kc + 1) * 128], ident_bf[:], [f"pin{b}", "ident_bf"], [f"pt{b}"])
            act(pT[:, :, t * 128:(t + 1) * 128], ptb[b][:, 0:256].rearrange("p (a b) -> p a b", b=128), AF.Copy, [f"pt{b}"], [f"pT{t}"])

        dma_pin(0)
        dma_pin(1)
        trans_pT(0)
        gi = 0
        for t in range(NTILE):
            norm_flush(keep=1)
            if t + 1 < NTILE:
                trans_pT(t + 1)
            if t + 2 < NTILE:
                dma_pin(t + 2)
            for half in range(2):
                hs = slice(half * 512, (half + 1) * 512)
                psg, kg_ = nextpf()
                for kc in range(8):
                    mm(psg[:], hT[:, kc, t * 128:(t + 1) * 128], wb[half][:, kc, :], kc == 0, kc == 7, [f"hT{t}", f"wb{half}"], [kg_])
                psp, kp_ = nextpf()
                for kc in range(2):
                    mm(psp[:], pT[:, kc, t * 128:(t + 1) * 128], wple[:, kc, hs], kc == 0, kc == 1, [f"pT{t}", "wple"], [kp_])
                act(gt[gi], psg[:], AF.Sigmoid, [kg_], [f"rl{gi}"])
                tt(gt[gi], gt[gi], psp[:], ALU.mult, [f"rl{gi}", kp_], [f"rl{gi}"])
                tt(xres[:, t, hs], xres[:, t, hs], gt[gi], ALU.add, [f"x{t}", f"rl{gi}"], [f"x{t}"])
                gi = 1 - gi
            if after_tile is not None:
                after_tile(t)

    P.barrier()
    norm_hT(0)
    for l in range(depth):
        last = (l == depth - 1)
        P.barrier()
        if phases >= 2:
            rwkv(l)
        P.barrier()
        if phases >= 3:
            obT = attention(l)
        P.barrier()
        if phases >= 4:
            merge(l, obT, after_tile=(lambda t, l=l: norm_tile_def(t, 32 + l * 8)) if phases >= 5 else None)
            norm_flush()
        dump("x1", xres[:], [128, NTILE, D], XK)
        P.barrier()
        if phases >= 5:
            ffn(l, after_tile=(lambda t, l=l: norm_tile_def(t, 64 + l * 8)) if phases >= 6 else None)
            norm_flush()
        dump("x2", xres[:], [128, NTILE, D], XK)
        if phases >= 6:
            if last:
                AR.reset(0)
                fin["gfin"] = AR.get([128, D], F32)
                fin["ob"] = AR.get([128, D], F32)
                P.dma("sp", "d_gfin", fin["gfin"], SM["final_norm_g"].partition_broadcast(128).rearrange("p o n -> p (o n)"), writes=["gfin"])
                ple(l, after_tile=final_tile)
            else:
                ple(l, after_tile=lambda t, l=l: norm_tile_def(t, (l + 1) * 8))
                norm_flush()
        dump("x3", xres[:], [128, NTILE, D], XK)
    P.wait_all("sp")
    P.emit()
    P.close()
    return nc, dbg_d


_CACHE = {}


def _alibi_tables():
    pos = np.arange(S)
    alq = np.zeros((16, S), np.float32)
    alk = np.zeros((16, S), np.float32)
    for h in range(4):
        slope = 2.0 ** (-8.0 * (h + 1) / 4)
        alq[4 * h + 0] = -slope * 128.0 * (pos // 128)
        alq[4 * h + 1] = -slope * (pos % 128)
        alq[4 * h + 2] = 1.0
        alq[4 * h + 3] = 1.0
        alk[4 * h + 0] = 1.0
        alk[4 * h + 1] = 1.0
        alk[4 * h + 2] = slope * 128.0 * (pos // 128)
        alk[4 * h + 3] = slope * (pos % 128)
    return alq, alk


def make_in_maps(inputs, ncores=8):
    alq, alk = _alibi_tables()
    shared = {}
    for k in BIGW:
        shared[k] = np.ascontiguousarray(inputs[k], dtype=np.float32)
    for k, shp in SMALL_SHAPES.items():
        shared[k] = np.ascontiguousarray(np.asarray(inputs[k], dtype=np.float32).reshape(shp))
    shared["alq"] = alq
    shared["alk"] = alk
    maps = []
    for b in range(ncores):
        m = dict(shared)
        m["x"] = np.ascontiguousarray(inputs["x"][b], dtype=np.float32)
        m["p"] = np.ascontiguousarray(inputs["p"][:, b], dtype=np.float32)
        maps.append(m)
    return maps


def kernel(**inputs):
    if "nc" not in _CACHE:
        _CACHE["nc"] = build(4)[0]
    nc = _CACHE["nc"]
    maps = make_in_maps(inputs, 8)
    res = run_bass_kernel_spmd(nc, maps, core_ids=list(range(8)))
    out = np.stack([np.asarray(r["y"], dtype=np.float32) for r in res.results], axis=0)
    return out
```
